# Optimizing a Trainium2 kernel written in Bass

```python
import math
import jax, jax.numpy as jnp
from jax import lax
import numpy as np

D_MODEL = 1024
BATCH = 4
SEQ = 4096
DEPTH = 4
DEC_BATCH = 128
DEC_SEQ = 8
PAST_LEN = 2048
PAGE_SIZE = 128

N_A_LAYERS = DEPTH // 2
N_B_LAYERS = DEPTH - N_A_LAYERS
A_HEADS = 4
A_HEAD_DIM = D_MODEL // A_HEADS
MLSTM_CHUNK = 64
B_HEADS = 16
B_HEAD_DIM = D_MODEL // B_HEADS
MOBA_BLOCK = 256
MOBA_TOPK = 3
MOBA_Q_CHUNK = 32
ROPE_THETA = 10000.0
PK_HEADS = 8
N_KEYS = 128
N_EXPERTS = N_KEYS * N_KEYS
PK_DIM = 256
PK_TOPK = 16
PEER_BLOCK = 256
DN_ALPHA = (2.0 * DEPTH) ** 0.25
DN_BETA = (8.0 * DEPTH) ** -0.25
LN_EPS = 1e-5
NEG_INF = -1e30

kernel_name = 'yoco_mlstm_moba_peer_step'


def layer_norm(x, g, b):
    xf = x.astype(jnp.float32)
    mu = xf.mean(-1, keepdims=True)
    var = jnp.square(xf - mu).mean(-1, keepdims=True)
    return ((xf - mu) * lax.rsqrt(var + LN_EPS) * g + b).astype(x.dtype)


def rope(x, pos):
    half = x.shape[-1] // 2
    inv = ROPE_THETA ** (-jnp.arange(half, dtype=jnp.float32) / half)
    ang = pos.astype(jnp.float32)[:, None] * inv[None, :]
    cos = jnp.cos(ang)[None, :, None, :]
    sin = jnp.sin(ang)[None, :, None, :]
    xf = x.astype(jnp.float32)
    x1, x2 = xf[..., :half], xf[..., half:]
    return jnp.concatenate([x1 * cos - x2 * sin, x2 * cos + x1 * sin], -1).astype(x.dtype)


def mlstm_chunkwise(q, k, v, ig, lf, C0, n0, m0):
    B, T, H, Dh = q.shape
    L = math.gcd(MLSTM_CHUNK, T)
    nc = T // L

    def to_chunks(a):
        a = a.reshape((B, nc, L, H) + a.shape[3:])
        return jnp.moveaxis(a, (1, 3), (0, 2))

    causal = jnp.tril(jnp.ones((L, L), bool))

    def step(carry, xs):
        C, n, m = carry
        qc, kc, vc, ic, fc = xs
        b = jnp.cumsum(fc, axis=-1)
        D = jnp.where(causal, b[..., :, None] - b[..., None, :] + ic[..., None, :], NEG_INF)
        m_inter = b + m[..., None]
        m_t = jnp.maximum(m_inter, D.max(-1))
        S = jnp.einsum('bhld,bhsd->bhls', qc, kc) * jnp.exp(D - m_t[..., None])
        w_inter = jnp.exp(m_inter - m_t)
        num = jnp.einsum('bhls,bhsd->bhld', S, vc) + w_inter[..., None] * jnp.einsum('bhvk,bhlk->bhlv', C, qc)
        den = S.sum(-1) + w_inter * jnp.einsum('bhk,bhlk->bhl', n, qc)
        h = num / jnp.maximum(jnp.abs(den), jnp.exp(-m_t))[..., None]
        b_last = b[..., -1]
        g = b_last[..., None] - b + ic
        m_new = jnp.maximum(b_last + m, g.max(-1))
        w = jnp.exp(g - m_new[..., None])
        decay = jnp.exp(b_last + m - m_new)
        C_new = decay[..., None, None] * C + jnp.einsum('bhs,bhsv,bhsk->bhvk', w, vc, kc)
        n_new = decay[..., None] * n + jnp.einsum('bhs,bhsk->bhk', w, kc)
        return (C_new, n_new, m_new), h

    xs = (to_chunks(q), to_chunks(k), to_chunks(v), to_chunks(ig), to_chunks(lf))
    (C, n, m), h = lax.scan(step, (C0, n0, m0), xs)
    h = jnp.moveaxis(h, (0, 2), (1, 3)).reshape(B, T, H, Dh)
    return h, C, n, m


def mlstm_layer(x, w_in, b_gates, norm_g, w_out, C0, n0, m0):
    B, T, _ = x.shape
    proj = x @ w_in
    q, k, v, o = jnp.split(proj[..., :4 * D_MODEL], 4, axis=-1)
    gates = proj[..., 4 * D_MODEL:].astype(jnp.float32) + b_gates
    ig = gates[..., :A_HEADS]
    lf = jax.nn.log_sigmoid(gates[..., A_HEADS:])

    def heads(a):
        return a.astype(jnp.float32).reshape(B, T, A_HEADS, A_HEAD_DIM)

    h, C, n, m = mlstm_chunkwise(heads(q), heads(k) / math.sqrt(A_HEAD_DIM), heads(v), ig, lf,
                                 C0.astype(jnp.float32), n0.astype(jnp.float32), m0.astype(jnp.float32))
    mu = h.mean(-1, keepdims=True)
    var = jnp.square(h - mu).mean(-1, keepdims=True)
    h = ((h - mu) * lax.rsqrt(var + LN_EPS)).reshape(B, T, D_MODEL) * norm_g
    h = jax.nn.sigmoid(o.astype(jnp.float32)) * h
    return h.astype(x.dtype) @ w_out, C, n, m


def moba_context(k_ctx, v_ctx):
    B, Tk, H, Dh = k_ctx.shape
    nb = -(-Tk // MOBA_BLOCK)
    pad = ((0, 0), (0, nb * MOBA_BLOCK - Tk), (0, 0), (0, 0))
    kb = jnp.pad(k_ctx, pad).reshape(B, nb, MOBA_BLOCK, H, Dh)
    vb = jnp.pad(v_ctx, pad).reshape(B, nb, MOBA_BLOCK, H, Dh)
    k_mean = kb.astype(jnp.float32).mean(axis=2)
    return kb, vb, k_mean


def moba_attention(q, q_pos, k_blocks, v_blocks, k_mean):
    B, Tq, H, Dh = q.shape
    nbk = k_blocks.shape[1]
    own = q_pos // MOBA_BLOCK
    gate = jnp.einsum('bqhd,bnhd->bqhn', q.astype(jnp.float32), k_mean)
    is_past = jnp.arange(nbk)[None, :] < own[:, None]
    gate = jnp.where(is_past[None, :, None, :], gate, NEG_INF)
    _, top_idx = lax.top_k(gate, min(MOBA_TOPK, nbk))
    own_b = jnp.broadcast_to(own[None, :, None, None], (B, Tq, H, 1))
    blocks = jnp.concatenate([top_idx, own_b], -1)
    valid = jnp.concatenate([top_idx < own[None, :, None, None], jnp.ones((B, Tq, H, 1), bool)], -1)
    n_slot = blocks.shape[-1]
    qc = math.gcd(MOBA_Q_CHUNK, Tq)
    nq = Tq // qc

    def rs(a):
        return a.reshape((B * nq, qc) + a.shape[2:])

    pos_c = jnp.broadcast_to(q_pos.reshape(1, nq, qc), (B, nq, qc)).reshape(B * nq, qc)
    bid = jnp.repeat(jnp.arange(B), nq)
    h_idx = jnp.arange(H)[None, :, None]
    offs = jnp.arange(MOBA_BLOCK)
    scale = Dh ** -0.5

    def one(args):
        qb, blk, val, pb, b = args
        kg = k_blocks[b, blk, :, h_idx]
        vg = v_blocks[b, blk, :, h_idx]
        key_pos = blk[..., None] * MOBA_BLOCK + offs
        mask = val[..., None] & (key_pos <= pb[:, None, None, None])
        s = jnp.einsum('qhd,qhskd->qhsk', qb, kg).astype(jnp.float32) * scale
        s = jnp.where(mask, s, NEG_INF).reshape(qc, H, n_slot * MOBA_BLOCK)
        p = jax.nn.softmax(s, axis=-1).reshape(qc, H, n_slot, MOBA_BLOCK)
        return jnp.einsum('qhsk,qhskd->qhd', p.astype(vg.dtype), vg)

    out = lax.map(one, (rs(q), rs(blocks), rs(valid), pos_c, bid))
    return out.reshape(B, Tq, H, Dh)


def peer_ffn(x, w_q, sub_keys, u_tab, v_tab):
    B, T, D = x.shape
    N = B * T
    xt = x.reshape(N, D)
    q = (xt @ w_q).astype(jnp.float32).reshape(N, PK_HEADS, 2, PK_DIM // 2)
    s = jnp.einsum('nhpd,hpkd->nhpk', q, sub_keys.astype(jnp.float32))
    s_top, i_top = lax.top_k(s, PK_TOPK)
    cand = s_top[:, :, 0, :, None] + s_top[:, :, 1, None, :]
    cand_idx = i_top[:, :, 0, :, None] * N_KEYS + i_top[:, :, 1, None, :]
    g_s, g_i = lax.top_k(cand.reshape(N, PK_HEADS, PK_TOPK * PK_TOPK), PK_TOPK)
    experts = jnp.take_along_axis(cand_idx.reshape(N, PK_HEADS, PK_TOPK * PK_TOPK), g_i, axis=-1)
    gate = jax.nn.softmax(g_s, axis=-1)
    nb = -(-N // PEER_BLOCK)
    pad = nb * PEER_BLOCK - N

    def padr(a):
        return jnp.pad(a, ((0, pad),) + ((0, 0),) * (a.ndim - 1))

    E = PK_HEADS * PK_TOPK
    xs = (padr(xt).reshape(nb, PEER_BLOCK, D),
          padr(experts.reshape(N, E)).reshape(nb, PEER_BLOCK, E),
          padr(gate.reshape(N, E)).reshape(nb, PEER_BLOCK, E))

    def block(args):
        xb, eb, gb = args
        act = jax.nn.gelu(jnp.einsum('pd,ped->pe', xb, u_tab[eb]).astype(jnp.float32))
        return jnp.einsum('pe,ped->pd', (gb * act).astype(x.dtype), v_tab[eb])

    y = lax.map(block, xs).reshape(nb * PEER_BLOCK, D)[:N]
    return y.reshape(B, T, D)


def trunk(x, pos, C0, n0, m0, k_past, v_past, ln_g, ln_b, w_in_a, b_gates_a, norm_a, w_out_a,
          w_kv, w_q_b, w_out_b, peer_wq, peer_keys, peer_u, peer_v):
    B, T, _ = x.shape
    Cs, ns, ms = [], [], []
    k_rows = v_rows = ctx = None
    for l in range(DEPTH):
        if l < N_A_LAYERS:
            y, C, n, m = mlstm_layer(x, w_in_a[l], b_gates_a[l], norm_a[l], w_out_a[l], C0[l], n0[l], m0[l])
            Cs.append(C)
            ns.append(n)
            ms.append(m)
        else:
            if l == N_A_LAYERS:
                kv = x @ w_kv
                k_rows = rope(kv[..., :D_MODEL].reshape(B, T, B_HEADS, B_HEAD_DIM), pos)
                v_rows = kv[..., D_MODEL:].reshape(B, T, B_HEADS, B_HEAD_DIM)
                if k_past is None:
                    ctx = moba_context(k_rows, v_rows)
                else:
                    ctx = moba_context(jnp.concatenate([k_past, k_rows.astype(k_past.dtype)], axis=1),
                                       jnp.concatenate([v_past, v_rows.astype(v_past.dtype)], axis=1))
            j = l - N_A_LAYERS
            q = rope((x @ w_q_b[j]).reshape(B, T, B_HEADS, B_HEAD_DIM), pos)
            y = moba_attention(q, pos, *ctx).reshape(B, T, D_MODEL).astype(x.dtype) @ w_out_b[j]
        x = layer_norm(DN_ALPHA * x + y, ln_g[l, 0], ln_b[l, 0])
        x = layer_norm(DN_ALPHA * x + peer_ffn(x, peer_wq[l], peer_keys[l], peer_u[l], peer_v[l]),
                       ln_g[l, 1], ln_b[l, 1])
    return x, jnp.stack(Cs), jnp.stack(ns), jnp.stack(ms), k_rows, v_rows


def setup_inputs(seed: int = 0) -> dict:
    key = jax.random.key(seed)
    ks = jax.random.split(key, 24)
    f32 = jnp.float32
    D = D_MODEL
    n_pages = PAST_LEN // PAGE_SIZE
    n_used = DEC_BATCH * n_pages
    n_phys = (n_used * 5) // 4
    nrm = jax.random.normal
    x_prompt = nrm(ks[0], (BATCH, SEQ, D), f32)
    x_sample = nrm(ks[1], (DEC_BATCH, DEC_SEQ, D), f32)
    state_C = 0.02 * nrm(ks[2], (N_A_LAYERS, DEC_BATCH, A_HEADS, A_HEAD_DIM, A_HEAD_DIM), f32)
    state_n = 0.1 * nrm(ks[3], (N_A_LAYERS, DEC_BATCH, A_HEADS, A_HEAD_DIM), f32)
    state_m = 1.0 + 0.5 * nrm(ks[4], (N_A_LAYERS, DEC_BATCH, A_HEADS), f32)
    cache_k = nrm(ks[5], (n_phys, PAGE_SIZE, B_HEADS, B_HEAD_DIM), f32)
    cache_v = 0.5 * nrm(ks[6], (n_phys, PAGE_SIZE, B_HEADS, B_HEAD_DIM), f32)
    page_table = jax.random.permutation(ks[7], n_phys)[:n_used].reshape(DEC_BATCH, n_pages).astype(jnp.int32)
    ln_g = 1.0 + 0.02 * nrm(ks[8], (DEPTH, 2, D), f32)
    ln_b = 0.02 * nrm(ks[9], (DEPTH, 2, D), f32)
    w_in_a = nrm(ks[10], (N_A_LAYERS, D, 4 * D + 2 * A_HEADS), f32) * D ** -0.5
    w_in_a = w_in_a.at[:, :, 2 * D:3 * D].multiply(DN_BETA)
    b_i = 0.1 * nrm(ks[11], (N_A_LAYERS, A_HEADS), f32)
    b_f = jnp.linspace(3.0, 6.0, A_HEADS, dtype=f32)[None, :] + 0.1 * nrm(ks[12], (N_A_LAYERS, A_HEADS), f32)
    b_gates_a = jnp.concatenate([b_i, b_f], axis=-1)
    norm_a = 1.0 + 0.02 * nrm(ks[13], (N_A_LAYERS, D), f32)
    w_out_a = nrm(ks[14], (N_A_LAYERS, D, D), f32) * (D ** -0.5 * DN_BETA)
    w_kv = nrm(ks[15], (D, 2 * D), f32) * D ** -0.5
    w_kv = w_kv.at[:, D:].multiply(DN_BETA)
    w_q_b = nrm(ks[16], (N_B_LAYERS, D, D), f32) * D ** -0.5
    w_out_b = nrm(ks[17], (N_B_LAYERS, D, D), f32) * (D ** -0.5 * DN_BETA)
    peer_wq = nrm(ks[18], (DEPTH, D, PK_HEADS * PK_DIM), f32) * D ** -0.5
    peer_keys = nrm(ks[19], (DEPTH, PK_HEADS, 2, N_KEYS, PK_DIM // 2), f32) * (PK_DIM // 2) ** -0.5
    peer_u = nrm(ks[20], (DEPTH, N_EXPERTS, D), f32) * D ** -0.5
    peer_v = nrm(ks[21], (DEPTH, N_EXPERTS, D), f32) * ((PK_HEADS * PK_TOPK) ** -0.5 * DN_BETA)
    return {'x_prompt': x_prompt, 'x_sample': x_sample, 'state_C': state_C, 'state_n': state_n,
            'state_m': state_m, 'cache_k': cache_k, 'cache_v': cache_v, 'page_table': page_table,
            'ln_g': ln_g, 'ln_b': ln_b, 'w_in_a': w_in_a, 'b_gates_a': b_gates_a, 'norm_a': norm_a,
            'w_out_a': w_out_a, 'w_kv': w_kv, 'w_q_b': w_q_b, 'w_out_b': w_out_b, 'peer_wq': peer_wq,
            'peer_keys': peer_keys, 'peer_u': peer_u, 'peer_v': peer_v}


def reference(x_prompt, x_sample, state_C, state_n, state_m, cache_k, cache_v, page_table,
              ln_g, ln_b, w_in_a, b_gates_a, norm_a, w_out_a, w_kv, w_q_b, w_out_b,
              peer_wq, peer_keys, peer_u, peer_v):
    Bp, Tp, _ = x_prompt.shape
    Bs, Ts, _ = x_sample.shape
    past_len = page_table.shape[1] * cache_k.shape[1]
    zC = jnp.zeros((N_A_LAYERS, Bp, A_HEADS, A_HEAD_DIM, A_HEAD_DIM), jnp.float32)
    zn = jnp.zeros((N_A_LAYERS, Bp, A_HEADS, A_HEAD_DIM), jnp.float32)
    zm = jnp.zeros((N_A_LAYERS, Bp, A_HEADS), jnp.float32)
    y_prompt, C_p, n_p, m_p, k_p, v_p = trunk(
        x_prompt, jnp.arange(Tp), zC, zn, zm, None, None, ln_g, ln_b, w_in_a, b_gates_a, norm_a,
        w_out_a, w_kv, w_q_b, w_out_b, peer_wq, peer_keys, peer_u, peer_v)
    k_past = cache_k[page_table].reshape(Bs, past_len, B_HEADS, B_HEAD_DIM)
    v_past = cache_v[page_table].reshape(Bs, past_len, B_HEADS, B_HEAD_DIM)
    y_sample, C_s, n_s, m_s, k_s, v_s = trunk(
        x_sample, past_len + jnp.arange(Ts), state_C, state_n, state_m, k_past, v_past, ln_g, ln_b,
        w_in_a, b_gates_a, norm_a, w_out_a, w_kv, w_q_b, w_out_b, peer_wq, peer_keys, peer_u, peer_v)
    return (y_prompt, y_sample, C_p, n_p, m_p, k_p, v_p, C_s, n_s, m_s, k_s, v_s)
```

```python
from contextlib import ExitStack
import numpy as np
import concourse.bass as bass
import concourse.mybir as mybir
from concourse.bass_utils import run_bass_kernel_spmd

F32 = mybir.dt.float32
BF16 = mybir.dt.bfloat16
I32 = mybir.dt.int32
U32 = mybir.dt.uint32
AF = mybir.ActivationFunctionType
ALU = mybir.AluOpType
AX = mybir.AxisListType

D = 1024
NT_P = 32
NT_S = 2
NT = NT_P + NT_S
NTOK = NT * 128
NSEQ_S = 32
NSEQ = 1 + NSEQ_S
LN_EPS = 1e-5
DN_ALPHA = (2.0 * 4) ** 0.25
NEG = -30000.0


class Buf:
    __slots__ = ("name", "w", "rs")

    def __init__(self, name=""):
        self.name = name
        self.w = None
        self.rs = {}


class Eng:
    SEM_LIMIT = 15000

    def __init__(self, S, name, handle, npool=10, self_sync=True):
        self.S = S
        self.name = name
        self.h = handle
        self.self_sync = self_sync
        self.waited = {}
        self.count = 0
        self.sem = None
        self.semkey = None
        self.nsem = 0
        self.pool = []
        self.pool_i = 0
        self.npool = npool
        self.n_inst = 0
        self.n_wait = 0

    def _new_sem(self):
        self.nsem += 1
        self.sem = self.S.nc.alloc_semaphore(name=f"s_{self.name}_{self.nsem}")
        self.semkey = f"{self.name}_{self.nsem}"
        self.count = 0

    def wait_tok(self, tok):
        semkey, sem, val = tok
        if self.waited.get(semkey, 0) >= val:
            return
        self.h.wait_ge(sem, val)
        self.n_wait += 1
        self.waited[semkey] = val


class Sched:
    def __init__(self, nc):
        self.nc = nc
        self.pe = Eng(self, "pe", nc.tensor, self_sync=False)
        self.dve = Eng(self, "dve", nc.vector)
        self.act = Eng(self, "act", nc.scalar)
        self.pool = Eng(self, "pool", nc.gpsimd, npool=48)
        self.sp = Eng(self, "sp", nc.sync)
        self.engs = [self.pe, self.dve, self.act, self.pool, self.sp]

    def emit(self, eng, fn, r=(), w=(), dma=False):
        toks = {}

        def add(tok):
            k = tok[0]
            if k not in toks or toks[k][2] < tok[2]:
                toks[k] = tok
        for b in r:
            if b.w is not None:
                add(b.w)
        for b in w:
            if b.w is not None:
                add(b.w)
            for k, (sem, val) in b.rs.items():
                add((k, sem, val))
        for k, tok in toks.items():
            if (not dma) and (not eng.self_sync) and k == eng.semkey:
                continue
            eng.wait_tok(tok)
        if dma:
            if len(eng.pool) < eng.npool:
                nm = f"d_{eng.name}_{len(eng.pool)}"
                ent = [nm, self.nc.alloc_semaphore(name=nm), 0]
                eng.pool.append(ent)
            else:
                ent = eng.pool[eng.pool_i % eng.npool]
            eng.pool_i += 1
            if ent[2] > 0:
                eng.wait_tok((ent[0], ent[1], ent[2]))
            inst = fn()
            ent[2] += 16
            inst.then_inc(ent[1], 16)
            tok = (ent[0], ent[1], ent[2])
        else:
            if eng.sem is None or eng.count >= Eng.SEM_LIMIT:
                eng._new_sem()
            inst = fn()
            eng.count += 1
            inst.then_inc(eng.sem, 1)
            tok = (eng.semkey, eng.sem, eng.count)
        eng.n_inst += 1
        k = tok[0]
        for b in r:
            if k not in b.rs or b.rs[k][1] < tok[2]:
                b.rs[k] = (tok[1], tok[2])
        for b in w:
            b.w = tok
            b.rs = {}
        return tok

    def finish(self, bufs):
        for b in bufs:
            if b.w is not None:
                self.sp.wait_tok(b.w)


class T:
    def __init__(self, h, name, psum=False):
        self.h = h
        self.b = Buf(name)
        self.b_psum = psum

    def __getitem__(self, idx):
        return self.h[idx]


class K:
    def __init__(self, nc):
        self.nc = nc
        self.S = Sched(nc)
        self.stack = None
        self.uid = 0

    def sb(self, name, shape, dt, stack=None):
        self.uid += 1
        st = stack if stack is not None else self.stack
        h = st.enter_context(self.nc.sbuf_tensor(f"{name}_{self.uid}", list(shape), dt))
        return T(h, name)

    def ps(self, name, shape, dt, stack=None):
        self.uid += 1
        st = stack if stack is not None else self.stack
        h = st.enter_context(self.nc.psum_tensor(f"{name}_{self.uid}", list(shape), dt))
        return T(h, name, psum=True)

    def _rw(self, r, w):
        rb, wb = [], []
        for x in r:
            if isinstance(x, T):
                (wb if x.b_psum else rb).append(x.b)
            else:
                rb.append(x)
        for x in w:
            wb.append(x.b if isinstance(x, T) else x)
        return dict(r=rb, w=wb)

    def dma(self, out, in_, r=(), w=(), q=None, slow=False):
        S = self.S
        eng = q or S.sp
        kw = dict(allow_slow_non_contiguous=True) if slow else {}
        return S.emit(eng, lambda: eng.h.dma_start(out=out, in_=in_, **kw), **self._rw(r, w), dma=True)

    def mm(self, out, lhsT, rhs, start, stop, r=(), w=()):
        nc = self.nc
        return self.S.emit(self.S.pe, lambda: nc.tensor.matmul(out, lhsT, rhs, start=start, stop=stop),
                           **self._rw(r, w))

    def tr(self, out, in_, ident, r=(), w=()):
        nc = self.nc
        return self.S.emit(self.S.pe, lambda: nc.tensor.transpose(out, in_, ident), **self._rw(r, w))

    def act(self, out, in_, func, r=(), w=(), bias=None, scale=None):
        nc = self.nc
        kw = {}
        if bias is not None:
            kw["bias"] = bias
        if scale is not None:
            kw["scale"] = scale
        return self.S.emit(self.S.act, lambda: nc.scalar.activation(out=out, in_=in_, func=func, **kw),
                           **self._rw(r, w))

    def v(self, eng, fn, r=(), w=()):
        return self.S.emit(eng, fn, **self._rw(r, w))

    def ts(self, eng, out, in0, s1, op0, s2=None, op1=None, r=(), w=()):
        kw = dict(out=out, in0=in0, scalar1=s1, scalar2=s2, op0=op0)
        if op1 is not None:
            kw["op1"] = op1
        return self.S.emit(eng, lambda: eng.h.tensor_scalar(**kw), **self._rw(r, w))

    def tt(self, eng, out, in0, in1, op, r=(), w=()):
        return self.S.emit(eng, lambda: eng.h.tensor_tensor(out=out, in0=in0, in1=in1, op=op),
                           **self._rw(r, w))

    def cp(self, eng, out, in_, r=(), w=()):
        return self.S.emit(eng, lambda: eng.h.tensor_copy(out=out, in_=in_), **self._rw(r, w))


def bc_rows(dram_ap_row, n):
    return dram_ap_row.to_broadcast([128, n])


def build_program(phases=("A0",), debug=False):
    nc = bass.Bass("TRN2", target_bir_lowering=False)
    kb = K(nc)
    S = kb.S
    DVE, ACT, POOL, PE = S.dve, S.act, S.pool, S.pe

    def din(name, shape, dt=F32):
        return nc.dram_tensor(name, list(shape), dt, kind="ExternalInput").ap()

    def dout(name, shape, dt=F32):
        return nc.dram_tensor(name, list(shape), dt, kind="ExternalOutput").ap()

    def dint(name, shape, dt=F32):
        return nc.dram_tensor(name, list(shape), dt, kind="Internal").ap()

    x_in = din("x_in", [NTOK, D])
    stC_in = din("stC_in", [2, NSEQ_S, 4, 256, 256])
    stn_in = din("stn_in", [2, NSEQ_S, 4, 256])
    stm_in = din("stm_in", [2, NSEQ_S, 4])
    w_in_a = din("w_in_a", [2, D, 4104])
    b_gates_a = din("b_gates_a", [2, 8])
    norm_a = din("norm_a", [2, D])
    w_out_a = din("w_out_a", [2, D, D])
    ln_g = din("ln_g", [4, 2, D])
    ln_b = din("ln_b", [4, 2, D])
    w_kv = din("w_kv", [D, 2048])
    w_q_b = din("w_q_b", [2, D, D])
    w_out_b = din("w_out_b", [2, D, D])
    cache_k = din("cache_k", [2560 * 128, D])
    cache_v = din("cache_v", [2560 * 128, D])
    page_table = din("page_table", [NSEQ_S, 16], I32)
    k_rows = dout("k_rows", [NTOK, D])
    v_rows = dout("v_rows", [NTOK, D])
    peer_wq = din("peer_wq", [4, D, 2048])
    peer_keysT = din("peer_keysT", [4, 16, 128, 128])
    peer_u = [din(f"peer_u{i}", [16384, D]) for i in range(4)]
    peer_v = [din(f"peer_v{i}", [16384, D]) for i in range(4)]

    y_out = dout("y_out", [NTOK, D])
    C_out = dout("C_out", [2, NSEQ, 4, 256, 256])
    n_out = dout("n_out", [2, NSEQ, 4, 256])
    m_out = dout("m_out", [2, NSEQ, 4])
    out_bufs = {k: Buf(k) for k in ("y", "C", "n", "m", "k", "v")}

    xs = [dint("xs0", [NTOK, D]), dint("xs1", [NTOK, D])]
    xs_b = [Buf("xs0"), Buf("xs1")]

    root = ExitStack()
    kb.stack = root
    ident_f = kb.sb("ident_f", [128, 128], F32)
    ident_b = kb.sb("ident_b", [128, 128], BF16)
    iota_pi = kb.sb("iota_pi", [128, 1], I32)
    iota_fi = kb.sb("iota_fi", [128, 128], I32)
    iota_p = kb.sb("iota_p", [128, 1], F32)
    iota_f = kb.sb("iota_f", [128, 128], F32)
    pdiv = kb.sb("pdiv", [128, 1], F32)
    fdiv = kb.sb("fdiv", [128, 128], F32)
    tmpi = kb.sb("tmpi", [128, 128], I32)
    maskP = kb.sb("maskP", [128, 128], F32)
    maskS = kb.sb("maskS", [128, 128], F32)
    seq1h = kb.sb("seq1h", [128, 16], F32)
    sel4 = kb.sb("sel4", [4, 4, 128], F32)
    scn = kb.sb("scn", [4, 4, 128], F32)
    one_col = kb.sb("one_col", [128, 1], F32)
    eps_col = kb.sb("eps_col", [128, 1], F32)

    kb.v(POOL, lambda: nc.gpsimd.iota(iota_pi[:], pattern=[[0, 1]], base=0, channel_multiplier=1), w=[iota_pi])
    kb.v(POOL, lambda: nc.gpsimd.iota(iota_fi[:], pattern=[[1, 128]], base=0, channel_multiplier=0), w=[iota_fi])
    kb.cp(DVE, iota_p[:], iota_pi[:], r=[iota_pi], w=[iota_p])
    kb.cp(DVE, iota_f[:], iota_fi[:], r=[iota_fi], w=[iota_f])
    kb.v(DVE, lambda: nc.vector.tensor_single_scalar(out=tmpi[:, 0:1], in_=iota_pi[:], scalar=3, op=ALU.arith_shift_right),
         r=[iota_pi], w=[tmpi])
    kb.cp(DVE, pdiv[:], tmpi[:, 0:1], r=[tmpi], w=[pdiv])
    kb.v(DVE, lambda: nc.vector.tensor_single_scalar(out=tmpi[:], in_=iota_fi[:], scalar=3, op=ALU.arith_shift_right),
         r=[iota_fi], w=[tmpi])
    kb.cp(DVE, fdiv[:], tmpi[:], r=[tmpi], w=[fdiv])
    kb.ts(DVE, ident_f[:], iota_f[:], iota_p[:, 0:1], ALU.is_equal, r=[iota_f, iota_p], w=[ident_f])
    kb.cp(DVE, ident_b[:], ident_f[:], r=[ident_f], w=[ident_b])
    kb.ts(DVE, maskP[:], iota_f[:], iota_p[:, 0:1], ALU.is_ge, -1.0, ALU.add, r=[iota_f, iota_p], w=[maskP])
    kb.ts(DVE, maskP[:], maskP[:], -NEG, ALU.mult, r=[maskP], w=[maskP])
    kb.ts(DVE, maskS[:], iota_f[:], iota_p[:, 0:1], ALU.is_ge, r=[iota_f, iota_p], w=[maskS])
    kb.ts(DVE, fdiv[:], fdiv[:], pdiv[:, 0:1], ALU.is_equal, r=[fdiv, pdiv], w=[fdiv])
    kb.tt(DVE, maskS[:], maskS[:], fdiv[:], ALU.mult, r=[maskS, fdiv], w=[maskS])
    kb.ts(DVE, maskS[:], maskS[:], -1.0, ALU.add, -NEG, ALU.mult, r=[maskS], w=[maskS])
    kb.ts(DVE, seq1h[:], iota_f[:, 0:16], pdiv[:, 0:1], ALU.is_equal, r=[iota_f, pdiv], w=[seq1h])
    kb.cp(DVE, sel4[:], ident_f[0:4, 0:4].unsqueeze(2).to_broadcast([4, 4, 128]), r=[ident_f], w=[sel4])
    kb.v(DVE, lambda: nc.vector.memset(scn[:, 0, :], 1.0), w=[scn])
    kb.v(DVE, lambda: nc.vector.memset(scn[:, 0, 0:1], 0.0), w=[scn])
    kb.v(DVE, lambda: nc.vector.memset(scn[:, 1, :], 0.0), w=[scn])
    kb.v(DVE, lambda: nc.vector.memset(scn[:, 1, 0:1], -1e30), w=[scn])
    kb.v(DVE, lambda: nc.vector.memset(scn[:, 2, :], 1.0), w=[scn])
    kb.v(DVE, lambda: nc.vector.memset(scn[:, 3, :], 0.0), w=[scn])
    kb.v(DVE, lambda: nc.vector.memset(scn[:, 2, :].rearrange("p (j t) -> p j t", t=8)[:, :, 0:1], 0.0), w=[scn])
    kb.v(DVE, lambda: nc.vector.memset(scn[:, 3, :].rearrange("p (j t) -> p j t", t=8)[:, :, 0:1], -1e30), w=[scn])
    kb.v(DVE, lambda: nc.vector.memset(one_col[:], 1.0), w=[one_col])
    kb.v(DVE, lambda: nc.vector.memset(eps_col[:], LN_EPS), w=[eps_col])

    C = dict(ident_f=ident_f, ident_b=ident_b, maskP=maskP, maskS=maskS, seq1h=seq1h, sel4=sel4, scn=scn,
             one_col=one_col, eps_col=eps_col)

    def layer_norm_tile(z, lng, lnb, out_t, tmp_stats):
        st, mv, rstd = tmp_stats
        for i in range(2):
            kb.v(DVE, lambda i=i: nc.vector.bn_stats(out=st[:, i, :], in_=z[:, i * 512:(i + 1) * 512]), r=[z], w=[st])
        kb.v(DVE, lambda: nc.vector.bn_aggr(out=mv[:], in_=st[:].rearrange("p a b -> p (a b)")), r=[st], w=[mv])
        kb.act(rstd[:], mv[:, 1:2], AF.Sqrt, r=[mv, eps_col], w=[rstd], bias=eps_col[:, 0:1], scale=1.0)
        kb.v(DVE, lambda: nc.vector.reciprocal(out=rstd[:], in_=rstd[:]), r=[rstd], w=[rstd])
        kb.ts(DVE, out_t[:], z[:], mv[:, 0:1], ALU.subtract, rstd[:, 0:1], ALU.mult, r=[z, mv, rstd], w=[out_t])
        kb.tt(POOL, out_t[:], out_t[:], lng[:], ALU.mult, r=[out_t, lng], w=[out_t])
        kb.tt(POOL, out_t[:], out_t[:], lnb[:], ALU.add, r=[out_t, lnb], w=[out_t])

    def mlstm_layer(l, src, src_b, dst, dst_b):
        with ExitStack() as ph:
            kb.stack = ph
            wbf = kb.sb("wbf", [128, 8, 4104], BF16)
            wout = kb.sb("wout", [128, 8, D], BF16)
            wst = [kb.sb("wst", [128, 1026], F32) for _ in range(2)]
            normg = kb.sb("normg", [128, D], F32)
            lng = kb.sb("lng", [128, D], F32)
            lnb = kb.sb("lnb", [128, D], F32)
            bi_col = kb.sb("bi_col", [4, 1], F32)
            nbf_col = kb.sb("nbf_col", [4, 1], F32)
            i = 0
            for kc in range(8):
                for q4 in range(4):
                    st = wst[i % 2]
                    n = 1026
                    kb.dma(st[:, 0:n], w_in_a[l, kc * 128:(kc + 1) * 128, q4 * 1026:q4 * 1026 + n], w=[st])
                    eng = (DVE, POOL)[i % 2]
                    kb.cp(eng, wbf[:, kc, q4 * 1026:q4 * 1026 + n], st[:, 0:n], r=[st], w=[wbf])
                    i += 1
            for kc in range(8):
                st = wst[i % 2]
                kb.dma(st[:, 0:1024], w_out_a[l, kc * 128:(kc + 1) * 128, :], w=[st])
                eng = (DVE, POOL)[i % 2]
                kb.cp(eng, wout[:, kc, :], st[:, 0:1024], r=[st], w=[wout])
                i += 1
            kb.dma(normg[:], bc_rows(norm_a[l:l + 1, :], D), w=[normg])
            kb.dma(lng[:], bc_rows(ln_g[l, 0:1, :], D), w=[lng])
            kb.dma(lnb[:], bc_rows(ln_b[l, 0:1, :], D), w=[lnb])
            kb.dma(bi_col[:], b_gates_a[l, 0:4].rearrange("(p o) -> p o", o=1), w=[bi_col], slow=True)
            kb.dma(nbf_col[:], b_gates_a[l, 4:8].rearrange("(p o) -> p o", o=1), w=[nbf_col], slow=True)
            kb.ts(DVE, nbf_col[:], nbf_col[:], -1.0, ALU.mult, r=[nbf_col], w=[nbf_col])

            x_t = [kb.sb("x_t", [128, D], F32) for _ in range(2)]
            x_b = kb.sb("x_b", [128, D], BF16)
            xT = kb.sb("xT", [128, 8, 128], BF16)
            qT = kb.sb("qT", [128, 8, 128], BF16)
            kT = kb.sb("kT", [128, 8, 128], BF16)
            k_tm = kb.sb("k_tm", [128, D], BF16)
            v_aug = kb.sb("v_aug", [128, 4, 257], BF16)
            sig_o = kb.sb("sig_o", [128, D], BF16)
            hfin = kb.sb("hfin", [128, D], F32)
            hfin_b = x_b
            hT = xT
            z_t = kb.sb("z_t", [128, D], F32)
            xo_t = z_t
            rows = kb.sb("rows", [4, 12, 128], F32)
            seqr = kb.sb("seqr", [4, 4, 16], F32)
            cols = kb.sb("cols", [128, 3, 4], F32)
            decay_bc = kb.sb("decay_bc", [128, 4, 16], F32)
            E_sb = kb.sb("E_sb", [128, 128], F32)
            PT = kb.sb("PT", [128, 128], BF16)
            wi_bc = kb.sb("wi_bc", [128, 128], F32)
            qpT = kb.sb("qpT", [128, 2, 2176], BF16)
            wm = kb.sb("wm", [128, 16], F32)
            wv_blk = kb.sb("wv_blk", [128, 16, 257], BF16)
            dm = kb.sb("dm", [128, 2], F32)
            hraw = kb.sb("hraw", [128, 256], F32)
            bst = kb.sb("bst", [128, 6], F32)
            bmv = kb.sb("bmv", [128, 2], F32)
            brs = kb.sb("brs", [128, 1], F32)
            lst = kb.sb("lst", [128, 2, 6], F32)
            lmv = kb.sb("lmv", [128, 2], F32)
            lrs = kb.sb("lrs", [128, 1], F32)
            CTp = kb.sb("CTp", [128, 4, 2, 257], F32)
            CTbp = kb.sb("CTbp", [128, 4, 2, 257], BF16)
            CTf = [kb.sb("CTf", [128, 2, 257], F32) for _ in range(2)]
            CTbs = kb.sb("CTbs", [128, 2, 2, 257], BF16)
            Cio = [kb.sb("Cio", [128, 2, 256], F32) for _ in range(2)]
            nio = kb.sb("nio", [16, 256], F32)
            m_carry = kb.sb("m_carry", [4, 1], F32)
            p_a = [kb.ps("p_a", [128, 512], F32) for _ in range(2)]
            p_tr = kb.ps("p_tr", [128, 1024], BF16)
            p_e = kb.ps("p_e", [128, 512], F32)
            p_s = kb.ps("p_s", [128, 512], F32)
            p_n = kb.ps("p_n", [128, 512], F32)
            p_u = [kb.ps("p_u", [128, 512], F32) for _ in range(2)]
            pe_b = pw_b = pd_b = pc_b = p_e
            ps_b = pg_b = p_s

            kb.v(DVE, lambda: nc.vector.memset(v_aug[:, :, 256:257], 1.0), w=[v_aug])
            kb.v(DVE, lambda: nc.vector.memset(qpT[:], 0.0), w=[qpT])
            kb.v(POOL, lambda: nc.gpsimd.memset(CTp[:], 0.0), w=[CTp])
            kb.v(POOL, lambda: nc.gpsimd.memset(CTbp[:], 0.0), w=[CTbp])
            kb.v(DVE, lambda: nc.vector.memset(m_carry[:], 0.0), w=[m_carry])

            pa_i = [0]

            def next_pa():
                pa_i[0] += 1
                return p_a[pa_i[0] % 2]

            def state_in(h, j, seq, ctf):
                cio = Cio[j % 2]
                kb.dma(cio[:], stC_in[l, seq, h].rearrange("(vc p) k -> p vc k", p=128), w=[cio])
                pt = next_pa()
                for vc in range(2):
                    for kc in range(2):
                        kb.tr(pt[:, kc * 256 + vc * 128: kc * 256 + vc * 128 + 128], cio[:, vc, kc * 128:(kc + 1) * 128],
                              ident_f[:], r=[cio, ident_f], w=[pt])
                kb.cp(DVE, ctf[:, :, 0:256], pt[:].rearrange("p (c v) -> p c v", c=2), r=[pt], w=[ctf])

            def state_out(h, seq_out, ctf, j):
                cio = Cio[j % 2]
                pt = next_pa()
                for vc in range(2):
                    for kc in range(2):
                        kb.tr(pt[:, vc * 256 + kc * 128: vc * 256 + kc * 128 + 128], ctf[:, kc, vc * 128:(vc + 1) * 128],
                              ident_f[:], r=[ctf, ident_f], w=[pt])
                kb.cp(ACT_or_DVE(j), cio[:], pt[:].rearrange("p (c v) -> p c v", c=2), r=[pt], w=[cio])
                kb.dma(C_out[l, seq_out, h].rearrange("(vc p) k -> p vc k", p=128), cio[:], r=[cio], w=[out_bufs["C"]])

            def ACT_or_DVE(j):
                return DVE

            def tile_step(ti):
                is_p = ti < NT_P
                nseq = 1 if is_p else 16
                L = 128 // nseq
                mask = maskP if is_p else maskS
                r_row = scn[:, 0, :] if is_p else scn[:, 2, :]
                a_row = scn[:, 1, :] if is_p else scn[:, 3, :]
                xt = x_t[ti % 2]
                kb.dma(xt[:], src[ti * 128:(ti + 1) * 128, :], r=[src_b], w=[xt])
                kb.act(x_b[:], xt[:], AF.Copy, r=[xt], w=[x_b])
                for kc in range(8):
                    kb.tr(p_tr[:, kc * 128:(kc + 1) * 128], x_b[:, kc * 128:(kc + 1) * 128], ident_b[:], r=[x_b, ident_b], w=[p_tr])
                kb.cp(DVE, xT[:].rearrange("p a b -> p (a b)"), p_tr[:], r=[p_tr], w=[xT])
                for g4 in range(4):
                    pa = next_pa()
                    for gg in range(4):
                        g = g4 * 4 + gg
                        col = g * 128 if g < 8 else 1024 + (g - 8) * 128
                        for kc in range(8):
                            kb.mm(pa[:, gg * 128:(gg + 1) * 128], wbf[:, kc, col:col + 128], xT[:, kc, :], kc == 0, kc == 7,
                                  r=[wbf, xT], w=[pa])
                    if g4 < 2:
                        kb.act(qT[:, g4 * 4:(g4 + 1) * 4, :].rearrange("p a b -> p (a b)"), pa[:], AF.Copy, r=[pa], w=[qT])
                    else:
                        kb.act(kT[:, (g4 - 2) * 4:(g4 - 1) * 4, :].rearrange("p a b -> p (a b)"), pa[:], AF.Copy, r=[pa], w=[kT],
                               scale=1.0 / 16.0)
                for blk in range(6):
                    pa = next_pa()
                    col = 1024 + blk * 512
                    for kc in range(8):
                        kb.mm(pa[:], xT[:, kc, :], wbf[:, kc, col:col + 512], kc == 0, kc == 7, r=[wbf, xT], w=[pa])
                    if blk < 2:
                        kb.act(k_tm[:, blk * 512:(blk + 1) * 512], pa[:], AF.Copy, r=[pa], w=[k_tm], scale=1.0 / 16.0)
                    elif blk < 4:
                        b2 = blk - 2
                        kb.cp(DVE, v_aug[:, 2 * b2:2 * b2 + 2, 0:256], pa[:].rearrange("p (h v) -> p h v", h=2), r=[pa], w=[v_aug])
                    else:
                        b2 = blk - 4
                        kb.act(sig_o[:, b2 * 512:(b2 + 1) * 512], pa[:], AF.Sigmoid, r=[pa], w=[sig_o])
                for gi in range(2):
                    for kc in range(8):
                        kb.mm(p_s[0:4, 128 + gi * 128: 256 + gi * 128], wbf[:, kc, 4096 + gi * 4:4100 + gi * 4], xT[:, kc, :],
                              kc == 0, kc == 7, r=[wbf, xT], w=[pg_b])
                R = lambda i: rows[:, i, :]
                kb.ts(DVE, R(0), p_s[0:4, 128:256], bi_col[:, 0:1], ALU.add, r=[pg_b, bi_col], w=[rows])
                kb.act(R(1), p_s[0:4, 256:384], AF.Exp, r=[pg_b, nbf_col], w=[rows], bias=nbf_col[:, 0:1], scale=-1.0)
                kb.act(R(1), R(1), AF.Ln, r=[rows, one_col], w=[rows], bias=one_col[0:4, 0:1], scale=1.0)
                kb.v(DVE, lambda: nc.vector.tensor_tensor_scan(out=R(2), data0=r_row, data1=R(1), initial=0.0,
                                                                op0=ALU.mult, op1=ALU.add), r=[rows, scn], w=[rows])
                kb.tt(DVE, R(3), R(0), R(2), ALU.add, r=[rows], w=[rows])
                kb.cp(DVE, R(4), R(3), r=[rows], w=[rows])
                if is_p:
                    kb.cp(DVE, seqr[:, 0, 0:1], m_carry[:, 0:1], r=[m_carry], w=[seqr])
                else:
                    s0 = (ti - NT_P) * 16
                    kb.dma(seqr[:, 0, 0:16], stm_in[l, s0:s0 + 16, :].rearrange("j h -> h j"), w=[seqr], slow=True)
                v3 = lambda ap: ap.rearrange("p (j t) -> p j t", t=L)
                kb.tt(DVE, v3(R(4))[:, :, 0], v3(R(3))[:, :, 0], seqr[:, 0, 0:nseq], ALU.max, r=[rows, seqr], w=[rows])
                kb.v(DVE, lambda: nc.vector.tensor_tensor_scan(out=R(5), data0=a_row, data1=R(4), initial=0.0,
                                                                op0=ALU.add, op1=ALU.max), r=[rows, scn], w=[rows])
                kb.ts(DVE, R(6), R(5), -1.0, ALU.mult, r=[rows], w=[rows])
                kb.tt(DVE, R(7), R(2), R(5), ALU.subtract, r=[rows], w=[rows])
                kb.cp(DVE, seqr[:, 1, 0:nseq], v3(R(5))[:, :, L - 1], r=[rows], w=[seqr])
                kb.cp(DVE, v3(R(8)), seqr[:, 0, 0:nseq].unsqueeze(2).to_broadcast([4, nseq, L]), r=[seqr], w=[rows])
                kb.cp(DVE, v3(R(9)), seqr[:, 1, 0:nseq].unsqueeze(2).to_broadcast([4, nseq, L]), r=[seqr], w=[rows])
                kb.tt(DVE, R(10), R(8), R(5), ALU.subtract, r=[rows], w=[rows])
                kb.tt(DVE, R(11), R(3), R(9), ALU.subtract, r=[rows], w=[rows])
                kb.tt(DVE, seqr[:, 2, 0:nseq], seqr[:, 0, 0:nseq], seqr[:, 1, 0:nseq], ALU.subtract, r=[seqr], w=[seqr])
                kb.tt(DVE, seqr[:, 3, 0:nseq], seqr[:, 1, 0:nseq], v3(R(2))[:, :, L - 1], ALU.subtract, r=[seqr, rows], w=[seqr])
                if is_p:
                    kb.cp(DVE, m_carry[:, 0:1], seqr[:, 3, 0:1], r=[seqr], w=[m_carry])
                for i, ri in enumerate((3, 7, 11)):
                    kb.tr(p_e[:, 320 + 4 * i: 324 + 4 * i], R(ri), ident_f[0:4, 0:4], r=[rows, ident_f], w=[pc_b])
                kb.cp(DVE, cols[:, 0, :], p_e[:, 320:324], r=[pc_b], w=[cols])
                kb.act(cols[:, 1:3, :].rearrange("p a b -> p (a b)"), p_e[:, 324:332], AF.Exp, r=[pc_b], w=[cols])
                for h in range(4):
                    kb.mm(p_e[:, 256 + h * 16: 256 + h * 16 + nseq], sel4[:, h, :], seqr[:, 2, 0:nseq], True, True,
                          r=[sel4, seqr], w=[pd_b])
                kb.act(decay_bc[:, :, 0:nseq], p_e[:, 256:320].rearrange("p (h j) -> p h j", h=4)[:, :, 0:nseq], AF.Exp,
                       r=[pd_b], w=[decay_bc])
                for h in range(4):
                    kb.mm(p_e[:, 0:128], sel4[:, h, :], R(6), True, False, r=[sel4, rows], w=[pe_b])
                    kb.mm(p_e[:, 0:128], ident_f[:], mask[:], False, True, r=[ident_f, mask], w=[pe_b])
                    kb.act(E_sb[:], p_e[:, 0:128], AF.Exp, r=[pe_b, cols], w=[E_sb], bias=cols[:, 0, h:h + 1], scale=1.0)
                    for c in range(2):
                        kb.mm(p_s[:, 0:128], kT[:, 2 * h + c, :], qT[:, 2 * h + c, :], c == 0, c == 1, r=[kT, qT], w=[ps_b])
                    kb.tt(DVE, PT[:], p_s[:, 0:128], E_sb[:], ALU.mult, r=[ps_b, E_sb], w=[PT])
                    kb.mm(p_e[:, 128:256], sel4[:, h, :], R(10), True, True, r=[sel4, rows], w=[pw_b])
                    kb.act(wi_bc[:], p_e[:, 128:256], AF.Exp, r=[pw_b], w=[wi_bc])
                    qv = qpT[:].rearrange("p c (j x) -> p c j x", x=136)[:, :, 0:nseq, 0:L] if not is_p else None
                    if is_p:
                        kb.tt(DVE, qpT[:, :, 0:128], qT[:, 2 * h:2 * h + 2, :], wi_bc[:].unsqueeze(1).to_broadcast([128, 2, 128]),
                              ALU.mult, r=[qT, wi_bc], w=[qpT])
                    else:
                        kb.tt(DVE, qv, qT[:, 2 * h:2 * h + 2, :].rearrange("p c (j t) -> p c j t", t=L),
                              wi_bc[:].rearrange("p (j t) -> p j t", t=L).unsqueeze(1).to_broadcast([128, 2, nseq, L]),
                              ALU.mult, r=[qT, wi_bc], w=[qpT])
                    ctfs = []
                    if not is_p:
                        s0 = (ti - NT_P) * 16
                        kb.dma(nio[:], stn_in[l, s0:s0 + 16, h, :], w=[nio])
                        pa = next_pa()
                        for c in range(2):
                            kb.tr(pa[:, c * 16:(c + 1) * 16], nio[:, c * 128:(c + 1) * 128], ident_f[0:16, 0:16],
                                  r=[nio, ident_f], w=[pa])
                        kb.cp(DVE, nstage[:], pa[:, 0:32], r=[pa], w=[nstage])
                    nmm = 1 + 2 * nseq
                    kb.mm(p_n[:, 0:257], PT[:], v_aug[:, h, :], True, False, r=[PT, v_aug], w=[p_n])
                    if is_p:
                        for c in range(2):
                            kb.mm(p_n[:, 0:257], qpT[:, c, 0:128], CTbp[:, h, c, :], False, c == 1, r=[qpT, CTbp], w=[p_n])
                    else:
                        for j in range(nseq):
                            ctf = CTf[j % 2]
                            state_in(h, j, s0 + j, ctf)
                            kb.cp(DVE, ctf[:, :, 256], nstage[:, :].rearrange("p (c j) -> p c j", c=2)[:, :, j], r=[nstage], w=[ctf])
                            kb.cp(POOL, CTbs[:, j % 2, :, :], ctf[:], r=[ctf], w=[CTbs])
                            for c in range(2):
                                kb.mm(p_n[:, 0:257], qpT[:, c, j * 128:(j + 1) * 128], CTbs[:, j % 2, c, :], False,
                                      (j == nseq - 1) and c == 1, r=[qpT, CTbs], w=[p_n])
                            if j == 0:
                                kb.ts(DVE, wm[:, 0:16], seq1h[:], cols[:, 2, h:h + 1], ALU.mult, r=[seq1h, cols], w=[wm])
                                kb.tt(DVE, wv_blk[:], v_aug[:, h, :].unsqueeze(1).to_broadcast([128, 16, 257]),
                                      wm[:].unsqueeze(2).to_broadcast([128, 16, 257]), ALU.mult, r=[v_aug, wm], w=[wv_blk])
                            for c in range(2):
                                pu = p_u[c]
                                kb.mm(pu[:, 0:257], k_tm[:, h * 256 + c * 128: h * 256 + c * 128 + 128], wv_blk[:, j, :], True, True,
                                      r=[k_tm, wv_blk], w=[pu])
                                kb.v(DVE, lambda c=c, pu=pu, ctf=ctf, j=j: nc.vector.scalar_tensor_tensor(
                                    out=ctf[:, c, :], in0=ctf[:, c, :], scalar=decay_bc[:, h, j:j + 1], in1=pu[:, 0:257],
                                    op0=ALU.mult, op1=ALU.add), r=[ctf, decay_bc, pu], w=[ctf])
                            state_out(h, 1 + s0 + j, ctf, j)
                            kb.cp(DVE, nstage2[:].rearrange("p (c j) -> p c j", c=2)[:, :, j], ctf[:, :, 256], r=[ctf], w=[nstage2])
                        pa = next_pa()
                        for c in range(2):
                            kb.tr(pa[0:16, c * 128:(c + 1) * 128], nstage2[:, c * 16:(c + 1) * 16], ident_f[:], r=[nstage2, ident_f], w=[pa])
                        kb.cp(DVE, nio[:], pa[0:16, 0:256], r=[pa], w=[nio])
                        kb.dma(n_out[l, 1 + s0:1 + s0 + 16, h, :], nio[:], r=[nio], w=[out_bufs["n"]])
                    kb.act(dm[:, 0:1], p_n[:, 256:257], AF.Abs, r=[p_n], w=[dm])
                    kb.ts(DVE, dm[:, 0:1], dm[:, 0:1], cols[:, 1, h:h + 1], ALU.max, r=[dm, cols], w=[dm])
                    kb.v(DVE, lambda: nc.vector.reciprocal(out=dm[:, 1:2], in_=dm[:, 0:1]), r=[dm], w=[dm])
                    kb.ts(DVE, hraw[:], p_n[:, 0:256], dm[:, 1:2], ALU.mult, r=[p_n, dm], w=[hraw])
                    kb.v(DVE, lambda: nc.vector.bn_stats(out=bst[:], in_=hraw[:]), r=[hraw], w=[bst])
                    kb.v(DVE, lambda: nc.vector.bn_aggr(out=bmv[:], in_=bst[:]), r=[bst], w=[bmv])
                    kb.act(brs[:], bmv[:, 1:2], AF.Sqrt, r=[bmv, eps_col], w=[brs], bias=eps_col[:, 0:1], scale=1.0)
                    kb.v(DVE, lambda: nc.vector.reciprocal(out=brs[:], in_=brs[:]), r=[brs], w=[brs])
                    kb.ts(DVE, hfin[:, h * 256:(h + 1) * 256], hraw[:], bmv[:, 0:1], ALU.subtract, brs[:, 0:1], ALU.mult,
                          r=[hraw, bmv, brs], w=[hfin])
                    if is_p:
                        kb.ts(DVE, wv_blk[:, 0, :], v_aug[:, h, :], cols[:, 2, h:h + 1], ALU.mult, r=[v_aug, cols], w=[wv_blk])
                        for c in range(2):
                            pu = p_u[c]
                            kb.mm(pu[:, 0:257], k_tm[:, h * 256 + c * 128: h * 256 + c * 128 + 128], wv_blk[:, 0, :], True, True,
                                  r=[k_tm, wv_blk], w=[pu])
                            kb.v(DVE, lambda c=c, pu=pu: nc.vector.scalar_tensor_tensor(
                                out=CTp[:, h, c, :], in0=CTp[:, h, c, :], scalar=decay_bc[:, h, 0:1], in1=pu[:, 0:257],
                                op0=ALU.mult, op1=ALU.add), r=[CTp, decay_bc, pu], w=[CTp])
                        kb.cp(POOL, CTbp[:, h, :, :], CTp[:, h, :, :], r=[CTp], w=[CTbp])
                if not is_p:
                    s0 = (ti - NT_P) * 16
                    kb.dma(m_out[l, 1 + s0:1 + s0 + 16, :].rearrange("j h -> h j"), seqr[:, 3, 0:16], r=[seqr], w=[out_bufs["m"]], slow=True)
                kb.tt(POOL, hfin[:], hfin[:], normg[:], ALU.mult, r=[hfin, normg], w=[hfin])
                kb.tt(DVE, hfin_b[:], hfin[:], sig_o[:], ALU.mult, r=[hfin, sig_o], w=[hfin_b])
                for kc in range(8):
                    kb.tr(p_tr[:, kc * 128:(kc + 1) * 128], hfin_b[:, kc * 128:(kc + 1) * 128], ident_b[:], r=[hfin_b, ident_b], w=[p_tr])
                kb.cp(DVE, hT[:].rearrange("p a b -> p (a b)"), p_tr[:], r=[p_tr], w=[hT])
                for half in range(2):
                    pa = next_pa()
                    for kc in range(8):
                        kb.mm(pa[:], hT[:, kc, :], wout[:, kc, half * 512:(half + 1) * 512], kc == 0, kc == 7, r=[hT, wout], w=[pa])
                    kb.v(DVE, lambda half=half, pa=pa: nc.vector.scalar_tensor_tensor(
                        out=z_t[:, half * 512:(half + 1) * 512], in0=xt[:, half * 512:(half + 1) * 512], scalar=DN_ALPHA,
                        in1=pa[:], op0=ALU.mult, op1=ALU.add), r=[xt, pa], w=[z_t])
                layer_norm_tile(z_t, lng, lnb, xo_t, (lst, lmv, lrs))
                kb.dma(dst[ti * 128:(ti + 1) * 128, :], xo_t[:], r=[xo_t], w=[dst_b])

            nstage = kb.sb("nstage", [128, 32], F32)
            nstage2 = kb.sb("nstage2", [128, 32], F32)
            for ti in range(NT):
                tile_step(ti)
                if ti == NT_P - 1:
                    kb.v(DVE, lambda: nc.vector.memset(qpT[:], 0.0), w=[qpT])
                    for h in range(4):
                        ctf = CTf[h % 2]
                        kb.cp(DVE, ctf[:], CTp[:, h, :, :], r=[CTp], w=[ctf])
                        state_out(h, 0, ctf, h)
                        pa = next_pa()
                        for c in range(2):
                            kb.tr(pa[0:1, c * 128:(c + 1) * 128], CTp[:, h, c, 256:257], ident_f[:], r=[CTp, ident_f], w=[pa])
                        kb.cp(DVE, nio[0:1, :], pa[0:1, 0:256], r=[pa], w=[nio])
                        kb.dma(n_out[l, 0:1, h, :], nio[0:1, :], r=[nio], w=[out_bufs["n"]])
                    kb.dma(m_out[l, 0:1, :].rearrange("j h -> h j"), m_carry[:, 0:1], r=[m_carry], w=[out_bufs["m"]], slow=True)
        kb.stack = root

    def peer_layer(l, src, src_b, dst, dst_b):
        with ExitStack() as ph:
            kb.stack = ph
            wq = kb.sb("wq", [128, 8, 2048], BF16)
            wst = [kb.sb("wst", [128, 1024], F32) for _ in range(2)]
            keysT = kb.sb("keysT", [128, 16, 128], BF16)
            lng = kb.sb("lng", [128, D], F32)
            lnb = kb.sb("lnb", [128, D], F32)
            i = 0
            for kc in range(8):
                for hf in range(2):
                    st = wst[i % 2]
                    kb.dma(st[:], peer_wq[l, kc * 128:(kc + 1) * 128, hf * 1024:(hf + 1) * 1024], w=[st])
                    kb.cp((DVE, POOL)[i % 2], wq[:, kc, hf * 1024:(hf + 1) * 1024], st[:], r=[st], w=[wq])
                    i += 1
            for g in range(2):
                st = wst[i % 2]
                kb.dma(st[:].rearrange("p (a b) -> p a b", a=8), peer_keysT[l, g * 8:(g + 1) * 8].rearrange("a d k -> d a k"), w=[st])
                kb.cp((DVE, POOL)[i % 2], keysT[:, g * 8:(g + 1) * 8, :], st[:].rearrange("p (a b) -> p a b", a=8), r=[st], w=[keysT])
                i += 1
            kb.dma(lng[:], bc_rows(ln_g[l, 1:2, :], D), w=[lng])
            kb.dma(lnb[:], bc_rows(ln_b[l, 1:2, :], D), w=[lnb])
            x_t = [kb.sb("x_t", [128, D], F32) for _ in range(2)]
            x_b = kb.sb("x_b", [128, D], BF16)
            xT = kb.sb("xT", [128, 8, 128], BF16)
            qT16 = kb.sb("qT16", [128, 16, 128], BF16)
            s_sb = kb.sb("s_sb", [128, 16, 128], F32)
            s_wk = kb.sb("s_wk", [128, 256], F32)
            stop = kb.sb("stop", [128, 16, 16], F32)
            itop = kb.sb("itop", [128, 16, 16], U32)
            itopf = kb.sb("itopf", [128, 16, 16], F32)
            cand = kb.sb("cand", [128, 8, 256], F32)
            gval = kb.sb("gval", [128, 8, 16], F32)
            gpos = kb.sb("gpos", [128, 8, 16], U32)
            ai = kb.sb("ai", [128, 128], U32)
            af = kb.sb("af", [128, 2, 128], F32)
            oh = kb.sb("oh", [128, 128, 16], F32)
            i01 = kb.sb("i01", [128, 2, 128], F32)
            eidx = kb.sb("eidx", [128, 128], I32)
            gate = kb.sb("gate", [128, 8, 16], F32)
            gsum = kb.sb("gsum", [128, 8], F32)
            actv = kb.sb("actv", [128, 128], F32)
            g1 = kb.sb("g1", [128, 128], F32)
            g2 = kb.sb("g2", [128, 128], F32)
            hw = kb.sb("hw", [128, 128], F32)
            NB = 4
            ubuf = [kb.sb("ubuf", [128, D], F32) for _ in range(NB)]
            vbuf = [kb.sb("vbuf", [128, D], F32) for _ in range(NB)]
            junk = kb.sb("junk", [128, D], F32)
            y_acc = kb.sb("y_acc", [128, D], F32)
            lst = kb.sb("lst", [128, 2, 6], F32)
            lmv = kb.sb("lmv", [128, 2], F32)
            lrs = kb.sb("lrs", [128, 1], F32)
            p_a = [kb.ps("p_a", [128, 512], F32) for _ in range(2)]
            p_tr = kb.ps("p_tr", [128, 1024], BF16)
            p_s4 = [kb.ps("p_s4", [128, 512], F32) for _ in range(2)]
            pa_i = [0]

            def next_pa():
                pa_i[0] += 1
                return p_a[pa_i[0] % 2]

            def top16(src_ap, vals_out, idx_out):
                n = src_ap.shape[1]
                kb.v(DVE, lambda: nc.vector.max(out=vals_out[:, 0:8], in_=src_ap), r=[s_sb, cand], w=[stop, gval])
                kb.v(DVE, lambda: nc.vector.max_index(out=idx_out[:, 0:8], in_max=vals_out[:, 0:8], in_values=src_ap),
                     r=[s_sb, cand, stop, gval], w=[itop, gpos])
                kb.v(DVE, lambda: nc.vector.match_replace(out=s_wk[:, 0:n], in_to_replace=vals_out[:, 0:8], in_values=src_ap,
                                                          imm_value=-1e30), r=[s_sb, cand, stop, gval], w=[s_wk])
                kb.v(DVE, lambda: nc.vector.max(out=vals_out[:, 8:16], in_=s_wk[:, 0:n]), r=[s_wk], w=[stop, gval])
                kb.v(DVE, lambda: nc.vector.max_index(out=idx_out[:, 8:16], in_max=vals_out[:, 8:16], in_values=s_wk[:, 0:n]),
                     r=[s_wk, stop, gval], w=[itop, gpos])

            for ti in range(NT):
                xt = x_t[ti % 2]
                kb.dma(xt[:], src[ti * 128:(ti + 1) * 128, :], r=[src_b], w=[xt])
                kb.act(x_b[:], xt[:], AF.Copy, r=[xt], w=[x_b])
                for kc in range(8):
                    kb.tr(p_tr[:, kc * 128:(kc + 1) * 128], x_b[:, kc * 128:(kc + 1) * 128], ident_b[:], r=[x_b, ident_b], w=[p_tr])
                kb.cp(DVE, xT[:].rearrange("p a b -> p (a b)"), p_tr[:], r=[p_tr], w=[xT])
                for g4 in range(4):
                    pa = next_pa()
                    for gg in range(4):
                        hp = g4 * 4 + gg
                        for kc in range(8):
                            kb.mm(pa[:, gg * 128:(gg + 1) * 128], wq[:, kc, hp * 128:(hp + 1) * 128], xT[:, kc, :], kc == 0, kc == 7,
                                  r=[wq, xT], w=[pa])
                    kb.act(qT16[:, g4 * 4:(g4 + 1) * 4, :].rearrange("p a b -> p (a b)"), pa[:], AF.Copy, r=[pa], w=[qT16])
                for g4 in range(4):
                    pq = p_s4[g4 % 2]
                    for gg in range(4):
                        hp = g4 * 4 + gg
                        kb.mm(pq[:, gg * 128:(gg + 1) * 128], qT16[:, hp, :], keysT[:, hp, :], True, True, r=[qT16, keysT], w=[pq])
                    kb.act(s_sb[:, g4 * 4:(g4 + 1) * 4, :].rearrange("p a b -> p (a b)"), pq[:], AF.Copy, r=[pq], w=[s_sb])
                for hp in range(16):
                    top16(s_sb[:, hp, :], stop[:, hp, :], itop[:, hp, :])
                kb.cp(DVE, itopf[:], itop[:], r=[itop], w=[itopf])
                for h in range(8):
                    kb.tt(DVE, cand[:, h, :].rearrange("p (a b) -> p a b", a=16),
                          stop[:, 2 * h, :].unsqueeze(2).to_broadcast([128, 16, 16]),
                          stop[:, 2 * h + 1, :].unsqueeze(1).to_broadcast([128, 16, 16]), ALU.add, r=[stop], w=[cand])
                for h in range(8):
                    top16(cand[:, h, :], gval[:, h, :], gpos[:, h, :])
                gp = gpos[:].rearrange("p a b -> p (a b)")
                kb.v(DVE, lambda: nc.vector.tensor_single_scalar(out=ai[:], in_=gp, scalar=4, op=ALU.logical_shift_right), r=[gpos], w=[ai])
                kb.cp(DVE, af[:, 0, :], ai[:], r=[ai], w=[af])
                kb.v(DVE, lambda: nc.vector.tensor_single_scalar(out=ai[:], in_=gp, scalar=15, op=ALU.bitwise_and), r=[gpos], w=[ai])
                kb.cp(DVE, af[:, 1, :], ai[:], r=[ai], w=[af])
                for p in range(2):
                    kb.tt(DVE, oh[:], iota_f[:, 0:16].unsqueeze(1).to_broadcast([128, 128, 16]),
                          af[:, p, :].unsqueeze(2).to_broadcast([128, 128, 16]), ALU.is_equal, r=[iota_f, af], w=[oh])
                    ohv = oh[:].rearrange("p (h k) a -> p h k a", h=8)
                    itv = itopf[:].rearrange("p (h q) a -> p h q a", q=2)[:, :, p, :]
                    for h in range(8):
                        kb.tt(DVE, ohv[:, h], ohv[:, h], itv[:, h].unsqueeze(1).to_broadcast([128, 16, 16]), ALU.mult, r=[oh, itopf], w=[oh])
                    kb.v(DVE, lambda p=p: nc.vector.tensor_reduce(out=i01[:, p, :], in_=oh[:], axis=AX.X, op=ALU.add), r=[oh], w=[i01])
                kb.v(DVE, lambda: nc.vector.scalar_tensor_tensor(out=i01[:, 0, :], in0=i01[:, 0, :], scalar=128.0, in1=i01[:, 1, :],
                                                                  op0=ALU.mult, op1=ALU.add), r=[i01], w=[i01])
                kb.cp(DVE, eidx[:], i01[:, 0, :], r=[i01], w=[eidx])
                kb.tt(DVE, gate[:], gval[:], gval[:, :, 0:1].to_broadcast([128, 8, 16]), ALU.subtract, r=[gval], w=[gate])
                kb.act(gate[:].rearrange("p a b -> p (a b)"), gate[:].rearrange("p a b -> p (a b)"), AF.Exp, r=[gate], w=[gate])
                kb.v(DVE, lambda: nc.vector.tensor_reduce(out=gsum[:], in_=gate[:], axis=AX.X, op=ALU.add), r=[gate], w=[gsum])
                kb.v(DVE, lambda: nc.vector.reciprocal(out=gsum[:], in_=gsum[:]), r=[gsum], w=[gsum])
                kb.tt(DVE, gate[:], gate[:], gsum[:].unsqueeze(2).to_broadcast([128, 8, 16]), ALU.mult, r=[gate, gsum], w=[gate])
                for sl in range(128):
                    ub = ubuf[sl % NB]
                    S.emit(POOL, lambda ub=ub, sl=sl: nc.gpsimd.indirect_dma_start(
                        out=ub[:], out_offset=None, in_=peer_u[l],
                        in_offset=bass.IndirectOffsetOnAxis(ap=eidx[:, sl:sl + 1], axis=0)), r=[eidx.b], w=[ub.b], dma=True)
                    kb.v(DVE, lambda ub=ub, sl=sl: nc.vector.scalar_tensor_tensor(
                        out=junk[:], in0=ub[:], scalar=1.0, in1=xt[:], op0=ALU.mult, op1=ALU.mult,
                        accum_out=actv[:, sl:sl + 1]), r=[ub, xt], w=[junk, actv])
                kb.tt(DVE, g1[:], actv[:], actv[:], ALU.mult, r=[actv], w=[g1])
                kb.ts(DVE, g1[:], g1[:], 0.044715, ALU.mult, 1.0, ALU.add, r=[g1], w=[g1])
                kb.tt(DVE, g1[:], g1[:], actv[:], ALU.mult, r=[g1, actv], w=[g1])
                kb.act(g2[:], g1[:], AF.Tanh, r=[g1], w=[g2], scale=0.7978845608028654)
                kb.ts(DVE, g2[:], g2[:], 1.0, ALU.add, 0.5, ALU.mult, r=[g2], w=[g2])
                kb.tt(DVE, g2[:], g2[:], actv[:], ALU.mult, r=[g2, actv], w=[g2])
                kb.tt(DVE, hw[:], g2[:], gate[:].rearrange("p a b -> p (a b)"), ALU.mult, r=[g2, gate], w=[hw])
                for sl in range(128):
                    vb = vbuf[sl % NB]
                    S.emit(POOL, lambda vb=vb, sl=sl: nc.gpsimd.indirect_dma_start(
                        out=vb[:], out_offset=None, in_=peer_v[l],
                        in_offset=bass.IndirectOffsetOnAxis(ap=eidx[:, sl:sl + 1], axis=0)), r=[eidx.b], w=[vb.b], dma=True)
                    if sl == 0:
                        kb.ts(DVE, y_acc[:], vb[:], hw[:, 0:1], ALU.mult, r=[vb, hw], w=[y_acc])
                    else:
                        kb.v(DVE, lambda vb=vb, sl=sl: nc.vector.scalar_tensor_tensor(
                            out=y_acc[:], in0=vb[:], scalar=hw[:, sl:sl + 1], in1=y_acc[:], op0=ALU.mult, op1=ALU.add),
                            r=[vb, hw, y_acc], w=[y_acc])
                kb.v(DVE, lambda: nc.vector.scalar_tensor_tensor(out=y_acc[:], in0=xt[:], scalar=DN_ALPHA, in1=y_acc[:],
                                                                  op0=ALU.mult, op1=ALU.add), r=[xt, y_acc], w=[y_acc])
                layer_norm_tile(y_acc, lng, lnb, y_acc, (lst, lmv, lrs))
                kb.dma(dst[ti * 128:(ti + 1) * 128, :], y_acc[:], r=[y_acc], w=[dst_b])
        kb.stack = root

    cosT = kb.sb("cosT", [128, NT, 32], F32)
    sinT = kb.sb("sinT", [128, NT, 32], F32)
    invf = kb.sb("invf", [128, 32], F32)
    posc = kb.sb("posc", [128, 2], F32)
    ang = kb.sb("ang", [128, 2, 32], F32)
    angq = kb.sb("angq", [128, 64], F32)
    angi = kb.sb("angi", [128, 64], I32)
    kmP = kb.sb("kmP", [128, 8, 16], BF16)
    kmS = kb.sb("kmS", [128, NSEQ_S, 8, 8], BF16)
    idx_all = kb.sb("idx_all", [128, NSEQ_S * 16], I32)
    idx_f = kb.sb("idx_f", [128, NSEQ_S * 16], F32)
    inv256 = kb.sb("inv256", [128, 1], F32)
    ones_b = kb.sb("ones_b", [128, 128], BF16)
    maskP_b = kb.sb("maskP_b", [128, 128], BF16)
    eye8 = kb.sb("eye8", [8, 8], F32)
    hm01 = kb.sb("hm01", [128, 2], F32)
    KT_d = dint("KT_d", [128, 8, NTOK], BF16)
    V_d = dint("V_d", [NTOK, D], BF16)
    attn_d = dint("attn_d", [NT_S * 128, D], F32)
    attn_dp = dint("attn_dp", [NT_P * 128, D], F32)
    attn_dpb = Buf("attn_dp")
    KT_db, V_db, attn_db = Buf("KT_d"), Buf("V_d"), Buf("attn_d")
    PI = 3.14159265358979

    def setup_moba_consts():
        kb.v(DVE, lambda: nc.vector.memset(inv256[:], 1.0 / 256.0), w=[inv256])
        kb.v(DVE, lambda: nc.vector.memset(ones_b[:], 1.0), w=[ones_b])
        kb.cp(DVE, maskP_b[:], maskP[:], r=[maskP], w=[maskP_b])
        kb.cp(DVE, eye8[:], ident_f[0:8, 0:8], r=[ident_f], w=[eye8])
        kb.ts(DVE, hm01[:, 1:2], iota_p[:], 64.0, ALU.is_ge, r=[iota_p], w=[hm01])
        kb.ts(DVE, hm01[:, 0:1], hm01[:, 1:2], -1.0, ALU.mult, 1.0, ALU.add, r=[hm01], w=[hm01])
        kb.act(invf[:], iota_f[:, 0:32], AF.Exp, r=[iota_f], w=[invf], scale=-float(np.log(10000.0)) / 32.0)
        kb.v(DVE, lambda: nc.vector.scalar_tensor_tensor(out=posc[:, 1:2], in0=pdiv[:], scalar=-8.0, in1=iota_p[:],
                                                          op0=ALU.mult, op1=ALU.add), r=[pdiv, iota_p], w=[posc])
        kb.ts(DVE, posc[:, 1:2], posc[:, 1:2], 2048.0, ALU.add, r=[posc], w=[posc])
        for ti in range(NT):
            if ti < NT_P:
                kb.ts(DVE, posc[:, 0:1], iota_p[:], float(ti * 128), ALU.add, r=[iota_p], w=[posc])
                pc = posc[:, 0:1]
            else:
                pc = posc[:, 1:2]
            kb.ts(DVE, ang[:, 0, :], invf[:], pc, ALU.mult, r=[invf, posc], w=[ang])
            kb.ts(DVE, ang[:, 1, :], ang[:, 0, :], 0.5 * PI, ALU.add, r=[ang], w=[ang])
            A2 = ang[:].rearrange("p a b -> p (a b)")
            kb.ts(DVE, angq[:], A2, 1.0 / (2.0 * PI), ALU.mult, r=[ang], w=[angq])
            kb.cp(DVE, angi[:], angq[:], r=[angq], w=[angi])
            kb.cp(DVE, angq[:], angi[:], r=[angi], w=[angq])
            kb.v(DVE, lambda: nc.vector.scalar_tensor_tensor(out=A2, in0=angq[:], scalar=-2.0 * PI, in1=A2, op0=ALU.mult, op1=ALU.add),
                 r=[angq, ang], w=[ang])
            kb.ts(DVE, angq[:], A2, PI, ALU.is_ge, r=[ang], w=[angq])
            kb.v(DVE, lambda: nc.vector.scalar_tensor_tensor(out=A2, in0=angq[:], scalar=-2.0 * PI, in1=A2, op0=ALU.mult, op1=ALU.add),
                 r=[angq, ang], w=[ang])
            kb.ts(DVE, angq[:], A2, -1.0, ALU.mult, PI, ALU.is_ge, r=[ang], w=[angq])
            kb.v(DVE, lambda: nc.vector.scalar_tensor_tensor(out=A2, in0=angq[:], scalar=2.0 * PI, in1=A2, op0=ALU.mult, op1=ALU.add),
                 r=[angq, ang], w=[ang])
            kb.act(sinT[:, ti, :], ang[:, 0, :], AF.Sin, r=[ang], w=[sinT], scale=0.999999)
            kb.act(cosT[:, ti, :], ang[:, 1, :], AF.Sin, r=[ang], w=[cosT], scale=0.999999)
        kb.dma(idx_all[:], page_table.rearrange("s p -> (s p)").rearrange("(o n) -> o n", o=1).to_broadcast([128, NSEQ_S * 16]), w=[idx_all])
        kb.cp(DVE, idx_f[:], idx_all[:], r=[idx_all], w=[idx_f])
        kb.ts(DVE, idx_f[:], idx_f[:], 128.0, ALU.mult, iota_p[:, 0:1], ALU.add, r=[idx_f, iota_p], w=[idx_f])
        kb.cp(DVE, idx_all[:], idx_f[:], r=[idx_f], w=[idx_all])

    def rope_tile(ti, srcf, dstf, tmp):
        sv = srcf[:].rearrange("p (h e d) -> p h e d", h=16, e=2)
        dv = dstf[:].rearrange("p (h e d) -> p h e d", h=16, e=2)
        tv = tmp[:].rearrange("p a (h d) -> p a h d", h=16)
        cb = cosT[:, ti, :].unsqueeze(1).to_broadcast([128, 16, 32])
        sb_ = sinT[:, ti, :].unsqueeze(1).to_broadcast([128, 16, 32])
        kb.tt(DVE, tv[:, 0], sv[:, :, 0, :], cb, ALU.mult, r=[srcf, cosT], w=[tmp])
        kb.tt(DVE, tv[:, 1], sv[:, :, 1, :], sb_, ALU.mult, r=[srcf, sinT], w=[tmp])
        kb.tt(DVE, dv[:, :, 0, :], tv[:, 0], tv[:, 1], ALU.subtract, r=[tmp], w=[dstf])
        kb.tt(POOL, tv[:, 0], sv[:, :, 1, :], cb, ALU.mult, r=[srcf, cosT, dstf], w=[tmp])
        kb.tt(POOL, tv[:, 1], sv[:, :, 0, :], sb_, ALU.mult, r=[srcf, sinT], w=[tmp])
        kb.tt(POOL, dv[:, :, 1, :], tv[:, 0], tv[:, 1], ALU.add, r=[tmp], w=[dstf])

    def load_w_bf16(dst_t, w_ap, ncols, wst, i0=0):
        i = i0
        for kc in range(8):
            for c0 in range(0, ncols, 1024):
                st = wst[i % 2]
                kb.dma(st[:], w_ap[kc * 128:(kc + 1) * 128, c0:c0 + 1024], w=[st])
                kb.cp((DVE, POOL)[i % 2], dst_t[:, kc, c0:c0 + 1024], st[:], r=[st], w=[dst_t])
                i += 1
        return i

    def x_to_xT(xt, x_b, xT, p_tr):
        kb.act(x_b[:], xt[:], AF.Copy, r=[xt], w=[x_b])
        for kc in range(8):
            kb.tr(p_tr[:, kc * 128:(kc + 1) * 128], x_b[:, kc * 128:(kc + 1) * 128], ident_b[:], r=[x_b, ident_b], w=[p_tr])
        kb.cp(DVE, xT[:].rearrange("p a b -> p (a b)"), p_tr[:], r=[p_tr], w=[xT])

    def kv_phase(src, src_b):
        with ExitStack() as ph:
            kb.stack = ph
            wkv = kb.sb("wkv", [128, 8, 2048], BF16)
            wst = [kb.sb("wst", [128, 1024], F32) for _ in range(2)]
            load_w_bf16(wkv, w_kv, 2048, wst)
            x_t = [kb.sb("x_t", [128, D], F32) for _ in range(2)]
            x_b = kb.sb("x_b", [128, D], BF16)
            xT = kb.sb("xT", [128, 8, 128], BF16)
            kf = kb.sb("kf", [128, D], F32)
            kr = kb.sb("kr", [128, D], F32)
            krb = kb.sb("krb", [128, D], BF16)
            vf = kb.sb("vf", [128, D], F32)
            vb = kb.sb("vb", [128, D], BF16)
            ktst = kb.sb("ktst", [128, 8, 128], BF16)
            rtmp = kb.sb("rtmp", [128, 2, 512], F32)
            kpg = [kb.sb("kpg", [128, D], F32) for _ in range(3)]
            p_a = [kb.ps("p_a", [128, 512], F32) for _ in range(2)]
            p_tr = kb.ps("p_tr", [128, 1024], BF16)
            p_km = kb.ps("p_km", [128, 512], F32)
            kb.v(DVE, lambda: nc.vector.memset(p_km[:], 0.0), w=[p_km])
            for ti in range(NT):
                xt = x_t[ti % 2]
                kb.dma(xt[:], src[ti * 128:(ti + 1) * 128, :], r=[src_b], w=[xt])
                x_to_xT(xt, x_b, xT, p_tr)
                for blk in range(4):
                    pa = p_a[blk % 2]
                    for kc in range(8):
                        kb.mm(pa[:], xT[:, kc, :], wkv[:, kc, blk * 512:(blk + 1) * 512], kc == 0, kc == 7, r=[wkv, xT], w=[pa])
                    if blk < 2:
                        kb.act(kf[:, blk * 512:(blk + 1) * 512], pa[:], AF.Copy, r=[pa], w=[kf])
                    else:
                        kb.act(vf[:, (blk - 2) * 512:(blk - 1) * 512], pa[:], AF.Copy, r=[pa], w=[vf])
                rope_tile(ti, kf, kr, rtmp)
                kb.dma(k_rows[ti * 128:(ti + 1) * 128, :], kr[:], r=[kr], w=[out_bufs["k"]])
                kb.dma(v_rows[ti * 128:(ti + 1) * 128, :], vf[:], r=[vf], w=[out_bufs["v"]])
                kb.cp(POOL, vb[:], vf[:], r=[vf], w=[vb])
                kb.dma(V_d[ti * 128:(ti + 1) * 128, :], vb[:], r=[vb], w=[V_db])
                kb.act(krb[:], kr[:], AF.Copy, r=[kr], w=[krb])
                for c in range(8):
                    kb.tr(p_tr[:, c * 128:(c + 1) * 128], krb[:, c * 128:(c + 1) * 128], ident_b[:], r=[krb, ident_b], w=[p_tr])
                kb.cp(DVE, ktst[:].rearrange("p a b -> p (a b)"), p_tr[:], r=[p_tr], w=[ktst])
                kb.dma(KT_d[:, :, ti * 128:(ti + 1) * 128], ktst[:], r=[ktst], w=[KT_db])
                if ti < NT_P:
                    kbk = ti // 2
                    for c in range(8):
                        S.emit(PE, lambda c=c, kbk=kbk: nc.tensor.matmul(p_km[:, c * 16 + kbk:c * 16 + kbk + 1], kr[:, c * 128:(c + 1) * 128],
                                                                         inv256[:], start=False, stop=True, skip_group_check=True),
                               r=[kr.b, inv256.b], w=[p_km.b])
            kb.cp(DVE, kmP[:].rearrange("p a b -> p (a b)"), p_km[:, 0:128], r=[p_km], w=[kmP])
            for sq in range(NSEQ_S):
                kb.v(DVE, lambda: nc.vector.memset(p_km[:, 0:64], 0.0), w=[p_km])
                for pg in range(16):
                    kp = kpg[pg % 3]
                    col = sq * 16 + pg
                    S.emit(POOL, lambda kp=kp, col=col: nc.gpsimd.indirect_dma_start(
                        out=kp[:], out_offset=None, in_=cache_k,
                        in_offset=bass.IndirectOffsetOnAxis(ap=idx_all[:, col:col + 1], axis=0)), r=[idx_all.b], w=[kp.b], dma=True)
                    for c in range(8):
                        S.emit(PE, lambda c=c, pg=pg, kp=kp: nc.tensor.matmul(
                            p_km[:, c * 8 + pg // 2:c * 8 + pg // 2 + 1], kp[:, c * 128:(c + 1) * 128], inv256[:],
                            start=False, stop=True, skip_group_check=True), r=[kp.b, inv256.b], w=[p_km.b])
                kb.cp(DVE, kmS[:, sq, :, :].rearrange("p a b -> p (a b)"), p_km[:, 0:64], r=[p_km], w=[kmS])
        kb.stack = root

    def moba_layer(jl, l, src, src_b, dst, dst_b):
        with ExitStack() as ph:
            kb.stack = ph
            wqb = kb.sb("wqb", [128, 8, D], BF16)
            with ExitStack() as phw:
                kb.stack = phw
                wst = [kb.sb("wst", [128, 1024], F32) for _ in range(2)]
                load_w_bf16(wqb, w_q_b[jl], 1024, wst)
            kb.stack = ph
            wob_box = []
            lng = kb.sb("lng", [128, D], F32)
            lnb = kb.sb("lnb", [128, D], F32)
            kb.dma(lng[:], bc_rows(ln_g[l, 0:1, :], D), w=[lng])
            kb.dma(lnb[:], bc_rows(ln_b[l, 0:1, :], D), w=[lnb])
            x_t = [kb.sb("x_t", [128, D], F32)] * 2
            x_b = kb.sb("x_b", [128, D], BF16)
            xT = kb.sb("xT", [128, 8, 128], BF16)
            qTe = [kb.sb("qTe", [128, 8, 128], BF16) for _ in range(2)]
            rtmp = kb.sb("rtmp", [128, 2, 512], F32)
            attn = kb.sb("attn", [128, D], F32)
            z_t = kb.sb("z_t", [128, D], F32)
            qf = z_t
            qr = attn
            lst = kb.sb("lst", [128, 2, 6], F32)
            lmv = kb.sb("lmv", [128, 2], F32)
            lrs = kb.sb("lrs", [128, 1], F32)
            g0 = kb.sb("g0", [128, 16, 16], F32)
            gw = kb.sb("gw", [128, 16, 16], F32)
            ge = kb.sb("ge", [128, 16, 16], F32)
            gm = kb.sb("gm", [128, 16], F32)
            selb = kb.sb("selb", [128, 16, 16], F32)
            pastm = kb.sb("pastm", [128, 16], F32)
            dg = [kb.sb("dg", [128, 128], BF16) for _ in range(2)]
            PTm = [kb.sb("PTm", [128, 128], BF16) for _ in range(2)]
            rden = kb.sb("rden", [128, 16], F32)
            p_a = [kb.ps("p_a", [128, 512], F32) for _ in range(2)]
            p_tr = kb.ps("p_tr", [128, 1024], BF16)
            p_g = kb.ps("p_g", [128, 512], F32)
            p_sc = [kb.ps("p_sc", [128, 512], F32) for _ in range(2)]
            p_o = [kb.ps("p_o", [128, 512], F32) for _ in range(2)]

            def q_proj(ti, xt):
                x_to_xT(xt, x_b, xT, p_tr)
                for half in range(2):
                    pa = p_a[half]
                    for kc in range(8):
                        kb.mm(pa[:], xT[:, kc, :], wqb[:, kc, half * 512:(half + 1) * 512], kc == 0, kc == 7, r=[wqb, xT], w=[pa])
                    kb.act(qf[:, half * 512:(half + 1) * 512], pa[:], AF.Copy, r=[pa], w=[qf])
                rope_tile(ti, qf, qr, rtmp)
                kb.act(x_b[:], qr[:], AF.Copy, r=[qr], w=[x_b])
                for c in range(8):
                    kb.tr(p_tr[:, c * 128:(c + 1) * 128], x_b[:, c * 128:(c + 1) * 128], ident_b[:], r=[x_b, ident_b], w=[p_tr])
                for e in range(2):
                    kb.ts(DVE, qTe[e][:].rearrange("p a b -> p (a b)"), p_tr[:], hm01[:, e:e + 1], ALU.mult, r=[p_tr, hm01], w=[qTe[e]])

            def select_blocks(nq, nkb, own, km_of_head):
                for h in range(16):
                    e, hp = h % 2, h // 2
                    kb.mm(p_g[0:nq, h * 16:h * 16 + nkb], qTe[e][:, hp, 0:nq] if nq == 128 else qcols(e, hp),
                          km_of_head(e, hp), True, True, r=[qTe[0], qTe[1], kmP, kmS], w=[p_g])
                kb.v(DVE, lambda: nc.vector.memset(g0[0:nq], -1e9), w=[g0])
                npast = min(own, nkb)
                if npast > 0:
                    kb.cp(DVE, g0[0:nq, :, 0:npast], p_g[0:nq, 0:256].rearrange("p (h k) -> p h k", h=16)[:, :, 0:npast], r=[p_g], w=[g0])
                kb.cp(DVE, gw[0:nq], g0[0:nq], r=[g0], w=[gw])
                for rnd in range(3):
                    kb.v(DVE, lambda: nc.vector.tensor_reduce(out=gm[0:nq], in_=gw[0:nq], axis=AX.X, op=ALU.max), r=[gw], w=[gm])
                    if rnd < 2:
                        kb.tt(DVE, ge[0:nq], gw[0:nq], gm[0:nq].unsqueeze(2).to_broadcast([nq, 16, 16]), ALU.is_equal, r=[gw, gm], w=[ge])
                        kb.v(DVE, lambda: nc.vector.scalar_tensor_tensor(out=gw[0:nq], in0=ge[0:nq], scalar=-1e9, in1=gw[0:nq],
                                                                          op0=ALU.mult, op1=ALU.add), r=[ge, gw], w=[gw])
                kb.tt(DVE, ge[0:nq], g0[0:nq], gm[0:nq].unsqueeze(2).to_broadcast([nq, 16, 16]), ALU.is_ge, r=[g0, gm], w=[ge])
                kb.ts(DVE, selb[0:nq], ge[0:nq], -1.0, ALU.add, -NEG, ALU.mult, r=[ge], w=[selb])
                kb.ts(DVE, pastm[0:nq], iota_f[0:nq, 0:16], float(npast), ALU.is_ge, r=[iota_f], w=[pastm])
                kb.ts(DVE, pastm[0:nq], pastm[0:nq], -1.0, ALU.mult, 1.0, ALU.add, r=[pastm], w=[pastm])
                kb.tt(DVE, selb[0:nq], selb[0:nq], pastm[0:nq].unsqueeze(1).to_broadcast([nq, 16, 16]), ALU.mult, r=[selb, pastm], w=[selb])

            qcols_state = {}

            def qcols(e, hp):
                j = qcols_state["j"]
                return qTe[e][:, hp, 8 * j:8 * j + 8]

            def out_proj(ti, xt, attn_src):
                kb.act(x_b[:], attn_src[:], AF.Copy, r=[attn_src], w=[x_b])
                for c in range(8):
                    kb.tr(p_tr[:, c * 128:(c + 1) * 128], x_b[:, c * 128:(c + 1) * 128], ident_b[:], r=[x_b, ident_b], w=[p_tr])
                kb.cp(DVE, xT[:].rearrange("p a b -> p (a b)"), p_tr[:], r=[p_tr], w=[xT])
                for half in range(2):
                    pa = p_a[half]
                    for kc in range(8):
                        kb.mm(pa[:], xT[:, kc, :], wob_box[0][:, kc, half * 512:(half + 1) * 512], kc == 0, kc == 7, r=[xT, wob_box[0]], w=[pa])
                    kb.v(DVE, lambda half=half, pa=pa: nc.vector.scalar_tensor_tensor(
                        out=z_t[:, half * 512:(half + 1) * 512], in0=xt[:, half * 512:(half + 1) * 512], scalar=DN_ALPHA,
                        in1=pa[:], op0=ALU.mult, op1=ALU.add), r=[xt, pa], w=[z_t])
                layer_norm_tile(z_t, lng, lnb, z_t, (lst, lmv, lrs))
                kb.dma(dst[ti * 128:(ti + 1) * 128, :], z_t[:], r=[z_t], w=[dst_b])

            with ExitStack() as ph2:
                kb.stack = ph2
                KT = kb.sb("KT", [128, 8, NT_P * 128], BF16)
                Vaug = kb.sb("Vaug", [128, NT_P, 16, 64], BF16)
                for c in range(8):
                    kb.dma(KT[:, c, :], KT_d[:, c, 0:NT_P * 128], r=[KT_db], w=[KT])
                for t in range(NT_P):
                    kb.dma(Vaug[:, t, :, 0:64], V_d[t * 128:(t + 1) * 128, :].rearrange("p (h d) -> p h d", h=16), r=[V_db], w=[Vaug])
                for qt in range(DBG["np_tiles"]):
                    xt = x_t[qt % 2]
                    kb.dma(xt[:], src[qt * 128:(qt + 1) * 128, :], r=[src_b], w=[xt])
                    q_proj(qt, xt)
                    own = qt // 2
                    if DBG["select"]:
                        select_blocks(128, 16, own, lambda e, hp: kmP[:, hp, :])
                    vi = 0
                    for h in range(DBG["heads"]):
                        e, hp = h % 2, h // 2
                        if e == 1 and not DBG["e1"]:
                            continue
                        po = p_o[h % 2]
                        for kc in range(qt + 1):
                            kbk = kc // 2
                            psc = p_sc[vi % 2]
                            ptm = PTm[vi % 2]
                            need_sel = kbk < own
                            need_caus = kc == qt
                            kb.mm(psc[:, 0:128], KT[:, hp, kc * 128:(kc + 1) * 128], qTe[e][:, hp, :],
                                  True, not (need_sel or need_caus), r=[KT, qTe[e]], w=[psc])
                            if need_sel:
                                if kc % 2 == 0:
                                    d_ = dg[(kc // 2) % 2]
                                    kb.ts(DVE, d_[:], ident_f[:], selb[:, h, kbk:kbk + 1], ALU.mult, r=[ident_f, selb], w=[d_])
                                d_ = dg[(kc // 2) % 2]
                                kb.mm(psc[:, 0:128], ones_b[:], d_[:], False, True, r=[ones_b, d_], w=[psc])
                            if need_caus:
                                kb.mm(psc[:, 0:128], ident_b[:], maskP_b[:], False, True, r=[ident_b, maskP_b], w=[psc])
                            kb.act(ptm[:], psc[:, 0:128], AF.Exp, r=[psc], w=[ptm], scale=0.125)
                            if kc == 0:
                                kb.v(DVE, lambda po=po: nc.vector.memset(po[:, 0:65], 0.0), w=[po])
                            S.emit(PE, lambda po=po, ptm=ptm, kc=kc, h=h: nc.tensor.matmul(
                                po[:, 0:64], ptm[:], Vaug[:, kc, h, :], start=False, stop=True, skip_group_check=True),
                                r=[ptm.b, Vaug.b], w=[po.b])
                            S.emit(PE, lambda po=po, ptm=ptm: nc.tensor.matmul(
                                po[:, 64:65], ptm[:], ones_b[:, 0:1], start=False, stop=True, skip_group_check=True),
                                r=[ptm.b, ones_b.b], w=[po.b])
                            vi += 1
                        kb.v(DVE, lambda po=po, h=h: nc.vector.reciprocal(out=rden[:, h:h + 1], in_=po[:, 64:65]), r=[po], w=[rden])
                        kb.ts(DVE, attn[:, h * 64:(h + 1) * 64], po[:, 0:64], rden[:, h:h + 1], ALU.mult, r=[po, rden], w=[attn])
                    kb.dma(attn_dp[qt * 128:(qt + 1) * 128, :], attn[:], r=[attn], w=[attn_dpb])
            with ExitStack() as ph2:
                kb.stack = ph2
                wob = kb.sb("wob", [128, 8, D], BF16)
                wob_box.append(wob)
                with ExitStack() as phw:
                    kb.stack = phw
                    wst = [kb.sb("wst", [128, 1024], F32) for _ in range(2)]
                    load_w_bf16(wob, w_out_b[jl], 1024, wst)
                kb.stack = ph2
                for qt in range(NT_P):
                    xt = x_t[0]
                    kb.dma(xt[:], src[qt * 128:(qt + 1) * 128, :], r=[src_b], w=[xt])
                    kb.dma(attn[:], attn_dp[qt * 128:(qt + 1) * 128, :], r=[attn_dpb], w=[attn])
                    out_proj(qt, xt, attn)
                KTn = kb.sb("KTn", [128, 8, 128], BF16)
                Vn = kb.sb("Vn", [8, 16, 65], BF16)
                kpg = [kb.sb("kpg", [128, D], F32) for _ in range(2)]
                vpg = [kb.sb("vpg", [128, D], F32) for _ in range(2)]
                vpb = kb.sb("vpb", [128, 16, 65], BF16)
                KTs = kb.sb("KTs", [128, 8, 128], BF16)
                rsel = kb.sb("rsel", [8, 9, 16, 8], BF16)
                o_s = kb.sb("o_s", [8, D], F32)
                dn_s = kb.sb("dn_s", [8, 16], F32)
                p_t4 = [p_a[0]]
                kb.v(DVE, lambda: nc.vector.memset(vpb[:, :, 64:65], 1.0), w=[vpb])
                kb.v(DVE, lambda: nc.vector.memset(Vn[:, :, 64:65], 1.0), w=[Vn])
                kb.v(DVE, lambda: nc.vector.memset(rsel[:], 0.0), w=[rsel])
                p_oa, p_ob, p_od = p_o[0], p_o[1], p_g
                for ts_ in range(DBG["ns_tiles"]):
                    ti = NT_P + ts_
                    xt = x_t[ti % 2]
                    kb.dma(xt[:], src[ti * 128:(ti + 1) * 128, :], r=[src_b], w=[xt])
                    q_proj(ti, xt)
                    kb.dma(KTn[:], KT_d[:, :, ti * 128:(ti + 1) * 128], r=[KT_db], w=[KTn])
                    for j in range(DBG["ns_seq"]):
                        sq = ts_ * 16 + j
                        qcols_state["j"] = j
                        kb.dma(Vn[:, :, 0:64], V_d[ti * 128 + 8 * j: ti * 128 + 8 * j + 8, :].rearrange("p (h d) -> p h d", h=16),
                               r=[V_db], w=[Vn])
                        select_blocks(8, 8, 8, lambda e, hp: kmS[:, sq, hp, :])
                        for kbk in range(8):
                            kb.tt(DVE, rsel[:, kbk, :, :], selb[0:8, :, kbk:kbk + 1].to_broadcast([8, 16, 8]),
                                  eye8[:].unsqueeze(1).to_broadcast([8, 16, 8]), ALU.mult, r=[selb, eye8], w=[rsel])
                        kb.v(DVE, lambda: nc.vector.memset(p_oa[0:8, :], 0.0), w=[p_oa])
                        kb.v(DVE, lambda: nc.vector.memset(p_ob[0:8, :], 0.0), w=[p_ob])
                        kb.v(DVE, lambda: nc.vector.memset(p_od[0:8, 256:272], 0.0), w=[p_od])
                        for pg in range(17):
                            psc = p_sc[pg % 2]
                            ptm = PTm[pg % 2]
                            if pg < 16:
                                nk = 128
                                kp, vp = kpg[pg % 2], vpg[pg % 2]
                                col = sq * 16 + pg
                                S.emit(POOL, lambda kp=kp, col=col: nc.gpsimd.indirect_dma_start(
                                    out=kp[:], out_offset=None, in_=cache_k,
                                    in_offset=bass.IndirectOffsetOnAxis(ap=idx_all[:, col:col + 1], axis=0)), r=[idx_all.b], w=[kp.b], dma=True)
                                S.emit(POOL, lambda vp=vp, col=col: nc.gpsimd.indirect_dma_start(
                                    out=vp[:], out_offset=None, in_=cache_v,
                                    in_offset=bass.IndirectOffsetOnAxis(ap=idx_all[:, col:col + 1], axis=0)), r=[idx_all.b], w=[vp.b], dma=True)
                                for half in range(2):
                                    pt = p_t4[0]
                                    for c4 in range(4):
                                        c = half * 4 + c4
                                        kb.tr(pt[:, c4 * 128:(c4 + 1) * 128], kp[:, c * 128:(c + 1) * 128], ident_f[:], r=[kp, ident_f], w=[pt])
                                    kb.act(KTs[:, half * 4:(half + 1) * 4, :].rearrange("p a b -> p (a b)"), pt[:], AF.Copy, r=[pt], w=[KTs])
                                kb.cp(POOL, vpb[:, :, 0:64], vp[:].rearrange("p (h d) -> p h d", h=16), r=[vp], w=[vpb])
                                ksrc = lambda e, hp: KTs[:, hp, :]
                                vsrc = lambda h: vpb[:, h, 0:64]
                                onesrc = vpb[:, 0, 64:65]
                                ksb, vsb = KTs, vpb
                            else:
                                nk = 8
                                ksrc = lambda e, hp: KTn[:, hp, 8 * j:8 * j + 8]
                                vsrc = lambda h: Vn[:, h, 0:64]
                                onesrc = Vn[:, 0, 64:65]
                                ksb, vsb = KTn, Vn
                            kb.v(DVE, lambda psc=psc: nc.vector.memset(psc[:, 0:128], 0.0), w=[psc])
                            for h in range(16):
                                e, hp = h % 2, h // 2
                                S.emit(PE, lambda h=h, e=e, hp=hp, ksrc=ksrc, psc=psc, nk=nk: nc.tensor.matmul(
                                    psc[0:nk, h * 8:(h + 1) * 8], ksrc(e, hp), qcols(e, hp), start=False, stop=True,
                                    skip_group_check=True), r=[ksb.b, qTe[0].b, qTe[1].b], w=[psc.b])
                            if pg < 16:
                                S.emit(PE, lambda psc=psc, pg=pg: nc.tensor.matmul(
                                    psc[:, 0:128], ones_b[0:8, :], rsel[:, pg // 2, :, :].rearrange("p a b -> p (a b)"), start=False, stop=True,
                                    skip_group_check=True), r=[ones_b.b, rsel.b], w=[psc.b])
                            else:
                                S.emit(PE, lambda psc=psc: nc.tensor.matmul(
                                    psc[0:8, 0:128], ident_b[0:8, 0:8], caus8[:].rearrange("p a b -> p (a b)"), start=False, stop=True,
                                    skip_group_check=True), r=[ident_b.b, caus8.b], w=[psc.b])
                            kb.act(ptm[0:nk, :], psc[0:nk, 0:128], AF.Exp, r=[psc], w=[ptm], scale=0.125)
                            for h in range(16):
                                po = p_oa if h < 8 else p_ob
                                S.emit(PE, lambda h=h, po=po, ptm=ptm, vsrc=vsrc, nk=nk: nc.tensor.matmul(
                                    po[0:8, (h % 8) * 64:(h % 8 + 1) * 64], ptm[0:nk, h * 8:(h + 1) * 8], vsrc(h), start=False, stop=True,
                                    skip_group_check=True), r=[ptm.b, vsb.b], w=[po.b])
                                S.emit(PE, lambda h=h, ptm=ptm, onesrc=onesrc, nk=nk: nc.tensor.matmul(
                                    p_od[0:8, 256 + h:257 + h], ptm[0:nk, h * 8:(h + 1) * 8], onesrc, start=False, stop=True,
                                    skip_group_check=True), r=[ptm.b, vsb.b], w=[p_od.b])
                        kb.v(DVE, lambda: nc.vector.reciprocal(out=dn_s[:], in_=p_od[0:8, 256:272]), r=[p_od], w=[dn_s])
                        for hh in range(2):
                            po = p_oa if hh == 0 else p_ob
                            kb.tt(DVE, o_s[:, hh * 512:(hh + 1) * 512].rearrange("p (h d) -> p h d", h=8),
                                  po[0:8, :].rearrange("p (h d) -> p h d", h=8),
                                  dn_s[:, hh * 8:(hh + 1) * 8].unsqueeze(2).to_broadcast([8, 8, 64]), ALU.mult, r=[po, dn_s], w=[o_s])
                        kb.dma(attn_d[ts_ * 128 + 8 * j: ts_ * 128 + 8 * j + 8, :], o_s[:], r=[o_s], w=[attn_db])
                    kb.dma(attn[:], attn_d[ts_ * 128:(ts_ + 1) * 128, :], r=[attn_db], w=[attn])
                    out_proj(ti, xt, attn)
        kb.stack = root

    caus8 = kb.sb("caus8", [8, 16, 8], BF16)

    cur, cur_b = x_in, Buf("x_in")
    nxt = 0
    for phs in phases:
        if phs[0] == "A":
            mlstm_layer(int(phs[1]), cur, cur_b, xs[nxt], xs_b[nxt])
        elif phs[0] == "P":
            peer_layer(int(phs[1]), cur, cur_b, xs[nxt], xs_b[nxt])
        elif phs == "KV":
            setup_moba_consts()
            kb.cp(DVE, caus8[:], maskP[0:8, 0:8].unsqueeze(1).to_broadcast([8, 16, 8]), r=[maskP], w=[caus8])
            kv_phase(cur, cur_b)
            continue
        elif phs[0] == "B":
            moba_layer(int(phs[1]) - 2, int(phs[1]), cur, cur_b, xs[nxt], xs_b[nxt])
        cur, cur_b = xs[nxt], xs_b[nxt]
        nxt = 1 - nxt
    with ExitStack() as ph:
        kb.stack = ph
        cpb = [kb.sb("cpb", [128, D], F32) for _ in range(2)]
        for ti in range(NT):
            t = cpb[ti % 2]
            kb.dma(t[:], cur[ti * 128:(ti + 1) * 128, :], r=[cur_b], w=[t])
            kb.dma(y_out[ti * 128:(ti + 1) * 128, :], t[:], r=[t], w=[out_bufs["y"]])
    S.finish(list(out_bufs.values()) + xs_b + [KT_db, V_db, attn_db, attn_dpb])
    root.close()
    print("instr counts:", [(e.name, e.n_inst, e.n_wait) for e in S.engs])
    return nc


def make_in_maps(inp):
    maps = []
    for c in range(8):
        b = c % 4
        x = np.concatenate([inp["x_prompt"][b], inp["x_sample"][32 * b:32 * b + 32].reshape(256, D)], axis=0)
        m = {
            "x_in": np.ascontiguousarray(x),
            "stC_in": np.ascontiguousarray(inp["state_C"][:, 32 * b:32 * b + 32]),
            "stn_in": np.ascontiguousarray(inp["state_n"][:, 32 * b:32 * b + 32]),
            "stm_in": np.ascontiguousarray(inp["state_m"][:, 32 * b:32 * b + 32]),
            "w_in_a": inp["w_in_a"], "b_gates_a": inp["b_gates_a"], "norm_a": inp["norm_a"],
            "w_out_a": inp["w_out_a"], "ln_g": inp["ln_g"], "ln_b": inp["ln_b"],
            "peer_wq": inp["peer_wq"],
            "w_kv": inp["w_kv"], "w_q_b": inp["w_q_b"], "w_out_b": inp["w_out_b"],
            "cache_k": inp["cache_k"].reshape(2560 * 128, D), "cache_v": inp["cache_v"].reshape(2560 * 128, D),
            "page_table": np.ascontiguousarray(inp["page_table"][32 * b:32 * b + 32]).astype(np.int32),
            "peer_keysT": np.ascontiguousarray(inp["peer_keys"].reshape(4, 16, 128, 128).transpose(0, 1, 3, 2)),
        }
        for i in range(4):
            m[f"peer_u{i}"] = inp["peer_u"][i]
            m[f"peer_v{i}"] = inp["peer_v"][i]
        maps.append(m)
    return maps


DBG = {"np_tiles": NT_P, "ns_tiles": NT_S, "ns_seq": 16, "heads": 16, "select": True, "e1": True}
FULL_PHASES = ("A0", "P0", "A1", "P1", "KV", "B2", "P2", "B3", "P3")


def kernel(**inp):
    inp = {k: np.asarray(v) for k, v in inp.items()}
    nc = build_program(phases=FULL_PHASES)
    res = run_bass_kernel_spmd(nc, make_in_maps(inp), core_ids=list(range(8)))
    R = res.results
    f32 = np.float32
    y_p = np.stack([R[b]["y_out"][:4096] for b in range(4)]).astype(f32)
    y_s = np.concatenate([R[b]["y_out"][4096:].reshape(32, 8, D) for b in range(4)]).astype(f32)
    C_p = np.stack([R[b]["C_out"][:, 0] for b in range(4)], axis=1).astype(f32)
    n_p = np.stack([R[b]["n_out"][:, 0] for b in range(4)], axis=1).astype(f32)
    m_p = np.stack([R[b]["m_out"][:, 0] for b in range(4)], axis=1).astype(f32)
    C_s = np.concatenate([R[b]["C_out"][:, 1:] for b in range(4)], axis=1).astype(f32)
    n_s = np.concatenate([R[b]["n_out"][:, 1:] for b in range(4)], axis=1).astype(f32)
    m_s = np.concatenate([R[b]["m_out"][:, 1:] for b in range(4)], axis=1).astype(f32)
    k_p = np.stack([R[b]["k_rows"][:4096].reshape(4096, 16, 64) for b in range(4)]).astype(f32)
    v_p = np.stack([R[b]["v_rows"][:4096].reshape(4096, 16, 64) for b in range(4)]).astype(f32)
    k_s = np.concatenate([R[b]["k_rows"][4096:].reshape(32, 8, 16, 64) for b in range(4)]).astype(f32)
    v_s = np.concatenate([R[b]["v_rows"][4096:].reshape(32, 8, 16, 64) for b in range(4)]).astype(f32)
    return (y_p, y_s, C_p, n_p, m_p, k_p, v_p, C_s, n_s, m_s, k_s, v_s)
```

```python
from contextlib import ExitStack
import numpy as np
import concourse.bass as bass
import concourse.mybir as mybir
from concourse.bass_utils import run_bass_kernel_spmd

F32 = mybir.dt.float32
BF16 = mybir.dt.bfloat16
I32 = mybir.dt.int32
U32 = mybir.dt.uint32
AF = mybir.ActivationFunctionType
ALU = mybir.AluOpType
AX = mybir.AxisListType

D = 1024
NT_P = 32
NT_S = 2
NT = NT_P + NT_S
NTOK = NT * 128
NSEQ_S = 32
NSEQ = 1 + NSEQ_S
LN_EPS = 1e-5
DN_ALPHA = (2.0 * 4) ** 0.25
NEG = -30000.0


class Buf:
    __slots__ = ("name", "w", "rs")

    def __init__(self, name=""):
        self.name = name
        self.w = None
        self.rs = {}


class Eng:
    SEM_LIMIT = 15000

    def __init__(self, S, name, handle, npool=10, self_sync=True):
        self.S = S
        self.name = name
        self.h = handle
        self.self_sync = self_sync
        self.waited = {}
        self.count = 0
        self.sem = None
        self.semkey = None
        self.nsem = 0
        self.pool = []
        self.pool_i = 0
        self.npool = npool
        self.n_inst = 0
        self.n_wait = 0

    def _new_sem(self):
        self.nsem += 1
        self.sem = self.S.nc.alloc_semaphore(name=f"s_{self.name}_{self.nsem}")
        self.semkey = f"{self.name}_{self.nsem}"
        self.count = 0

    def wait_tok(self, tok):
        semkey, sem, val = tok
        if self.waited.get(semkey, 0) >= val:
            return
        self.h.wait_ge(sem, val)
        self.n_wait += 1
        self.waited[semkey] = val


class Sched:
    def __init__(self, nc):
        self.nc = nc
        self.pe = Eng(self, "pe", nc.tensor, self_sync=False)
        self.dve = Eng(self, "dve", nc.vector)
        self.act = Eng(self, "act", nc.scalar)
        self.pool = Eng(self, "pool", nc.gpsimd, npool=48)
        self.sp = Eng(self, "sp", nc.sync)
        self.engs = [self.pe, self.dve, self.act, self.pool, self.sp]

    def emit(self, eng, fn, r=(), w=(), dma=False):
        toks = {}

        def add(tok):
            k = tok[0]
            if k not in toks or toks[k][2] < tok[2]:
                toks[k] = tok
        for b in r:
            if b.w is not None:
                add(b.w)
        for b in w:
            if b.w is not None:
                add(b.w)
            for k, (sem, val) in b.rs.items():
                add((k, sem, val))
        for k, tok in toks.items():
            if (not dma) and (not eng.self_sync) and k == eng.semkey:
                continue
            eng.wait_tok(tok)
        if dma:
            if len(eng.pool) < eng.npool:
                nm = f"d_{eng.name}_{len(eng.pool)}"
                ent = [nm, self.nc.alloc_semaphore(name=nm), 0]
                eng.pool.append(ent)
            else:
                ent = eng.pool[eng.pool_i % eng.npool]
            eng.pool_i += 1
            if ent[2] > 0:
                eng.wait_tok((ent[0], ent[1], ent[2]))
            inst = fn()
            ent[2] += 16
            inst.then_inc(ent[1], 16)
            tok = (ent[0], ent[1], ent[2])
        else:
            if eng.sem is None or eng.count >= Eng.SEM_LIMIT:
                eng._new_sem()
            inst = fn()
            eng.count += 1
            inst.then_inc(eng.sem, 1)
            tok = (eng.semkey, eng.sem, eng.count)
        eng.n_inst += 1
        k = tok[0]
        for b in r:
            if k not in b.rs or b.rs[k][1] < tok[2]:
                b.rs[k] = (tok[1], tok[2])
        for b in w:
            b.w = tok
            b.rs = {}
        return tok

    def finish(self, bufs):
        for b in bufs:
            if b.w is not None:
                self.sp.wait_tok(b.w)


class T:
    def __init__(self, h, name, psum=False):
        self.h = h
        self.b = Buf(name)
        self.b_psum = psum

    def __getitem__(self, idx):
        return self.h[idx]


class K:
    def __init__(self, nc):
        self.nc = nc
        self.S = Sched(nc)
        self.stack = None
        self.uid = 0

    def sb(self, name, shape, dt, stack=None):
        self.uid += 1
        st = stack if stack is not None else self.stack
        h = st.enter_context(self.nc.sbuf_tensor(f"{name}_{self.uid}", list(shape), dt))
        return T(h, name)

    def ps(self, name, shape, dt, stack=None):
        self.uid += 1
        st = stack if stack is not None else self.stack
        h = st.enter_context(self.nc.psum_tensor(f"{name}_{self.uid}", list(shape), dt))
        return T(h, name, psum=True)

    def _rw(self, r, w):
        rb, wb = [], []
        for x in r:
            if isinstance(x, T):
                (wb if x.b_psum else rb).append(x.b)
            else:
                rb.append(x)
        for x in w:
            wb.append(x.b if isinstance(x, T) else x)
        return dict(r=rb, w=wb)

    def dma(self, out, in_, r=(), w=(), q=None, slow=False):
        S = self.S
        eng = q or S.sp
        kw = dict(allow_slow_non_contiguous=True) if slow else {}
        return S.emit(eng, lambda: eng.h.dma_start(out=out, in_=in_, **kw), **self._rw(r, w), dma=True)

    def mm(self, out, lhsT, rhs, start, stop, r=(), w=()):
        nc = self.nc
        return self.S.emit(self.S.pe, lambda: nc.tensor.matmul(out, lhsT, rhs, start=start, stop=stop),
                           **self._rw(r, w))

    def tr(self, out, in_, ident, r=(), w=()):
        nc = self.nc
        return self.S.emit(self.S.pe, lambda: nc.tensor.transpose(out, in_, ident), **self._rw(r, w))

    def act(self, out, in_, func, r=(), w=(), bias=None, scale=None):
        nc = self.nc
        kw = {}
        if bias is not None:
            kw["bias"] = bias
        if scale is not None:
            kw["scale"] = scale
        return self.S.emit(self.S.act, lambda: nc.scalar.activation(out=out, in_=in_, func=func, **kw),
                           **self._rw(r, w))

    def v(self, eng, fn, r=(), w=()):
        return self.S.emit(eng, fn, **self._rw(r, w))

    def ts(self, eng, out, in0, s1, op0, s2=None, op1=None, r=(), w=()):
        kw = dict(out=out, in0=in0, scalar1=s1, scalar2=s2, op0=op0)
        if op1 is not None:
            kw["op1"] = op1
        return self.S.emit(eng, lambda: eng.h.tensor_scalar(**kw), **self._rw(r, w))

    def tt(self, eng, out, in0, in1, op, r=(), w=()):
        return self.S.emit(eng, lambda: eng.h.tensor_tensor(out=out, in0=in0, in1=in1, op=op),
                           **self._rw(r, w))

    def cp(self, eng, out, in_, r=(), w=()):
        return self.S.emit(eng, lambda: eng.h.tensor_copy(out=out, in_=in_), **self._rw(r, w))


def bc_rows(dram_ap_row, n):
    return dram_ap_row.to_broadcast([128, n])


def build_program(phases=("A0",), debug=False):
    nc = bass.Bass("TRN2", target_bir_lowering=False)
    kb = K(nc)
    S = kb.S
    DVE, ACT, POOL, PE = S.dve, S.act, S.pool, S.pe

    def din(name, shape, dt=F32):
        return nc.dram_tensor(name, list(shape), dt, kind="ExternalInput").ap()

    def dout(name, shape, dt=F32):
        return nc.dram_tensor(name, list(shape), dt, kind="ExternalOutput").ap()

    def dint(name, shape, dt=F32):
        return nc.dram_tensor(name, list(shape), dt, kind="Internal").ap()

    x_in = din("x_in", [NTOK, D])
    stC_in = din("stC_in", [2, NSEQ_S, 4, 256, 256])
    stn_in = din("stn_in", [2, NSEQ_S, 4, 256])
    stm_in = din("stm_in", [2, NSEQ_S, 4])
    w_in_a = din("w_in_a", [2, D, 4104])
    b_gates_a = din("b_gates_a", [2, 8])
    norm_a = din("norm_a", [2, D])
    w_out_a = din("w_out_a", [2, D, D])
    ln_g = din("ln_g", [4, 2, D])
    ln_b = din("ln_b", [4, 2, D])
    w_kv = din("w_kv", [D, 2048])
    w_q_b = din("w_q_b", [2, D, D])
    w_out_b = din("w_out_b", [2, D, D])
    cache_k = din("cache_k", [2560 * 128, D])
    cache_v = din("cache_v", [2560 * 128, D])
    page_table = din("page_table", [NSEQ_S, 16], I32)
    k_rows = dout("k_rows", [NTOK, D])
    v_rows = dout("v_rows", [NTOK, D])
    peer_wq = din("peer_wq", [4, D, 2048])
    peer_keysT = din("peer_keysT", [4, 16, 128, 128])
    peer_u = [din(f"peer_u{i}", [16384, D]) for i in range(4)]
    peer_v = [din(f"peer_v{i}", [16384, D]) for i in range(4)]

    y_out = dout("y_out", [NTOK, D])
    C_out = dout("C_out", [2, NSEQ, 4, 256, 256])
    n_out = dout("n_out", [2, NSEQ, 4, 256])
    m_out = dout("m_out", [2, NSEQ, 4])
    out_bufs = {k: Buf(k) for k in ("y", "C", "n", "m", "k", "v")}

    xs = [dint("xs0", [NTOK, D]), dint("xs1", [NTOK, D])]
    ub16 = [dint(f"ub16_{i}", [16384, D], BF16) for i in range(4)]
    vb16 = [dint(f"vb16_{i}", [16384, D], BF16) for i in range(4)]
    tb_b = {("u", i): Buf(f"ub{i}") for i in range(4)}
    tb_b.update({("v", i): Buf(f"vb{i}") for i in range(4)})
    xs_b = [Buf("xs0"), Buf("xs1")]

    root = ExitStack()
    kb.stack = root
    ident_f = kb.sb("ident_f", [128, 128], F32)
    ident_b = kb.sb("ident_b", [128, 128], BF16)
    iota_pi = kb.sb("iota_pi", [128, 1], I32)
    iota_fi = kb.sb("iota_fi", [128, 128], I32)
    iota_p = kb.sb("iota_p", [128, 1], F32)
    iota_f = kb.sb("iota_f", [128, 128], F32)
    pdiv = kb.sb("pdiv", [128, 1], F32)
    fdiv = kb.sb("fdiv", [128, 128], F32)
    tmpi = kb.sb("tmpi", [128, 128], I32)
    maskP = kb.sb("maskP", [128, 128], F32)
    maskS = kb.sb("maskS", [128, 128], F32)
    seq1h = kb.sb("seq1h", [128, 16], F32)
    sel4 = kb.sb("sel4", [4, 4, 128], F32)
    scn = kb.sb("scn", [4, 4, 128], F32)
    one_col = kb.sb("one_col", [128, 1], F32)
    eps_col = kb.sb("eps_col", [128, 1], F32)

    kb.v(POOL, lambda: nc.gpsimd.iota(iota_pi[:], pattern=[[0, 1]], base=0, channel_multiplier=1), w=[iota_pi])
    kb.v(POOL, lambda: nc.gpsimd.iota(iota_fi[:], pattern=[[1, 128]], base=0, channel_multiplier=0), w=[iota_fi])
    kb.cp(DVE, iota_p[:], iota_pi[:], r=[iota_pi], w=[iota_p])
    kb.cp(DVE, iota_f[:], iota_fi[:], r=[iota_fi], w=[iota_f])
    kb.v(DVE, lambda: nc.vector.tensor_single_scalar(out=tmpi[:, 0:1], in_=iota_pi[:], scalar=3, op=ALU.arith_shift_right),
         r=[iota_pi], w=[tmpi])
    kb.cp(DVE, pdiv[:], tmpi[:, 0:1], r=[tmpi], w=[pdiv])
    kb.v(DVE, lambda: nc.vector.tensor_single_scalar(out=tmpi[:], in_=iota_fi[:], scalar=3, op=ALU.arith_shift_right),
         r=[iota_fi], w=[tmpi])
    kb.cp(DVE, fdiv[:], tmpi[:], r=[tmpi], w=[fdiv])
    kb.ts(DVE, ident_f[:], iota_f[:], iota_p[:, 0:1], ALU.is_equal, r=[iota_f, iota_p], w=[ident_f])
    kb.cp(DVE, ident_b[:], ident_f[:], r=[ident_f], w=[ident_b])
    kb.ts(DVE, maskP[:], iota_f[:], iota_p[:, 0:1], ALU.is_ge, -1.0, ALU.add, r=[iota_f, iota_p], w=[maskP])
    kb.ts(DVE, maskP[:], maskP[:], -NEG, ALU.mult, r=[maskP], w=[maskP])
    kb.ts(DVE, maskS[:], iota_f[:], iota_p[:, 0:1], ALU.is_ge, r=[iota_f, iota_p], w=[maskS])
    kb.ts(DVE, fdiv[:], fdiv[:], pdiv[:, 0:1], ALU.is_equal, r=[fdiv, pdiv], w=[fdiv])
    kb.tt(DVE, maskS[:], maskS[:], fdiv[:], ALU.mult, r=[maskS, fdiv], w=[maskS])
    kb.ts(DVE, maskS[:], maskS[:], -1.0, ALU.add, -NEG, ALU.mult, r=[maskS], w=[maskS])
    kb.ts(DVE, seq1h[:], iota_f[:, 0:16], pdiv[:, 0:1], ALU.is_equal, r=[iota_f, pdiv], w=[seq1h])
    kb.cp(DVE, sel4[:], ident_f[0:4, 0:4].unsqueeze(2).to_broadcast([4, 4, 128]), r=[ident_f], w=[sel4])
    kb.v(DVE, lambda: nc.vector.memset(scn[:, 0, :], 1.0), w=[scn])
    kb.v(DVE, lambda: nc.vector.memset(scn[:, 0, 0:1], 0.0), w=[scn])
    kb.v(DVE, lambda: nc.vector.memset(scn[:, 1, :], 0.0), w=[scn])
    kb.v(DVE, lambda: nc.vector.memset(scn[:, 1, 0:1], -1e30), w=[scn])
    kb.v(DVE, lambda: nc.vector.memset(scn[:, 2, :], 1.0), w=[scn])
    kb.v(DVE, lambda: nc.vector.memset(scn[:, 3, :], 0.0), w=[scn])
    kb.v(DVE, lambda: nc.vector.memset(scn[:, 2, :].rearrange("p (j t) -> p j t", t=8)[:, :, 0:1], 0.0), w=[scn])
    kb.v(DVE, lambda: nc.vector.memset(scn[:, 3, :].rearrange("p (j t) -> p j t", t=8)[:, :, 0:1], -1e30), w=[scn])
    kb.v(DVE, lambda: nc.vector.memset(one_col[:], 1.0), w=[one_col])
    kb.v(DVE, lambda: nc.vector.memset(eps_col[:], LN_EPS), w=[eps_col])

    C = dict(ident_f=ident_f, ident_b=ident_b, maskP=maskP, maskS=maskS, seq1h=seq1h, sel4=sel4, scn=scn,
             one_col=one_col, eps_col=eps_col)

    def layer_norm_tile(z, lng, lnb, out_t, tmp_stats):
        st, mv, rstd = tmp_stats
        for i in range(2):
            kb.v(DVE, lambda i=i: nc.vector.bn_stats(out=st[:, i, :], in_=z[:, i * 512:(i + 1) * 512]), r=[z], w=[st])
        kb.v(DVE, lambda: nc.vector.bn_aggr(out=mv[:], in_=st[:].rearrange("p a b -> p (a b)")), r=[st], w=[mv])
        kb.act(rstd[:], mv[:, 1:2], AF.Sqrt, r=[mv, eps_col], w=[rstd], bias=eps_col[:, 0:1], scale=1.0)
        kb.v(DVE, lambda: nc.vector.reciprocal(out=rstd[:], in_=rstd[:]), r=[rstd], w=[rstd])
        kb.ts(DVE, out_t[:], z[:], mv[:, 0:1], ALU.subtract, rstd[:, 0:1], ALU.mult, r=[z, mv, rstd], w=[out_t])
        kb.tt(POOL, out_t[:], out_t[:], lng[:], ALU.mult, r=[out_t, lng], w=[out_t])
        kb.tt(POOL, out_t[:], out_t[:], lnb[:], ALU.add, r=[out_t, lnb], w=[out_t])

    def mlstm_layer(l, src, src_b, dst, dst_b):
        with ExitStack() as ph:
            kb.stack = ph
            wbf = kb.sb("wbf", [128, 8, 4104], BF16)
            wout = kb.sb("wout", [128, 8, D], BF16)
            wst = [kb.sb("wst", [128, 1026], F32) for _ in range(2)]
            normg = kb.sb("normg", [128, D], F32)
            lng = kb.sb("lng", [128, D], F32)
            lnb = kb.sb("lnb", [128, D], F32)
            bi_col = kb.sb("bi_col", [4, 1], F32)
            nbf_col = kb.sb("nbf_col", [4, 1], F32)
            i = 0
            for kc in range(8):
                for q4 in range(4):
                    st = wst[i % 2]
                    n = 1026
                    kb.dma(st[:, 0:n], w_in_a[l, kc * 128:(kc + 1) * 128, q4 * 1026:q4 * 1026 + n], w=[st])
                    eng = (DVE, POOL)[i % 2]
                    kb.cp(eng, wbf[:, kc, q4 * 1026:q4 * 1026 + n], st[:, 0:n], r=[st], w=[wbf])
                    i += 1
            for kc in range(8):
                st = wst[i % 2]
                kb.dma(st[:, 0:1024], w_out_a[l, kc * 128:(kc + 1) * 128, :], w=[st])
                eng = (DVE, POOL)[i % 2]
                kb.cp(eng, wout[:, kc, :], st[:, 0:1024], r=[st], w=[wout])
                i += 1
            kb.dma(normg[:], bc_rows(norm_a[l:l + 1, :], D), w=[normg])
            kb.dma(lng[:], bc_rows(ln_g[l, 0:1, :], D), w=[lng])
            kb.dma(lnb[:], bc_rows(ln_b[l, 0:1, :], D), w=[lnb])
            kb.dma(bi_col[:], b_gates_a[l, 0:4].rearrange("(p o) -> p o", o=1), w=[bi_col], slow=True)
            kb.dma(nbf_col[:], b_gates_a[l, 4:8].rearrange("(p o) -> p o", o=1), w=[nbf_col], slow=True)
            kb.ts(DVE, nbf_col[:], nbf_col[:], -1.0, ALU.mult, r=[nbf_col], w=[nbf_col])

            x_t = [kb.sb("x_t", [128, D], F32) for _ in range(2)]
            x_b = kb.sb("x_b", [128, D], BF16)
            xT = kb.sb("xT", [128, 8, 128], BF16)
            qT = kb.sb("qT", [128, 8, 128], BF16)
            kT = kb.sb("kT", [128, 8, 128], BF16)
            k_tm = kb.sb("k_tm", [128, D], BF16)
            v_aug = kb.sb("v_aug", [128, 4, 257], BF16)
            sig_o = kb.sb("sig_o", [128, D], BF16)
            hfin = kb.sb("hfin", [128, D], F32)
            hfin_b = x_b
            hT = xT
            z_t = kb.sb("z_t", [128, D], F32)
            xo_t = z_t
            rows = kb.sb("rows", [4, 12, 128], F32)
            seqr = kb.sb("seqr", [4, 4, 16], F32)
            cols = kb.sb("cols", [128, 3, 4], F32)
            decay_bc = kb.sb("decay_bc", [128, 4, 16], F32)
            E_sb = kb.sb("E_sb", [128, 128], F32)
            PT = kb.sb("PT", [128, 128], BF16)
            wi_bc = kb.sb("wi_bc", [128, 128], F32)
            qpT = kb.sb("qpT", [128, 2, 2176], BF16)
            wm = kb.sb("wm", [128, 16], F32)
            wv_blk = kb.sb("wv_blk", [128, 16, 257], BF16)
            dm = kb.sb("dm", [128, 2], F32)
            hraw = kb.sb("hraw", [128, 256], F32)
            bst = kb.sb("bst", [128, 6], F32)
            bmv = kb.sb("bmv", [128, 2], F32)
            brs = kb.sb("brs", [128, 1], F32)
            lst = kb.sb("lst", [128, 2, 6], F32)
            lmv = kb.sb("lmv", [128, 2], F32)
            lrs = kb.sb("lrs", [128, 1], F32)
            CTp = kb.sb("CTp", [128, 4, 2, 257], F32)
            CTbp = kb.sb("CTbp", [128, 4, 2, 257], BF16)
            CTf = [kb.sb("CTf", [128, 2, 257], F32) for _ in range(2)]
            CTbs = kb.sb("CTbs", [128, 2, 2, 257], BF16)
            Cio = [kb.sb("Cio", [128, 2, 256], F32) for _ in range(2)]
            nio = kb.sb("nio", [16, 256], F32)
            m_carry = kb.sb("m_carry", [4, 1], F32)
            p_a = [kb.ps("p_a", [128, 512], F32) for _ in range(2)]
            p_tr = kb.ps("p_tr", [128, 1024], BF16)
            p_e = kb.ps("p_e", [128, 512], F32)
            p_s = kb.ps("p_s", [128, 512], F32)
            p_n = kb.ps("p_n", [128, 512], F32)
            p_u = [kb.ps("p_u", [128, 512], F32) for _ in range(2)]
            pe_b = pw_b = pd_b = pc_b = p_e
            ps_b = pg_b = p_s

            kb.v(DVE, lambda: nc.vector.memset(v_aug[:, :, 256:257], 1.0), w=[v_aug])
            kb.v(DVE, lambda: nc.vector.memset(qpT[:], 0.0), w=[qpT])
            kb.v(POOL, lambda: nc.gpsimd.memset(CTp[:], 0.0), w=[CTp])
            kb.v(POOL, lambda: nc.gpsimd.memset(CTbp[:], 0.0), w=[CTbp])
            kb.v(DVE, lambda: nc.vector.memset(m_carry[:], 0.0), w=[m_carry])

            pa_i = [0]

            def next_pa():
                pa_i[0] += 1
                return p_a[pa_i[0] % 2]

            def state_in(h, j, seq, ctf):
                cio = Cio[j % 2]
                kb.dma(cio[:], stC_in[l, seq, h].rearrange("(vc p) k -> p vc k", p=128), w=[cio])
                pt = next_pa()
                for vc in range(2):
                    for kc in range(2):
                        kb.tr(pt[:, kc * 256 + vc * 128: kc * 256 + vc * 128 + 128], cio[:, vc, kc * 128:(kc + 1) * 128],
                              ident_f[:], r=[cio, ident_f], w=[pt])
                kb.cp(DVE, ctf[:, :, 0:256], pt[:].rearrange("p (c v) -> p c v", c=2), r=[pt], w=[ctf])

            def state_out(h, seq_out, ctf, j):
                cio = Cio[j % 2]
                pt = next_pa()
                for vc in range(2):
                    for kc in range(2):
                        kb.tr(pt[:, vc * 256 + kc * 128: vc * 256 + kc * 128 + 128], ctf[:, kc, vc * 128:(vc + 1) * 128],
                              ident_f[:], r=[ctf, ident_f], w=[pt])
                kb.cp(ACT_or_DVE(j), cio[:], pt[:].rearrange("p (c v) -> p c v", c=2), r=[pt], w=[cio])
                kb.dma(C_out[l, seq_out, h].rearrange("(vc p) k -> p vc k", p=128), cio[:], r=[cio], w=[out_bufs["C"]])

            def ACT_or_DVE(j):
                return DVE

            def tile_step(ti):
                is_p = ti < NT_P
                nseq = 1 if is_p else 16
                L = 128 // nseq
                mask = maskP if is_p else maskS
                r_row = scn[:, 0, :] if is_p else scn[:, 2, :]
                a_row = scn[:, 1, :] if is_p else scn[:, 3, :]
                xt = x_t[ti % 2]
                kb.dma(xt[:], src[ti * 128:(ti + 1) * 128, :], r=[src_b], w=[xt])
                kb.act(x_b[:], xt[:], AF.Copy, r=[xt], w=[x_b])
                for kc in range(8):
                    kb.tr(p_tr[:, kc * 128:(kc + 1) * 128], x_b[:, kc * 128:(kc + 1) * 128], ident_b[:], r=[x_b, ident_b], w=[p_tr])
                kb.cp(DVE, xT[:].rearrange("p a b -> p (a b)"), p_tr[:], r=[p_tr], w=[xT])
                for g4 in range(4):
                    pa = next_pa()
                    for gg in range(4):
                        g = g4 * 4 + gg
                        col = g * 128 if g < 8 else 1024 + (g - 8) * 128
                        for kc in range(8):
                            kb.mm(pa[:, gg * 128:(gg + 1) * 128], wbf[:, kc, col:col + 128], xT[:, kc, :], kc == 0, kc == 7,
                                  r=[wbf, xT], w=[pa])
                    if g4 < 2:
                        kb.act(qT[:, g4 * 4:(g4 + 1) * 4, :].rearrange("p a b -> p (a b)"), pa[:], AF.Copy, r=[pa], w=[qT])
                    else:
                        kb.act(kT[:, (g4 - 2) * 4:(g4 - 1) * 4, :].rearrange("p a b -> p (a b)"), pa[:], AF.Copy, r=[pa], w=[kT],
                               scale=1.0 / 16.0)
                for blk in range(6):
                    pa = next_pa()
                    col = 1024 + blk * 512
                    for kc in range(8):
                        kb.mm(pa[:], xT[:, kc, :], wbf[:, kc, col:col + 512], kc == 0, kc == 7, r=[wbf, xT], w=[pa])
                    if blk < 2:
                        kb.act(k_tm[:, blk * 512:(blk + 1) * 512], pa[:], AF.Copy, r=[pa], w=[k_tm], scale=1.0 / 16.0)
                    elif blk < 4:
                        b2 = blk - 2
                        kb.cp(DVE, v_aug[:, 2 * b2:2 * b2 + 2, 0:256], pa[:].rearrange("p (h v) -> p h v", h=2), r=[pa], w=[v_aug])
                    else:
                        b2 = blk - 4
                        kb.act(sig_o[:, b2 * 512:(b2 + 1) * 512], pa[:], AF.Sigmoid, r=[pa], w=[sig_o])
                for gi in range(2):
                    for kc in range(8):
                        kb.mm(p_s[0:4, 128 + gi * 128: 256 + gi * 128], wbf[:, kc, 4096 + gi * 4:4100 + gi * 4], xT[:, kc, :],
                              kc == 0, kc == 7, r=[wbf, xT], w=[pg_b])
                R = lambda i: rows[:, i, :]
                kb.ts(DVE, R(0), p_s[0:4, 128:256], bi_col[:, 0:1], ALU.add, r=[pg_b, bi_col], w=[rows])
                kb.act(R(1), p_s[0:4, 256:384], AF.Exp, r=[pg_b, nbf_col], w=[rows], bias=nbf_col[:, 0:1], scale=-1.0)
                kb.act(R(1), R(1), AF.Ln, r=[rows, one_col], w=[rows], bias=one_col[0:4, 0:1], scale=1.0)
                kb.v(DVE, lambda: nc.vector.tensor_tensor_scan(out=R(2), data0=r_row, data1=R(1), initial=0.0,
                                                                op0=ALU.mult, op1=ALU.add), r=[rows, scn], w=[rows])
                kb.tt(DVE, R(3), R(0), R(2), ALU.add, r=[rows], w=[rows])
                kb.cp(DVE, R(4), R(3), r=[rows], w=[rows])
                if is_p:
                    kb.cp(DVE, seqr[:, 0, 0:1], m_carry[:, 0:1], r=[m_carry], w=[seqr])
                else:
                    s0 = (ti - NT_P) * 16
                    kb.dma(seqr[:, 0, 0:16], stm_in[l, s0:s0 + 16, :].rearrange("j h -> h j"), w=[seqr], slow=True)
                v3 = lambda ap: ap.rearrange("p (j t) -> p j t", t=L)
                kb.tt(DVE, v3(R(4))[:, :, 0], v3(R(3))[:, :, 0], seqr[:, 0, 0:nseq], ALU.max, r=[rows, seqr], w=[rows])
                kb.v(DVE, lambda: nc.vector.tensor_tensor_scan(out=R(5), data0=a_row, data1=R(4), initial=0.0,
                                                                op0=ALU.add, op1=ALU.max), r=[rows, scn], w=[rows])
                kb.ts(DVE, R(6), R(5), -1.0, ALU.mult, r=[rows], w=[rows])
                kb.tt(DVE, R(7), R(2), R(5), ALU.subtract, r=[rows], w=[rows])
                kb.cp(DVE, seqr[:, 1, 0:nseq], v3(R(5))[:, :, L - 1], r=[rows], w=[seqr])
                kb.cp(DVE, v3(R(8)), seqr[:, 0, 0:nseq].unsqueeze(2).to_broadcast([4, nseq, L]), r=[seqr], w=[rows])
                kb.cp(DVE, v3(R(9)), seqr[:, 1, 0:nseq].unsqueeze(2).to_broadcast([4, nseq, L]), r=[seqr], w=[rows])
                kb.tt(DVE, R(10), R(8), R(5), ALU.subtract, r=[rows], w=[rows])
                kb.tt(DVE, R(11), R(3), R(9), ALU.subtract, r=[rows], w=[rows])
                kb.tt(DVE, seqr[:, 2, 0:nseq], seqr[:, 0, 0:nseq], seqr[:, 1, 0:nseq], ALU.subtract, r=[seqr], w=[seqr])
                kb.tt(DVE, seqr[:, 3, 0:nseq], seqr[:, 1, 0:nseq], v3(R(2))[:, :, L - 1], ALU.subtract, r=[seqr, rows], w=[seqr])
                if is_p:
                    kb.cp(DVE, m_carry[:, 0:1], seqr[:, 3, 0:1], r=[seqr], w=[m_carry])
                for i, ri in enumerate((3, 7, 11)):
                    kb.tr(p_e[:, 320 + 4 * i: 324 + 4 * i], R(ri), ident_f[0:4, 0:4], r=[rows, ident_f], w=[pc_b])
                kb.cp(DVE, cols[:, 0, :], p_e[:, 320:324], r=[pc_b], w=[cols])
                kb.act(cols[:, 1:3, :].rearrange("p a b -> p (a b)"), p_e[:, 324:332], AF.Exp, r=[pc_b], w=[cols])
                for h in range(4):
                    kb.mm(p_e[:, 256 + h * 16: 256 + h * 16 + nseq], sel4[:, h, :], seqr[:, 2, 0:nseq], True, True,
                          r=[sel4, seqr], w=[pd_b])
                kb.act(decay_bc[:, :, 0:nseq], p_e[:, 256:320].rearrange("p (h j) -> p h j", h=4)[:, :, 0:nseq], AF.Exp,
                       r=[pd_b], w=[decay_bc])
                for h in range(4):
                    kb.mm(p_e[:, 0:128], sel4[:, h, :], R(6), True, False, r=[sel4, rows], w=[pe_b])
                    kb.mm(p_e[:, 0:128], ident_f[:], mask[:], False, True, r=[ident_f, mask], w=[pe_b])
                    kb.act(E_sb[:], p_e[:, 0:128], AF.Exp, r=[pe_b, cols], w=[E_sb], bias=cols[:, 0, h:h + 1], scale=1.0)
                    for c in range(2):
                        kb.mm(p_s[:, 0:128], kT[:, 2 * h + c, :], qT[:, 2 * h + c, :], c == 0, c == 1, r=[kT, qT], w=[ps_b])
                    kb.tt(DVE, PT[:], p_s[:, 0:128], E_sb[:], ALU.mult, r=[ps_b, E_sb], w=[PT])
                    kb.mm(p_e[:, 128:256], sel4[:, h, :], R(10), True, True, r=[sel4, rows], w=[pw_b])
                    kb.act(wi_bc[:], p_e[:, 128:256], AF.Exp, r=[pw_b], w=[wi_bc])
                    qv = qpT[:].rearrange("p c (j x) -> p c j x", x=136)[:, :, 0:nseq, 0:L] if not is_p else None
                    if is_p:
                        kb.tt(DVE, qpT[:, :, 0:128], qT[:, 2 * h:2 * h + 2, :], wi_bc[:].unsqueeze(1).to_broadcast([128, 2, 128]),
                              ALU.mult, r=[qT, wi_bc], w=[qpT])
                    else:
                        kb.tt(DVE, qv, qT[:, 2 * h:2 * h + 2, :].rearrange("p c (j t) -> p c j t", t=L),
                              wi_bc[:].rearrange("p (j t) -> p j t", t=L).unsqueeze(1).to_broadcast([128, 2, nseq, L]),
                              ALU.mult, r=[qT, wi_bc], w=[qpT])
                    ctfs = []
                    if not is_p:
                        s0 = (ti - NT_P) * 16
                        kb.dma(nio[:], stn_in[l, s0:s0 + 16, h, :], w=[nio])
                        pa = next_pa()
                        for c in range(2):
                            kb.tr(pa[:, c * 16:(c + 1) * 16], nio[:, c * 128:(c + 1) * 128], ident_f[0:16, 0:16],
                                  r=[nio, ident_f], w=[pa])
                        kb.cp(DVE, nstage[:], pa[:, 0:32], r=[pa], w=[nstage])
                    nmm = 1 + 2 * nseq
                    kb.mm(p_n[:, 0:257], PT[:], v_aug[:, h, :], True, False, r=[PT, v_aug], w=[p_n])
                    if is_p:
                        for c in range(2):
                            kb.mm(p_n[:, 0:257], qpT[:, c, 0:128], CTbp[:, h, c, :], False, c == 1, r=[qpT, CTbp], w=[p_n])
                    else:
                        for j in range(nseq):
                            ctf = CTf[j % 2]
                            state_in(h, j, s0 + j, ctf)
                            kb.cp(DVE, ctf[:, :, 256], nstage[:, :].rearrange("p (c j) -> p c j", c=2)[:, :, j], r=[nstage], w=[ctf])
                            kb.cp(POOL, CTbs[:, j % 2, :, :], ctf[:], r=[ctf], w=[CTbs])
                            for c in range(2):
                                kb.mm(p_n[:, 0:257], qpT[:, c, j * 128:(j + 1) * 128], CTbs[:, j % 2, c, :], False,
                                      (j == nseq - 1) and c == 1, r=[qpT, CTbs], w=[p_n])
                            if j == 0:
                                kb.ts(DVE, wm[:, 0:16], seq1h[:], cols[:, 2, h:h + 1], ALU.mult, r=[seq1h, cols], w=[wm])
                                kb.tt(DVE, wv_blk[:], v_aug[:, h, :].unsqueeze(1).to_broadcast([128, 16, 257]),
                                      wm[:].unsqueeze(2).to_broadcast([128, 16, 257]), ALU.mult, r=[v_aug, wm], w=[wv_blk])
                            for c in range(2):
                                pu = p_u[c]
                                kb.mm(pu[:, 0:257], k_tm[:, h * 256 + c * 128: h * 256 + c * 128 + 128], wv_blk[:, j, :], True, True,
                                      r=[k_tm, wv_blk], w=[pu])
                                kb.v(DVE, lambda c=c, pu=pu, ctf=ctf, j=j: nc.vector.scalar_tensor_tensor(
                                    out=ctf[:, c, :], in0=ctf[:, c, :], scalar=decay_bc[:, h, j:j + 1], in1=pu[:, 0:257],
                                    op0=ALU.mult, op1=ALU.add), r=[ctf, decay_bc, pu], w=[ctf])
                            state_out(h, 1 + s0 + j, ctf, j)
                            kb.cp(DVE, nstage2[:].rearrange("p (c j) -> p c j", c=2)[:, :, j], ctf[:, :, 256], r=[ctf], w=[nstage2])
                        pa = next_pa()
                        for c in range(2):
                            kb.tr(pa[0:16, c * 128:(c + 1) * 128], nstage2[:, c * 16:(c + 1) * 16], ident_f[:], r=[nstage2, ident_f], w=[pa])
                        kb.cp(DVE, nio[:], pa[0:16, 0:256], r=[pa], w=[nio])
                        kb.dma(n_out[l, 1 + s0:1 + s0 + 16, h, :], nio[:], r=[nio], w=[out_bufs["n"]])
                    kb.act(dm[:, 0:1], p_n[:, 256:257], AF.Abs, r=[p_n], w=[dm])
                    kb.ts(DVE, dm[:, 0:1], dm[:, 0:1], cols[:, 1, h:h + 1], ALU.max, r=[dm, cols], w=[dm])
                    kb.v(DVE, lambda: nc.vector.reciprocal(out=dm[:, 1:2], in_=dm[:, 0:1]), r=[dm], w=[dm])
                    kb.ts(DVE, hraw[:], p_n[:, 0:256], dm[:, 1:2], ALU.mult, r=[p_n, dm], w=[hraw])
                    kb.v(DVE, lambda: nc.vector.bn_stats(out=bst[:], in_=hraw[:]), r=[hraw], w=[bst])
                    kb.v(DVE, lambda: nc.vector.bn_aggr(out=bmv[:], in_=bst[:]), r=[bst], w=[bmv])
                    kb.act(brs[:], bmv[:, 1:2], AF.Sqrt, r=[bmv, eps_col], w=[brs], bias=eps_col[:, 0:1], scale=1.0)
                    kb.v(DVE, lambda: nc.vector.reciprocal(out=brs[:], in_=brs[:]), r=[brs], w=[brs])
                    kb.ts(DVE, hfin[:, h * 256:(h + 1) * 256], hraw[:], bmv[:, 0:1], ALU.subtract, brs[:, 0:1], ALU.mult,
                          r=[hraw, bmv, brs], w=[hfin])
                    if is_p:
                        kb.ts(DVE, wv_blk[:, 0, :], v_aug[:, h, :], cols[:, 2, h:h + 1], ALU.mult, r=[v_aug, cols], w=[wv_blk])
                        for c in range(2):
                            pu = p_u[c]
                            kb.mm(pu[:, 0:257], k_tm[:, h * 256 + c * 128: h * 256 + c * 128 + 128], wv_blk[:, 0, :], True, True,
                                  r=[k_tm, wv_blk], w=[pu])
                            kb.v(DVE, lambda c=c, pu=pu: nc.vector.scalar_tensor_tensor(
                                out=CTp[:, h, c, :], in0=CTp[:, h, c, :], scalar=decay_bc[:, h, 0:1], in1=pu[:, 0:257],
                                op0=ALU.mult, op1=ALU.add), r=[CTp, decay_bc, pu], w=[CTp])
                        kb.cp(POOL, CTbp[:, h, :, :], CTp[:, h, :, :], r=[CTp], w=[CTbp])
                if not is_p:
                    s0 = (ti - NT_P) * 16
                    kb.dma(m_out[l, 1 + s0:1 + s0 + 16, :].rearrange("j h -> h j"), seqr[:, 3, 0:16], r=[seqr], w=[out_bufs["m"]], slow=True)
                kb.tt(POOL, hfin[:], hfin[:], normg[:], ALU.mult, r=[hfin, normg], w=[hfin])
                kb.tt(DVE, hfin_b[:], hfin[:], sig_o[:], ALU.mult, r=[hfin, sig_o], w=[hfin_b])
                for kc in range(8):
                    kb.tr(p_tr[:, kc * 128:(kc + 1) * 128], hfin_b[:, kc * 128:(kc + 1) * 128], ident_b[:], r=[hfin_b, ident_b], w=[p_tr])
                kb.cp(DVE, hT[:].rearrange("p a b -> p (a b)"), p_tr[:], r=[p_tr], w=[hT])
                for half in range(2):
                    pa = next_pa()
                    for kc in range(8):
                        kb.mm(pa[:], hT[:, kc, :], wout[:, kc, half * 512:(half + 1) * 512], kc == 0, kc == 7, r=[hT, wout], w=[pa])
                    kb.v(DVE, lambda half=half, pa=pa: nc.vector.scalar_tensor_tensor(
                        out=z_t[:, half * 512:(half + 1) * 512], in0=xt[:, half * 512:(half + 1) * 512], scalar=DN_ALPHA,
                        in1=pa[:], op0=ALU.mult, op1=ALU.add), r=[xt, pa], w=[z_t])
                layer_norm_tile(z_t, lng, lnb, xo_t, (lst, lmv, lrs))
                kb.dma(dst[ti * 128:(ti + 1) * 128, :], xo_t[:], r=[xo_t], w=[dst_b])

            nstage = kb.sb("nstage", [128, 32], F32)
            nstage2 = kb.sb("nstage2", [128, 32], F32)
            for ti in range(NT):
                tile_step(ti)
                if ti == NT_P - 1:
                    kb.v(DVE, lambda: nc.vector.memset(qpT[:], 0.0), w=[qpT])
                    for h in range(4):
                        ctf = CTf[h % 2]
                        kb.cp(DVE, ctf[:], CTp[:, h, :, :], r=[CTp], w=[ctf])
                        state_out(h, 0, ctf, h)
                        pa = next_pa()
                        for c in range(2):
                            kb.tr(pa[0:1, c * 128:(c + 1) * 128], CTp[:, h, c, 256:257], ident_f[:], r=[CTp, ident_f], w=[pa])
                        kb.cp(DVE, nio[0:1, :], pa[0:1, 0:256], r=[pa], w=[nio])
                        kb.dma(n_out[l, 0:1, h, :], nio[0:1, :], r=[nio], w=[out_bufs["n"]])
                    kb.dma(m_out[l, 0:1, :].rearrange("j h -> h j"), m_carry[:, 0:1], r=[m_carry], w=[out_bufs["m"]], slow=True)
        kb.stack = root

    def convert_tables():
        with ExitStack() as ph:
            kb.stack = ph
            NR = 4
            stg = [kb.sb("cv_f", [128, NR * D], F32) for _ in range(2)]
            stb = [kb.sb("cv_b", [128, NR * D], BF16) for _ in range(2)]
            i = 0
            for l in range(4):
                for kind, src_t, dst_t in (("u", peer_u[l], ub16[l]), ("v", peer_v[l], vb16[l])):
                    sv = src_t.rearrange("(c p r) d -> c p (r d)", p=128, r=NR)
                    dv = dst_t.rearrange("(c p r) d -> c p (r d)", p=128, r=NR)
                    for c in range(16384 // (128 * NR)):
                        f, b_ = stg[i % 2], stb[i % 2]
                        kb.dma(f[:], sv[c], w=[f])
                        eng = (DVE, ACT, POOL)[i % 3]
                        if eng is ACT:
                            kb.act(b_[:], f[:], AF.Copy, r=[f], w=[b_])
                        else:
                            kb.cp(eng, b_[:], f[:], r=[f], w=[b_])
                        kb.dma(dv[c], b_[:], r=[b_], w=[tb_b[(kind, l)]], q=ACT)
                        i += 1
        kb.stack = root

    def peer_layer(l, src, src_b, dst, dst_b):
        with ExitStack() as ph:
            kb.stack = ph
            wq = kb.sb("wq", [128, 8, 2048], BF16)
            wst = [kb.sb("wst", [128, 1024], F32) for _ in range(2)]
            keysT = kb.sb("keysT", [128, 16, 128], BF16)
            lng = kb.sb("lng", [128, D], F32)
            lnb = kb.sb("lnb", [128, D], F32)
            i = 0
            for kc in range(8):
                for hf in range(2):
                    st = wst[i % 2]
                    kb.dma(st[:], peer_wq[l, kc * 128:(kc + 1) * 128, hf * 1024:(hf + 1) * 1024], w=[st])
                    kb.cp((DVE, POOL)[i % 2], wq[:, kc, hf * 1024:(hf + 1) * 1024], st[:], r=[st], w=[wq])
                    i += 1
            for g in range(2):
                st = wst[i % 2]
                kb.dma(st[:].rearrange("p (a b) -> p a b", a=8), peer_keysT[l, g * 8:(g + 1) * 8].rearrange("a d k -> d a k"), w=[st])
                kb.cp((DVE, POOL)[i % 2], keysT[:, g * 8:(g + 1) * 8, :], st[:].rearrange("p (a b) -> p a b", a=8), r=[st], w=[keysT])
                i += 1
            kb.dma(lng[:], bc_rows(ln_g[l, 1:2, :], D), w=[lng])
            kb.dma(lnb[:], bc_rows(ln_b[l, 1:2, :], D), w=[lnb])
            x_t = [kb.sb("x_t", [128, D], F32) for _ in range(2)]
            x_b = kb.sb("x_b", [128, D], BF16)
            xT = kb.sb("xT", [128, 8, 128], BF16)
            qT16 = kb.sb("qT16", [128, 16, 128], BF16)
            s_sb = kb.sb("s_sb", [128, 16, 128], F32)
            s_wk = kb.sb("s_wk", [128, 256], F32)
            stop = kb.sb("stop", [128, 16, 16], F32)
            itop = kb.sb("itop", [128, 16, 16], U32)
            itopf = kb.sb("itopf", [128, 16, 16], F32)
            cand = kb.sb("cand", [128, 8, 256], F32)
            gval = kb.sb("gval", [128, 8, 16], F32)
            gpos = kb.sb("gpos", [128, 8, 16], U32)
            ai = kb.sb("ai", [128, 128], U32)
            af = kb.sb("af", [128, 2, 128], F32)
            oh = kb.sb("oh", [128, 128, 16], F32)
            i01 = kb.sb("i01", [128, 2, 128], F32)
            eidx = kb.sb("eidx", [128, 128], I32)
            gate = kb.sb("gate", [128, 8, 16], F32)
            gsum = kb.sb("gsum", [128, 8], F32)
            actv = kb.sb("actv", [128, 128], F32)
            g1 = kb.sb("g1", [128, 128], F32)
            g2 = kb.sb("g2", [128, 128], F32)
            hw = kb.sb("hw", [128, 128], F32)
            NB = 8
            ubuf = [kb.sb("ubuf", [128, D], BF16) for _ in range(NB)]
            vbuf = [kb.sb("vbuf", [128, D], BF16) for _ in range(NB)]
            junk = kb.sb("junk", [128, D], F32)
            y_acc = kb.sb("y_acc", [128, D], F32)
            lst = kb.sb("lst", [128, 2, 6], F32)
            lmv = kb.sb("lmv", [128, 2], F32)
            lrs = kb.sb("lrs", [128, 1], F32)
            p_a = [kb.ps("p_a", [128, 512], F32) for _ in range(2)]
            p_tr = kb.ps("p_tr", [128, 1024], BF16)
            p_s4 = [kb.ps("p_s4", [128, 512], F32) for _ in range(2)]
            pa_i = [0]

            def next_pa():
                pa_i[0] += 1
                return p_a[pa_i[0] % 2]

            def top16(src_ap, vals_out, idx_out):
                n = src_ap.shape[1]
                kb.v(DVE, lambda: nc.vector.max(out=vals_out[:, 0:8], in_=src_ap), r=[s_sb, cand], w=[stop, gval])
                kb.v(DVE, lambda: nc.vector.max_index(out=idx_out[:, 0:8], in_max=vals_out[:, 0:8], in_values=src_ap),
                     r=[s_sb, cand, stop, gval], w=[itop, gpos])
                kb.v(DVE, lambda: nc.vector.match_replace(out=s_wk[:, 0:n], in_to_replace=vals_out[:, 0:8], in_values=src_ap,
                                                          imm_value=-1e30), r=[s_sb, cand, stop, gval], w=[s_wk])
                kb.v(DVE, lambda: nc.vector.max(out=vals_out[:, 8:16], in_=s_wk[:, 0:n]), r=[s_wk], w=[stop, gval])
                kb.v(DVE, lambda: nc.vector.max_index(out=idx_out[:, 8:16], in_max=vals_out[:, 8:16], in_values=s_wk[:, 0:n]),
                     r=[s_wk, stop, gval], w=[itop, gpos])

            for ti in range(NT):
                xt = x_t[ti % 2]
                kb.dma(xt[:], src[ti * 128:(ti + 1) * 128, :], r=[src_b], w=[xt])
                kb.act(x_b[:], xt[:], AF.Copy, r=[xt], w=[x_b])
                for kc in range(8):
                    kb.tr(p_tr[:, kc * 128:(kc + 1) * 128], x_b[:, kc * 128:(kc + 1) * 128], ident_b[:], r=[x_b, ident_b], w=[p_tr])
                kb.cp(DVE, xT[:].rearrange("p a b -> p (a b)"), p_tr[:], r=[p_tr], w=[xT])
                for g4 in range(4):
                    pa = next_pa()
                    for gg in range(4):
                        hp = g4 * 4 + gg
                        for kc in range(8):
                            kb.mm(pa[:, gg * 128:(gg + 1) * 128], wq[:, kc, hp * 128:(hp + 1) * 128], xT[:, kc, :], kc == 0, kc == 7,
                                  r=[wq, xT], w=[pa])
                    kb.act(qT16[:, g4 * 4:(g4 + 1) * 4, :].rearrange("p a b -> p (a b)"), pa[:], AF.Copy, r=[pa], w=[qT16])
                for g4 in range(4):
                    pq = p_s4[g4 % 2]
                    for gg in range(4):
                        hp = g4 * 4 + gg
                        kb.mm(pq[:, gg * 128:(gg + 1) * 128], qT16[:, hp, :], keysT[:, hp, :], True, True, r=[qT16, keysT], w=[pq])
                    kb.act(s_sb[:, g4 * 4:(g4 + 1) * 4, :].rearrange("p a b -> p (a b)"), pq[:], AF.Copy, r=[pq], w=[s_sb])
                for hp in range(16):
                    top16(s_sb[:, hp, :], stop[:, hp, :], itop[:, hp, :])
                kb.cp(DVE, itopf[:], itop[:], r=[itop], w=[itopf])
                for h in range(8):
                    kb.tt(DVE, cand[:, h, :].rearrange("p (a b) -> p a b", a=16),
                          stop[:, 2 * h, :].unsqueeze(2).to_broadcast([128, 16, 16]),
                          stop[:, 2 * h + 1, :].unsqueeze(1).to_broadcast([128, 16, 16]), ALU.add, r=[stop], w=[cand])
                for h in range(8):
                    top16(cand[:, h, :], gval[:, h, :], gpos[:, h, :])
                gp = gpos[:].rearrange("p a b -> p (a b)")
                kb.v(DVE, lambda: nc.vector.tensor_single_scalar(out=ai[:], in_=gp, scalar=4, op=ALU.logical_shift_right), r=[gpos], w=[ai])
                kb.cp(DVE, af[:, 0, :], ai[:], r=[ai], w=[af])
                kb.v(DVE, lambda: nc.vector.tensor_single_scalar(out=ai[:], in_=gp, scalar=15, op=ALU.bitwise_and), r=[gpos], w=[ai])
                kb.cp(DVE, af[:, 1, :], ai[:], r=[ai], w=[af])
                for p in range(2):
                    kb.tt(DVE, oh[:], iota_f[:, 0:16].unsqueeze(1).to_broadcast([128, 128, 16]),
                          af[:, p, :].unsqueeze(2).to_broadcast([128, 128, 16]), ALU.is_equal, r=[iota_f, af], w=[oh])
                    ohv = oh[:].rearrange("p (h k) a -> p h k a", h=8)
                    itv = itopf[:].rearrange("p (h q) a -> p h q a", q=2)[:, :, p, :]
                    for h in range(8):
                        kb.tt(DVE, ohv[:, h], ohv[:, h], itv[:, h].unsqueeze(1).to_broadcast([128, 16, 16]), ALU.mult, r=[oh, itopf], w=[oh])
                    kb.v(DVE, lambda p=p: nc.vector.tensor_reduce(out=i01[:, p, :], in_=oh[:], axis=AX.X, op=ALU.add), r=[oh], w=[i01])
                kb.v(DVE, lambda: nc.vector.scalar_tensor_tensor(out=i01[:, 0, :], in0=i01[:, 0, :], scalar=128.0, in1=i01[:, 1, :],
                                                                  op0=ALU.mult, op1=ALU.add), r=[i01], w=[i01])
                kb.cp(DVE, eidx[:], i01[:, 0, :], r=[i01], w=[eidx])
                kb.tt(DVE, gate[:], gval[:], gval[:, :, 0:1].to_broadcast([128, 8, 16]), ALU.subtract, r=[gval], w=[gate])
                kb.act(gate[:].rearrange("p a b -> p (a b)"), gate[:].rearrange("p a b -> p (a b)"), AF.Exp, r=[gate], w=[gate])
                kb.v(DVE, lambda: nc.vector.tensor_reduce(out=gsum[:], in_=gate[:], axis=AX.X, op=ALU.add), r=[gate], w=[gsum])
                kb.v(DVE, lambda: nc.vector.reciprocal(out=gsum[:], in_=gsum[:]), r=[gsum], w=[gsum])
                kb.tt(DVE, gate[:], gate[:], gsum[:].unsqueeze(2).to_broadcast([128, 8, 16]), ALU.mult, r=[gate, gsum], w=[gate])
                for sl in range(128):
                    ub = ubuf[sl % NB]
                    S.emit(POOL, lambda ub=ub, sl=sl: nc.gpsimd.indirect_dma_start(
                        out=ub[:], out_offset=None, in_=ub16[l],
                        in_offset=bass.IndirectOffsetOnAxis(ap=eidx[:, sl:sl + 1], axis=0)), r=[eidx.b, tb_b[("u", l)]], w=[ub.b], dma=True)
                    kb.v(DVE, lambda ub=ub, sl=sl: nc.vector.scalar_tensor_tensor(
                        out=junk[:], in0=ub[:], scalar=1.0, in1=xt[:], op0=ALU.mult, op1=ALU.mult,
                        accum_out=actv[:, sl:sl + 1]), r=[ub, xt], w=[junk, actv])
                kb.tt(DVE, g1[:], actv[:], actv[:], ALU.mult, r=[actv], w=[g1])
                kb.ts(DVE, g1[:], g1[:], 0.044715, ALU.mult, 1.0, ALU.add, r=[g1], w=[g1])
                kb.tt(DVE, g1[:], g1[:], actv[:], ALU.mult, r=[g1, actv], w=[g1])
                kb.act(g2[:], g1[:], AF.Tanh, r=[g1], w=[g2], scale=0.7978845608028654)
                kb.ts(DVE, g2[:], g2[:], 1.0, ALU.add, 0.5, ALU.mult, r=[g2], w=[g2])
                kb.tt(DVE, g2[:], g2[:], actv[:], ALU.mult, r=[g2, actv], w=[g2])
                kb.tt(DVE, hw[:], g2[:], gate[:].rearrange("p a b -> p (a b)"), ALU.mult, r=[g2, gate], w=[hw])
                for sl in range(128):
                    vb = vbuf[sl % NB]
                    S.emit(POOL, lambda vb=vb, sl=sl: nc.gpsimd.indirect_dma_start(
                        out=vb[:], out_offset=None, in_=vb16[l],
                        in_offset=bass.IndirectOffsetOnAxis(ap=eidx[:, sl:sl + 1], axis=0)), r=[eidx.b, tb_b[("v", l)]], w=[vb.b], dma=True)
                    if sl == 0:
                        kb.ts(DVE, y_acc[:], vb[:], hw[:, 0:1], ALU.mult, r=[vb, hw], w=[y_acc])
                    else:
                        kb.v(DVE, lambda vb=vb, sl=sl: nc.vector.scalar_tensor_tensor(
                            out=y_acc[:], in0=vb[:], scalar=hw[:, sl:sl + 1], in1=y_acc[:], op0=ALU.mult, op1=ALU.add),
                            r=[vb, hw, y_acc], w=[y_acc])
                kb.v(DVE, lambda: nc.vector.scalar_tensor_tensor(out=y_acc[:], in0=xt[:], scalar=DN_ALPHA, in1=y_acc[:],
                                                                  op0=ALU.mult, op1=ALU.add), r=[xt, y_acc], w=[y_acc])
                layer_norm_tile(y_acc, lng, lnb, y_acc, (lst, lmv, lrs))
                kb.dma(dst[ti * 128:(ti + 1) * 128, :], y_acc[:], r=[y_acc], w=[dst_b])
        kb.stack = root

    cosT = kb.sb("cosT", [128, NT, 32], F32)
    sinT = kb.sb("sinT", [128, NT, 32], F32)
    kmP = kb.sb("kmP", [128, 8, 16], BF16)
    kmS = kb.sb("kmS", [128, NSEQ_S, 8, 8], BF16)
    idx_all = kb.sb("idx_all", [128, NSEQ_S * 16], I32)
    inv256 = kb.sb("inv256", [128, 1], F32)
    ones_b = kb.sb("ones_b", [128, 128], BF16)
    maskP_b = kb.sb("maskP_b", [128, 128], BF16)
    eye8 = kb.sb("eye8", [8, 8], F32)
    hm01 = kb.sb("hm01", [128, 2], F32)
    KT_d = dint("KT_d", [128, 8, NTOK], BF16)
    V_d = dint("V_d", [NTOK, D], BF16)
    attn_d = dint("attn_d", [NT_S * 128, D], F32)
    attn_dp = dint("attn_dp", [NT_P * 128, D], F32)
    attn_dpb = Buf("attn_dp")
    KT_db, V_db, attn_db = Buf("KT_d"), Buf("V_d"), Buf("attn_d")
    PI = 3.14159265358979

    def setup_moba_consts():
        with ExitStack() as tmpst:
            kb.stack = tmpst
            _setup_moba_consts()
        kb.stack = root

    def _setup_moba_consts():
        invf = kb.sb("invf", [128, 32], F32)
        posc = kb.sb("posc", [128, 2], F32)
        ang = kb.sb("ang", [128, 2, 32], F32)
        angq = kb.sb("angq", [128, 64], F32)
        angi = kb.sb("angi", [128, 64], I32)
        idx_f = kb.sb("idx_f", [128, NSEQ_S * 16], F32)
        kb.v(DVE, lambda: nc.vector.memset(inv256[:], 1.0 / 256.0), w=[inv256])
        kb.v(DVE, lambda: nc.vector.memset(ones_b[:], 1.0), w=[ones_b])
        kb.cp(DVE, maskP_b[:], maskP[:], r=[maskP], w=[maskP_b])
        kb.cp(DVE, eye8[:], ident_f[0:8, 0:8], r=[ident_f], w=[eye8])
        kb.ts(DVE, hm01[:, 1:2], iota_p[:], 64.0, ALU.is_ge, r=[iota_p], w=[hm01])
        kb.ts(DVE, hm01[:, 0:1], hm01[:, 1:2], -1.0, ALU.mult, 1.0, ALU.add, r=[hm01], w=[hm01])
        kb.act(invf[:], iota_f[:, 0:32], AF.Exp, r=[iota_f], w=[invf], scale=-float(np.log(10000.0)) / 32.0)
        kb.v(DVE, lambda: nc.vector.scalar_tensor_tensor(out=posc[:, 1:2], in0=pdiv[:], scalar=-8.0, in1=iota_p[:],
                                                          op0=ALU.mult, op1=ALU.add), r=[pdiv, iota_p], w=[posc])
        kb.ts(DVE, posc[:, 1:2], posc[:, 1:2], 2048.0, ALU.add, r=[posc], w=[posc])
        for ti in range(NT):
            if ti < NT_P:
                kb.ts(DVE, posc[:, 0:1], iota_p[:], float(ti * 128), ALU.add, r=[iota_p], w=[posc])
                pc = posc[:, 0:1]
            else:
                pc = posc[:, 1:2]
            kb.ts(DVE, ang[:, 0, :], invf[:], pc, ALU.mult, r=[invf, posc], w=[ang])
            kb.ts(DVE, ang[:, 1, :], ang[:, 0, :], 0.5 * PI, ALU.add, r=[ang], w=[ang])
            A2 = ang[:].rearrange("p a b -> p (a b)")
            kb.ts(DVE, angq[:], A2, 1.0 / (2.0 * PI), ALU.mult, r=[ang], w=[angq])
            kb.cp(DVE, angi[:], angq[:], r=[angq], w=[angi])
            kb.cp(DVE, angq[:], angi[:], r=[angi], w=[angq])
            kb.v(DVE, lambda: nc.vector.scalar_tensor_tensor(out=A2, in0=angq[:], scalar=-2.0 * PI, in1=A2, op0=ALU.mult, op1=ALU.add),
                 r=[angq, ang], w=[ang])
            kb.ts(DVE, angq[:], A2, PI, ALU.is_ge, r=[ang], w=[angq])
            kb.v(DVE, lambda: nc.vector.scalar_tensor_tensor(out=A2, in0=angq[:], scalar=-2.0 * PI, in1=A2, op0=ALU.mult, op1=ALU.add),
                 r=[angq, ang], w=[ang])
            kb.ts(DVE, angq[:], A2, -1.0, ALU.mult, PI, ALU.is_ge, r=[ang], w=[angq])
            kb.v(DVE, lambda: nc.vector.scalar_tensor_tensor(out=A2, in0=angq[:], scalar=2.0 * PI, in1=A2, op0=ALU.mult, op1=ALU.add),
                 r=[angq, ang], w=[ang])
            kb.act(sinT[:, ti, :], ang[:, 0, :], AF.Sin, r=[ang], w=[sinT], scale=0.999999)
            kb.act(cosT[:, ti, :], ang[:, 1, :], AF.Sin, r=[ang], w=[cosT], scale=0.999999)
        kb.dma(idx_all[:], page_table.rearrange("s p -> (s p)").rearrange("(o n) -> o n", o=1).to_broadcast([128, NSEQ_S * 16]), w=[idx_all])
        kb.cp(DVE, idx_f[:], idx_all[:], r=[idx_all], w=[idx_f])
        kb.ts(DVE, idx_f[:], idx_f[:], 128.0, ALU.mult, iota_p[:, 0:1], ALU.add, r=[idx_f, iota_p], w=[idx_f])
        kb.cp(DVE, idx_all[:], idx_f[:], r=[idx_f], w=[idx_all])

    def rope_tile(ti, srcf, dstf, tmp):
        sv = srcf[:].rearrange("p (h e d) -> p h e d", h=16, e=2)
        dv = dstf[:].rearrange("p (h e d) -> p h e d", h=16, e=2)
        tv = tmp[:].rearrange("p a (h d) -> p a h d", h=16)
        cb = cosT[:, ti, :].unsqueeze(1).to_broadcast([128, 16, 32])
        sb_ = sinT[:, ti, :].unsqueeze(1).to_broadcast([128, 16, 32])
        kb.tt(DVE, tv[:, 0], sv[:, :, 0, :], cb, ALU.mult, r=[srcf, cosT], w=[tmp])
        kb.tt(DVE, tv[:, 1], sv[:, :, 1, :], sb_, ALU.mult, r=[srcf, sinT], w=[tmp])
        kb.tt(DVE, dv[:, :, 0, :], tv[:, 0], tv[:, 1], ALU.subtract, r=[tmp], w=[dstf])
        kb.tt(POOL, tv[:, 0], sv[:, :, 1, :], cb, ALU.mult, r=[srcf, cosT, dstf], w=[tmp])
        kb.tt(POOL, tv[:, 1], sv[:, :, 0, :], sb_, ALU.mult, r=[srcf, sinT], w=[tmp])
        kb.tt(POOL, dv[:, :, 1, :], tv[:, 0], tv[:, 1], ALU.add, r=[tmp], w=[dstf])

    def load_w_bf16(dst_t, w_ap, ncols, wst, i0=0):
        i = i0
        for kc in range(8):
            for c0 in range(0, ncols, 1024):
                st = wst[i % 2]
                kb.dma(st[:], w_ap[kc * 128:(kc + 1) * 128, c0:c0 + 1024], w=[st])
                kb.cp((DVE, POOL)[i % 2], dst_t[:, kc, c0:c0 + 1024], st[:], r=[st], w=[dst_t])
                i += 1
        return i

    def x_to_xT(xt, x_b, xT, p_tr):
        kb.act(x_b[:], xt[:], AF.Copy, r=[xt], w=[x_b])
        for kc in range(8):
            kb.tr(p_tr[:, kc * 128:(kc + 1) * 128], x_b[:, kc * 128:(kc + 1) * 128], ident_b[:], r=[x_b, ident_b], w=[p_tr])
        kb.cp(DVE, xT[:].rearrange("p a b -> p (a b)"), p_tr[:], r=[p_tr], w=[xT])

    def kv_phase(src, src_b):
        with ExitStack() as ph:
            kb.stack = ph
            wkv = kb.sb("wkv", [128, 8, 2048], BF16)
            wst = [kb.sb("wst", [128, 1024], F32) for _ in range(2)]
            load_w_bf16(wkv, w_kv, 2048, wst)
            x_t = [kb.sb("x_t", [128, D], F32) for _ in range(2)]
            x_b = kb.sb("x_b", [128, D], BF16)
            xT = kb.sb("xT", [128, 8, 128], BF16)
            kf = kb.sb("kf", [128, D], F32)
            kr = kb.sb("kr", [128, D], F32)
            krb = kb.sb("krb", [128, D], BF16)
            vf = kb.sb("vf", [128, D], F32)
            vb = kb.sb("vb", [128, D], BF16)
            ktst = kb.sb("ktst", [128, 8, 128], BF16)
            rtmp = kb.sb("rtmp", [128, 2, 512], F32)
            kpg = [kb.sb("kpg", [128, D], F32) for _ in range(3)]
            p_a = [kb.ps("p_a", [128, 512], F32) for _ in range(2)]
            p_tr = kb.ps("p_tr", [128, 1024], BF16)
            p_km = kb.ps("p_km", [128, 512], F32)
            kb.v(DVE, lambda: nc.vector.memset(p_km[:], 0.0), w=[p_km])
            for ti in range(NT):
                xt = x_t[ti % 2]
                kb.dma(xt[:], src[ti * 128:(ti + 1) * 128, :], r=[src_b], w=[xt])
                x_to_xT(xt, x_b, xT, p_tr)
                for blk in range(4):
                    pa = p_a[blk % 2]
                    for kc in range(8):
                        kb.mm(pa[:], xT[:, kc, :], wkv[:, kc, blk * 512:(blk + 1) * 512], kc == 0, kc == 7, r=[wkv, xT], w=[pa])
                    if blk < 2:
                        kb.act(kf[:, blk * 512:(blk + 1) * 512], pa[:], AF.Copy, r=[pa], w=[kf])
                    else:
                        kb.act(vf[:, (blk - 2) * 512:(blk - 1) * 512], pa[:], AF.Copy, r=[pa], w=[vf])
                rope_tile(ti, kf, kr, rtmp)
                kb.dma(k_rows[ti * 128:(ti + 1) * 128, :], kr[:], r=[kr], w=[out_bufs["k"]])
                kb.dma(v_rows[ti * 128:(ti + 1) * 128, :], vf[:], r=[vf], w=[out_bufs["v"]])
                kb.cp(POOL, vb[:], vf[:], r=[vf], w=[vb])
                kb.dma(V_d[ti * 128:(ti + 1) * 128, :], vb[:], r=[vb], w=[V_db])
                kb.act(krb[:], kr[:], AF.Copy, r=[kr], w=[krb])
                for c in range(8):
                    kb.tr(p_tr[:, c * 128:(c + 1) * 128], krb[:, c * 128:(c + 1) * 128], ident_b[:], r=[krb, ident_b], w=[p_tr])
                kb.cp(DVE, ktst[:].rearrange("p a b -> p (a b)"), p_tr[:], r=[p_tr], w=[ktst])
                kb.dma(KT_d[:, :, ti * 128:(ti + 1) * 128], ktst[:], r=[ktst], w=[KT_db])
                if ti < NT_P:
                    kbk = ti // 2
                    for c in range(8):
                        S.emit(PE, lambda c=c, kbk=kbk: nc.tensor.matmul(p_km[:, c * 16 + kbk:c * 16 + kbk + 1], kr[:, c * 128:(c + 1) * 128],
                                                                         inv256[:], start=False, stop=True, skip_group_check=True),
                               r=[kr.b, inv256.b], w=[p_km.b])
            kb.cp(DVE, kmP[:].rearrange("p a b -> p (a b)"), p_km[:, 0:128], r=[p_km], w=[kmP])
            for sq in range(NSEQ_S):
                kb.v(DVE, lambda: nc.vector.memset(p_km[:, 0:64], 0.0), w=[p_km])
                for pg in range(16):
                    kp = kpg[pg % 3]
                    col = sq * 16 + pg
                    S.emit(POOL, lambda kp=kp, col=col: nc.gpsimd.indirect_dma_start(
                        out=kp[:], out_offset=None, in_=cache_k,
                        in_offset=bass.IndirectOffsetOnAxis(ap=idx_all[:, col:col + 1], axis=0)), r=[idx_all.b], w=[kp.b], dma=True)
                    for c in range(8):
                        S.emit(PE, lambda c=c, pg=pg, kp=kp: nc.tensor.matmul(
                            p_km[:, c * 8 + pg // 2:c * 8 + pg // 2 + 1], kp[:, c * 128:(c + 1) * 128], inv256[:],
                            start=False, stop=True, skip_group_check=True), r=[kp.b, inv256.b], w=[p_km.b])
                kb.cp(DVE, kmS[:, sq, :, :].rearrange("p a b -> p (a b)"), p_km[:, 0:64], r=[p_km], w=[kmS])
        kb.stack = root

    def moba_layer(jl, l, src, src_b, dst, dst_b):
        with ExitStack() as ph:
            kb.stack = ph
            wqb = kb.sb("wqb", [128, 8, D], BF16)
            with ExitStack() as phw:
                kb.stack = phw
                wst = [kb.sb("wst", [128, 1024], F32) for _ in range(2)]
                load_w_bf16(wqb, w_q_b[jl], 1024, wst)
            kb.stack = ph
            wob_box = []
            lng = kb.sb("lng", [128, D], F32)
            lnb = kb.sb("lnb", [128, D], F32)
            kb.dma(lng[:], bc_rows(ln_g[l, 0:1, :], D), w=[lng])
            kb.dma(lnb[:], bc_rows(ln_b[l, 0:1, :], D), w=[lnb])
            x_t = [kb.sb("x_t", [128, D], F32)] * 2
            x_b = kb.sb("x_b", [128, D], BF16)
            xT = kb.sb("xT", [128, 8, 128], BF16)
            qTe = [kb.sb("qTe", [128, 8, 128], BF16) for _ in range(2)]
            rtmp = kb.sb("rtmp", [128, 2, 512], F32)
            attn = kb.sb("attn", [128, D], F32)
            z_t = kb.sb("z_t", [128, D], F32)
            qf = z_t
            qr = attn
            lst = kb.sb("lst", [128, 2, 6], F32)
            lmv = kb.sb("lmv", [128, 2], F32)
            lrs = kb.sb("lrs", [128, 1], F32)
            g0 = kb.sb("g0", [128, 16, 16], F32)
            gw = kb.sb("gw", [128, 16, 16], F32)
            ge = kb.sb("ge", [128, 16, 16], F32)
            gm = kb.sb("gm", [128, 16], F32)
            selb = kb.sb("selb", [128, 16, 16], F32)
            pastm = kb.sb("pastm", [128, 16], F32)
            dg = [kb.sb("dg", [128, 128], BF16) for _ in range(2)]
            PTm = [kb.sb("PTm", [128, 512], BF16) for _ in range(2)]
            rden = kb.sb("rden", [128, 16], F32)
            p_a = [kb.ps("p_a", [128, 512], F32) for _ in range(2)]
            p_tr = kb.ps("p_tr", [128, 1024], BF16)
            p_g = kb.ps("p_g", [128, 512], F32)
            p_sc = [kb.ps("p_sc", [128, 512], F32) for _ in range(2)]
            p_o = [kb.ps("p_o", [128, 512], F32) for _ in range(2)]

            def q_proj(ti, xt):
                x_to_xT(xt, x_b, xT, p_tr)
                for half in range(2):
                    pa = p_a[half]
                    for kc in range(8):
                        kb.mm(pa[:], xT[:, kc, :], wqb[:, kc, half * 512:(half + 1) * 512], kc == 0, kc == 7, r=[wqb, xT], w=[pa])
                    kb.act(qf[:, half * 512:(half + 1) * 512], pa[:], AF.Copy, r=[pa], w=[qf])
                rope_tile(ti, qf, qr, rtmp)
                kb.act(x_b[:], qr[:], AF.Copy, r=[qr], w=[x_b])
                for c in range(8):
                    kb.tr(p_tr[:, c * 128:(c + 1) * 128], x_b[:, c * 128:(c + 1) * 128], ident_b[:], r=[x_b, ident_b], w=[p_tr])
                for e in range(2):
                    kb.ts(DVE, qTe[e][:].rearrange("p a b -> p (a b)"), p_tr[:], hm01[:, e:e + 1], ALU.mult, r=[p_tr, hm01], w=[qTe[e]])

            def select_blocks(nq, nkb, own, km_of_head):
                for h in range(16):
                    e, hp = h % 2, h // 2
                    kb.mm(p_g[0:nq, h * 16:h * 16 + nkb], qTe[e][:, hp, 0:nq] if nq == 128 else qcols(e, hp),
                          km_of_head(e, hp), True, True, r=[qTe[0], qTe[1], kmP, kmS], w=[p_g])
                kb.v(DVE, lambda: nc.vector.memset(g0[0:nq], -1e9), w=[g0])
                npast = min(own, nkb)
                if npast > 0:
                    kb.cp(DVE, g0[0:nq, :, 0:npast], p_g[0:nq, 0:256].rearrange("p (h k) -> p h k", h=16)[:, :, 0:npast], r=[p_g], w=[g0])
                kb.cp(DVE, gw[0:nq], g0[0:nq], r=[g0], w=[gw])
                for rnd in range(3):
                    kb.v(DVE, lambda: nc.vector.tensor_reduce(out=gm[0:nq], in_=gw[0:nq], axis=AX.X, op=ALU.max), r=[gw], w=[gm])
                    if rnd < 2:
                        kb.tt(DVE, ge[0:nq], gw[0:nq], gm[0:nq].unsqueeze(2).to_broadcast([nq, 16, 16]), ALU.is_equal, r=[gw, gm], w=[ge])
                        kb.v(DVE, lambda: nc.vector.scalar_tensor_tensor(out=gw[0:nq], in0=ge[0:nq], scalar=-1e9, in1=gw[0:nq],
                                                                          op0=ALU.mult, op1=ALU.add), r=[ge, gw], w=[gw])
                kb.tt(DVE, ge[0:nq], g0[0:nq], gm[0:nq].unsqueeze(2).to_broadcast([nq, 16, 16]), ALU.is_ge, r=[g0, gm], w=[ge])
                kb.ts(DVE, selb[0:nq], ge[0:nq], -1.0, ALU.add, -NEG, ALU.mult, r=[ge], w=[selb])
                kb.ts(DVE, pastm[0:nq], iota_f[0:nq, 0:16], float(npast), ALU.is_ge, r=[iota_f], w=[pastm])
                kb.ts(DVE, pastm[0:nq], pastm[0:nq], -1.0, ALU.mult, 1.0, ALU.add, r=[pastm], w=[pastm])
                kb.tt(DVE, selb[0:nq], selb[0:nq], pastm[0:nq].unsqueeze(1).to_broadcast([nq, 16, 16]), ALU.mult, r=[selb, pastm], w=[selb])

            qcols_state = {}

            def qcols(e, hp):
                j = qcols_state["j"]
                return qTe[e][:, hp, 8 * j:8 * j + 8]

            def out_proj(ti, xt, attn_src):
                kb.act(x_b[:], attn_src[:], AF.Copy, r=[attn_src], w=[x_b])
                for c in range(8):
                    kb.tr(p_tr[:, c * 128:(c + 1) * 128], x_b[:, c * 128:(c + 1) * 128], ident_b[:], r=[x_b, ident_b], w=[p_tr])
                kb.cp(DVE, xT[:].rearrange("p a b -> p (a b)"), p_tr[:], r=[p_tr], w=[xT])
                for half in range(2):
                    pa = p_a[half]
                    for kc in range(8):
                        kb.mm(pa[:], xT[:, kc, :], wob_box[0][:, kc, half * 512:(half + 1) * 512], kc == 0, kc == 7, r=[xT, wob_box[0]], w=[pa])
                    kb.v(DVE, lambda half=half, pa=pa: nc.vector.scalar_tensor_tensor(
                        out=z_t[:, half * 512:(half + 1) * 512], in0=xt[:, half * 512:(half + 1) * 512], scalar=DN_ALPHA,
                        in1=pa[:], op0=ALU.mult, op1=ALU.add), r=[xt, pa], w=[z_t])
                layer_norm_tile(z_t, lng, lnb, z_t, (lst, lmv, lrs))
                kb.dma(dst[ti * 128:(ti + 1) * 128, :], z_t[:], r=[z_t], w=[dst_b])

            with ExitStack() as ph2:
                kb.stack = ph2
                KT = kb.sb("KT", [128, 8, NT_P * 128], BF16)
                Vaug = kb.sb("Vaug", [128, NT_P, 16, 64], BF16)
                for c in range(8):
                    kb.dma(KT[:, c, :], KT_d[:, c, 0:NT_P * 128], r=[KT_db], w=[KT])
                for t in range(NT_P):
                    kb.dma(Vaug[:, t, :, 0:64], V_d[t * 128:(t + 1) * 128, :].rearrange("p (h d) -> p h d", h=16), r=[V_db], w=[Vaug])
                for qt in range(DBG["np_tiles"]):
                    xt = x_t[qt % 2]
                    kb.dma(xt[:], src[qt * 128:(qt + 1) * 128, :], r=[src_b], w=[xt])
                    q_proj(qt, xt)
                    own = qt // 2
                    if DBG["select"]:
                        select_blocks(128, 16, own, lambda e, hp: kmP[:, hp, :])
                    vi = 0
                    for h in range(DBG["heads"]):
                        e, hp = h % 2, h // 2
                        if e == 1 and not DBG["e1"]:
                            continue
                        po = p_o[h % 2]
                        kb.v(DVE, lambda po=po: nc.vector.memset(po[:, 0:65], 0.0), w=[po])
                        for kc0 in range(0, qt + 1, 4):
                            ncz = min(4, qt + 1 - kc0)
                            psc = p_sc[vi % 2]
                            ptm = PTm[vi % 2]
                            for ci in range(ncz):
                                kc = kc0 + ci
                                kbk = kc // 2
                                need_sel = kbk < own
                                need_caus = kc == qt
                                reg = psc[:, ci * 128:(ci + 1) * 128]
                                kb.mm(reg, KT[:, hp, kc * 128:(kc + 1) * 128], qTe[e][:, hp, :],
                                      True, not (need_sel or need_caus), r=[KT, qTe[e]], w=[psc])
                                if need_sel:
                                    d_ = dg[(kc // 2) % 2]
                                    if kc % 2 == 0:
                                        kb.ts(DVE, d_[:], ident_f[:], selb[:, h, kbk:kbk + 1], ALU.mult, r=[ident_f, selb], w=[d_])
                                    kb.mm(reg, ones_b[:], d_[:], False, True, r=[ones_b, d_], w=[psc])
                                if need_caus:
                                    kb.mm(reg, ident_b[:], maskP_b[:], False, True, r=[ident_b, maskP_b], w=[psc])
                            kb.act(ptm[:, 0:ncz * 128], psc[:, 0:ncz * 128], AF.Exp, r=[psc], w=[ptm], scale=0.125)
                            for ci in range(ncz):
                                kc = kc0 + ci
                                S.emit(PE, lambda po=po, ptm=ptm, kc=kc, h=h, ci=ci: nc.tensor.matmul(
                                    po[:, 0:64], ptm[:, ci * 128:(ci + 1) * 128], Vaug[:, kc, h, :], start=False, stop=True,
                                    skip_group_check=True), r=[ptm.b, Vaug.b], w=[po.b])
                                S.emit(PE, lambda po=po, ptm=ptm, ci=ci: nc.tensor.matmul(
                                    po[:, 64:65], ptm[:, ci * 128:(ci + 1) * 128], ones_b[:, 0:1], start=False, stop=True,
                                    skip_group_check=True), r=[ptm.b, ones_b.b], w=[po.b])
                            vi += 1
                        kb.v(DVE, lambda po=po, h=h: nc.vector.reciprocal(out=rden[:, h:h + 1], in_=po[:, 64:65]), r=[po], w=[rden])
                        kb.ts(DVE, attn[:, h * 64:(h + 1) * 64], po[:, 0:64], rden[:, h:h + 1], ALU.mult, r=[po, rden], w=[attn])
                    kb.dma(attn_dp[qt * 128:(qt + 1) * 128, :], attn[:], r=[attn], w=[attn_dpb])
            with ExitStack() as ph2:
                kb.stack = ph2
                wob = kb.sb("wob", [128, 8, D], BF16)
                wob_box.append(wob)
                with ExitStack() as phw:
                    kb.stack = phw
                    wst = [kb.sb("wst", [128, 1024], F32) for _ in range(2)]
                    load_w_bf16(wob, w_out_b[jl], 1024, wst)
                kb.stack = ph2
                for qt in range(NT_P):
                    xt = x_t[0]
                    kb.dma(xt[:], src[qt * 128:(qt + 1) * 128, :], r=[src_b], w=[xt])
                    kb.dma(attn[:], attn_dp[qt * 128:(qt + 1) * 128, :], r=[attn_dpb], w=[attn])
                    out_proj(qt, xt, attn)
                KTn = kb.sb("KTn", [128, 8, 128], BF16)
                Vn = kb.sb("Vn", [8, 16, 65], BF16)
                kpg = [kb.sb("kpg", [128, D], F32) for _ in range(2)]
                vpg = [kb.sb("vpg", [128, D], F32) for _ in range(2)]
                vpb = kb.sb("vpb", [128, 16, 65], BF16)
                KTs = kb.sb("KTs", [128, 8, 128], BF16)
                rsel = kb.sb("rsel", [8, 9, 16, 8], BF16)
                o_s = kb.sb("o_s", [8, D], F32)
                dn_s = kb.sb("dn_s", [8, 16], F32)
                p_t4 = [p_a[0]]
                kb.v(DVE, lambda: nc.vector.memset(vpb[:, :, 64:65], 1.0), w=[vpb])
                kb.v(DVE, lambda: nc.vector.memset(Vn[:, :, 64:65], 1.0), w=[Vn])
                kb.v(DVE, lambda: nc.vector.memset(rsel[:], 0.0), w=[rsel])
                p_oa, p_ob, p_od = p_o[0], p_o[1], p_g
                for ts_ in range(DBG["ns_tiles"]):
                    ti = NT_P + ts_
                    xt = x_t[ti % 2]
                    kb.dma(xt[:], src[ti * 128:(ti + 1) * 128, :], r=[src_b], w=[xt])
                    q_proj(ti, xt)
                    kb.dma(KTn[:], KT_d[:, :, ti * 128:(ti + 1) * 128], r=[KT_db], w=[KTn])
                    for j in range(DBG["ns_seq"]):
                        sq = ts_ * 16 + j
                        qcols_state["j"] = j
                        kb.dma(Vn[:, :, 0:64], V_d[ti * 128 + 8 * j: ti * 128 + 8 * j + 8, :].rearrange("p (h d) -> p h d", h=16),
                               r=[V_db], w=[Vn])
                        select_blocks(8, 8, 8, lambda e, hp: kmS[:, sq, hp, :])
                        for kbk in range(8):
                            kb.tt(DVE, rsel[:, kbk, :, :], selb[0:8, :, kbk:kbk + 1].to_broadcast([8, 16, 8]),
                                  eye8[:].unsqueeze(1).to_broadcast([8, 16, 8]), ALU.mult, r=[selb, eye8], w=[rsel])
                        kb.v(DVE, lambda: nc.vector.memset(p_oa[0:8, :], 0.0), w=[p_oa])
                        kb.v(DVE, lambda: nc.vector.memset(p_ob[0:8, :], 0.0), w=[p_ob])
                        kb.v(DVE, lambda: nc.vector.memset(p_od[0:8, 256:272], 0.0), w=[p_od])
                        for pg in range(17):
                            psc = p_sc[pg % 2]
                            ptm = PTm[pg % 2]
                            if pg < 16:
                                nk = 128
                                kp, vp = kpg[pg % 2], vpg[pg % 2]
                                col = sq * 16 + pg
                                S.emit(POOL, lambda kp=kp, col=col: nc.gpsimd.indirect_dma_start(
                                    out=kp[:], out_offset=None, in_=cache_k,
                                    in_offset=bass.IndirectOffsetOnAxis(ap=idx_all[:, col:col + 1], axis=0)), r=[idx_all.b], w=[kp.b], dma=True)
                                S.emit(POOL, lambda vp=vp, col=col: nc.gpsimd.indirect_dma_start(
                                    out=vp[:], out_offset=None, in_=cache_v,
                                    in_offset=bass.IndirectOffsetOnAxis(ap=idx_all[:, col:col + 1], axis=0)), r=[idx_all.b], w=[vp.b], dma=True)
                                for half in range(2):
                                    pt = p_t4[0]
                                    for c4 in range(4):
                                        c = half * 4 + c4
                                        kb.tr(pt[:, c4 * 128:(c4 + 1) * 128], kp[:, c * 128:(c + 1) * 128], ident_f[:], r=[kp, ident_f], w=[pt])
                                    kb.act(KTs[:, half * 4:(half + 1) * 4, :].rearrange("p a b -> p (a b)"), pt[:], AF.Copy, r=[pt], w=[KTs])
                                kb.cp(POOL, vpb[:, :, 0:64], vp[:].rearrange("p (h d) -> p h d", h=16), r=[vp], w=[vpb])
                                ksrc = lambda e, hp: KTs[:, hp, :]
                                vsrc = lambda h: vpb[:, h, 0:64]
                                onesrc = vpb[:, 0, 64:65]
                                ksb, vsb = KTs, vpb
                            else:
                                nk = 8
                                ksrc = lambda e, hp: KTn[:, hp, 8 * j:8 * j + 8]
                                vsrc = lambda h: Vn[:, h, 0:64]
                                onesrc = Vn[:, 0, 64:65]
                                ksb, vsb = KTn, Vn
                            kb.v(DVE, lambda psc=psc: nc.vector.memset(psc[:, 0:128], 0.0), w=[psc])
                            for h in range(16):
                                e, hp = h % 2, h // 2
                                S.emit(PE, lambda h=h, e=e, hp=hp, ksrc=ksrc, psc=psc, nk=nk: nc.tensor.matmul(
                                    psc[0:nk, h * 8:(h + 1) * 8], ksrc(e, hp), qcols(e, hp), start=False, stop=True,
                                    skip_group_check=True), r=[ksb.b, qTe[0].b, qTe[1].b], w=[psc.b])
                            if pg < 16:
                                S.emit(PE, lambda psc=psc, pg=pg: nc.tensor.matmul(
                                    psc[:, 0:128], ones_b[0:8, :], rsel[:, pg // 2, :, :].rearrange("p a b -> p (a b)"), start=False, stop=True,
                                    skip_group_check=True), r=[ones_b.b, rsel.b], w=[psc.b])
                            else:
                                S.emit(PE, lambda psc=psc: nc.tensor.matmul(
                                    psc[0:8, 0:128], ident_b[0:8, 0:8], caus8[:].rearrange("p a b -> p (a b)"), start=False, stop=True,
                                    skip_group_check=True), r=[ident_b.b, caus8.b], w=[psc.b])
                            kb.act(ptm[0:nk, 0:128], psc[0:nk, 0:128], AF.Exp, r=[psc], w=[ptm], scale=0.125)
                            for h in range(16):
                                po = p_oa if h < 8 else p_ob
                                S.emit(PE, lambda h=h, po=po, ptm=ptm, vsrc=vsrc, nk=nk: nc.tensor.matmul(
                                    po[0:8, (h % 8) * 64:(h % 8 + 1) * 64], ptm[0:nk, h * 8:(h + 1) * 8], vsrc(h), start=False, stop=True,
                                    skip_group_check=True), r=[ptm.b, vsb.b], w=[po.b])
                                S.emit(PE, lambda h=h, ptm=ptm, onesrc=onesrc, nk=nk: nc.tensor.matmul(
                                    p_od[0:8, 256 + h:257 + h], ptm[0:nk, h * 8:(h + 1) * 8], onesrc, start=False, stop=True,
                                    skip_group_check=True), r=[ptm.b, vsb.b], w=[p_od.b])
                        kb.v(DVE, lambda: nc.vector.reciprocal(out=dn_s[:], in_=p_od[0:8, 256:272]), r=[p_od], w=[dn_s])
                        for hh in range(2):
                            po = p_oa if hh == 0 else p_ob
                            kb.tt(DVE, o_s[:, hh * 512:(hh + 1) * 512].rearrange("p (h d) -> p h d", h=8),
                                  po[0:8, :].rearrange("p (h d) -> p h d", h=8),
                                  dn_s[:, hh * 8:(hh + 1) * 8].unsqueeze(2).to_broadcast([8, 8, 64]), ALU.mult, r=[po, dn_s], w=[o_s])
                        kb.dma(attn_d[ts_ * 128 + 8 * j: ts_ * 128 + 8 * j + 8, :], o_s[:], r=[o_s], w=[attn_db])
                    kb.dma(attn[:], attn_d[ts_ * 128:(ts_ + 1) * 128, :], r=[attn_db], w=[attn])
                    out_proj(ti, xt, attn)
        kb.stack = root

    caus8 = kb.sb("caus8", [8, 16, 8], BF16)

    cur, cur_b = x_in, Buf("x_in")
    nxt = 0
    if any(p[0] == "P" for p in phases):
        convert_tables()
    for phs in phases:
        if phs[0] == "A":
            mlstm_layer(int(phs[1]), cur, cur_b, xs[nxt], xs_b[nxt])
        elif phs[0] == "P":
            peer_layer(int(phs[1]), cur, cur_b, xs[nxt], xs_b[nxt])
        elif phs == "KV":
            setup_moba_consts()
            kb.cp(DVE, caus8[:], maskP[0:8, 0:8].unsqueeze(1).to_broadcast([8, 16, 8]), r=[maskP], w=[caus8])
            kv_phase(cur, cur_b)
            continue
        elif phs[0] == "B":
            moba_layer(int(phs[1]) - 2, int(phs[1]), cur, cur_b, xs[nxt], xs_b[nxt])
        cur, cur_b = xs[nxt], xs_b[nxt]
        nxt = 1 - nxt
    with ExitStack() as ph:
        kb.stack = ph
        cpb = [kb.sb("cpb", [128, D], F32) for _ in range(2)]
        for ti in range(NT):
            t = cpb[ti % 2]
            kb.dma(t[:], cur[ti * 128:(ti + 1) * 128, :], r=[cur_b], w=[t])
            kb.dma(y_out[ti * 128:(ti + 1) * 128, :], t[:], r=[t], w=[out_bufs["y"]])
    S.finish(list(out_bufs.values()) + xs_b + [KT_db, V_db, attn_db, attn_dpb])
    root.close()
    print("instr counts:", [(e.name, e.n_inst, e.n_wait) for e in S.engs])
    return nc


def make_in_maps(inp):
    maps = []
    for c in range(8):
        b = c % 4
        x = np.concatenate([inp["x_prompt"][b], inp["x_sample"][32 * b:32 * b + 32].reshape(256, D)], axis=0)
        m = {
            "x_in": np.ascontiguousarray(x),
            "stC_in": np.ascontiguousarray(inp["state_C"][:, 32 * b:32 * b + 32]),
            "stn_in": np.ascontiguousarray(inp["state_n"][:, 32 * b:32 * b + 32]),
            "stm_in": np.ascontiguousarray(inp["state_m"][:, 32 * b:32 * b + 32]),
            "w_in_a": inp["w_in_a"], "b_gates_a": inp["b_gates_a"], "norm_a": inp["norm_a"],
            "w_out_a": inp["w_out_a"], "ln_g": inp["ln_g"], "ln_b": inp["ln_b"],
            "peer_wq": inp["peer_wq"],
            "w_kv": inp["w_kv"], "w_q_b": inp["w_q_b"], "w_out_b": inp["w_out_b"],
            "cache_k": inp["cache_k"].reshape(2560 * 128, D), "cache_v": inp["cache_v"].reshape(2560 * 128, D),
            "page_table": np.ascontiguousarray(inp["page_table"][32 * b:32 * b + 32]).astype(np.int32),
            "peer_keysT": np.ascontiguousarray(inp["peer_keys"].reshape(4, 16, 128, 128).transpose(0, 1, 3, 2)),
        }
        for i in range(4):
            m[f"peer_u{i}"] = inp["peer_u"][i]
            m[f"peer_v{i}"] = inp["peer_v"][i]
        maps.append(m)
    return maps


DBG = {"np_tiles": NT_P, "ns_tiles": NT_S, "ns_seq": 16, "heads": 16, "select": True, "e1": True}
FULL_PHASES = ("A0", "P0", "A1", "P1", "KV", "B2", "P2", "B3", "P3")


def kernel(**inp):
    inp = {k: np.asarray(v) for k, v in inp.items()}
    nc = build_program(phases=FULL_PHASES)
    res = run_bass_kernel_spmd(nc, make_in_maps(inp), core_ids=list(range(8)))
    R = res.results
    f32 = np.float32
    y_p = np.stack([R[b]["y_out"][:4096] for b in range(4)]).astype(f32)
    y_s = np.concatenate([R[b]["y_out"][4096:].reshape(32, 8, D) for b in range(4)]).astype(f32)
    C_p = np.stack([R[b]["C_out"][:, 0] for b in range(4)], axis=1).astype(f32)
    n_p = np.stack([R[b]["n_out"][:, 0] for b in range(4)], axis=1).astype(f32)
    m_p = np.stack([R[b]["m_out"][:, 0] for b in range(4)], axis=1).astype(f32)
    C_s = np.concatenate([R[b]["C_out"][:, 1:] for b in range(4)], axis=1).astype(f32)
    n_s = np.concatenate([R[b]["n_out"][:, 1:] for b in range(4)], axis=1).astype(f32)
    m_s = np.concatenate([R[b]["m_out"][:, 1:] for b in range(4)], axis=1).astype(f32)
    k_p = np.stack([R[b]["k_rows"][:4096].reshape(4096, 16, 64) for b in range(4)]).astype(f32)
    v_p = np.stack([R[b]["v_rows"][:4096].reshape(4096, 16, 64) for b in range(4)]).astype(f32)
    k_s = np.concatenate([R[b]["k_rows"][4096:].reshape(32, 8, 16, 64) for b in range(4)]).astype(f32)
    v_s = np.concatenate([R[b]["v_rows"][4096:].reshape(32, 8, 16, 64) for b in range(4)]).astype(f32)
    return (y_p, y_s, C_p, n_p, m_p, k_p, v_p, C_s, n_s, m_s, k_s, v_s)
```

```python
from contextlib import ExitStack
import numpy as np
import concourse.bass as bass
import concourse.mybir as mybir
from concourse.bass_utils import run_bass_kernel_spmd

F32 = mybir.dt.float32
BF16 = mybir.dt.bfloat16
I32 = mybir.dt.int32
U32 = mybir.dt.uint32
AF = mybir.ActivationFunctionType
ALU = mybir.AluOpType
AX = mybir.AxisListType

D = 1024
NT_P = 32
NT_S = 2
NT = NT_P + NT_S
NTOK = NT * 128
NSEQ_S = 32
NSEQ = 1 + NSEQ_S
LN_EPS = 1e-5
DN_ALPHA = (2.0 * 4) ** 0.25
NEG = -30000.0


class Buf:
    __slots__ = ("name", "w", "rs")

    def __init__(self, name=""):
        self.name = name
        self.w = None
        self.rs = {}


class Eng:
    SEM_LIMIT = 15000

    def __init__(self, S, name, handle, npool=10, self_sync=True):
        self.S = S
        self.name = name
        self.h = handle
        self.self_sync = self_sync
        self.waited = {}
        self.count = 0
        self.sem = None
        self.semkey = None
        self.nsem = 0
        self.pool = []
        self.pool_i = 0
        self.npool = npool
        self.n_inst = 0
        self.n_wait = 0

    def _new_sem(self):
        self.nsem += 1
        self.sem = self.S.nc.alloc_semaphore(name=f"s_{self.name}_{self.nsem}")
        self.semkey = f"{self.name}_{self.nsem}"
        self.count = 0

    def wait_tok(self, tok):
        semkey, sem, val = tok
        if self.waited.get(semkey, 0) >= val:
            return
        self.h.wait_ge(sem, val)
        self.n_wait += 1
        self.waited[semkey] = val


class Sched:
    def __init__(self, nc):
        self.nc = nc
        self.pe = Eng(self, "pe", nc.tensor, self_sync=False)
        self.dve = Eng(self, "dve", nc.vector)
        self.act = Eng(self, "act", nc.scalar)
        self.pool = Eng(self, "pool", nc.gpsimd, npool=48)
        self.sp = Eng(self, "sp", nc.sync)
        self.engs = [self.pe, self.dve, self.act, self.pool, self.sp]

    defer = None

    def flush(self, lst, n):
        k = 0
        while lst and k < n:
            eng, fn, r, w, dma = lst.pop(0)
            self.emit(eng, fn, r, w, dma)
            k += 1

    def emit(self, eng, fn, r=(), w=(), dma=False):
        if self.defer is not None:
            self.defer.append((eng, fn, list(r), list(w), dma))
            return None
        toks = {}

        def add(tok):
            k = tok[0]
            if k not in toks or toks[k][2] < tok[2]:
                toks[k] = tok
        for b in r:
            if b.w is not None:
                add(b.w)
        for b in w:
            if b.w is not None:
                add(b.w)
            for k, (sem, val) in b.rs.items():
                add((k, sem, val))
        for k, tok in toks.items():
            if (not dma) and (not eng.self_sync) and k == eng.semkey:
                continue
            eng.wait_tok(tok)
        if dma:
            if len(eng.pool) < eng.npool:
                nm = f"d_{eng.name}_{len(eng.pool)}"
                ent = [nm, self.nc.alloc_semaphore(name=nm), 0]
                eng.pool.append(ent)
            else:
                ent = eng.pool[eng.pool_i % eng.npool]
            eng.pool_i += 1
            if ent[2] > 0:
                eng.wait_tok((ent[0], ent[1], ent[2]))
            inst = fn()
            ent[2] += 16
            inst.then_inc(ent[1], 16)
            tok = (ent[0], ent[1], ent[2])
        else:
            if eng.sem is None or eng.count >= Eng.SEM_LIMIT:
                eng._new_sem()
            inst = fn()
            eng.count += 1
            inst.then_inc(eng.sem, 1)
            tok = (eng.semkey, eng.sem, eng.count)
        eng.n_inst += 1
        k = tok[0]
        for b in r:
            if k not in b.rs or b.rs[k][1] < tok[2]:
                b.rs[k] = (tok[1], tok[2])
        for b in w:
            b.w = tok
            b.rs = {}
        return tok

    def finish(self, bufs):
        for b in bufs:
            if b.w is not None:
                self.sp.wait_tok(b.w)


class T:
    def __init__(self, h, name, psum=False):
        self.h = h
        self.b = Buf(name)
        self.b_psum = psum

    def __getitem__(self, idx):
        return self.h[idx]


class K:
    def __init__(self, nc):
        self.nc = nc
        self.S = Sched(nc)
        self.stack = None
        self.uid = 0

    def sb(self, name, shape, dt, stack=None):
        self.uid += 1
        st = stack if stack is not None else self.stack
        h = st.enter_context(self.nc.sbuf_tensor(f"{name}_{self.uid}", list(shape), dt))
        return T(h, name)

    def ps(self, name, shape, dt, stack=None):
        self.uid += 1
        st = stack if stack is not None else self.stack
        h = st.enter_context(self.nc.psum_tensor(f"{name}_{self.uid}", list(shape), dt))
        return T(h, name, psum=True)

    def _rw(self, r, w):
        rb, wb = [], []
        for x in r:
            if isinstance(x, T):
                (wb if x.b_psum else rb).append(x.b)
            else:
                rb.append(x)
        for x in w:
            wb.append(x.b if isinstance(x, T) else x)
        return dict(r=rb, w=wb)

    def dma(self, out, in_, r=(), w=(), q=None, slow=False):
        S = self.S
        eng = q or S.sp
        kw = dict(allow_slow_non_contiguous=True) if slow else {}
        return S.emit(eng, lambda: eng.h.dma_start(out=out, in_=in_, **kw), **self._rw(r, w), dma=True)

    def mm(self, out, lhsT, rhs, start, stop, r=(), w=()):
        nc = self.nc
        return self.S.emit(self.S.pe, lambda: nc.tensor.matmul(out, lhsT, rhs, start=start, stop=stop),
                           **self._rw(r, w))

    def tr(self, out, in_, ident, r=(), w=()):
        nc = self.nc
        return self.S.emit(self.S.pe, lambda: nc.tensor.transpose(out, in_, ident), **self._rw(r, w))

    def act(self, out, in_, func, r=(), w=(), bias=None, scale=None):
        nc = self.nc
        kw = {}
        if bias is not None:
            kw["bias"] = bias
        if scale is not None:
            kw["scale"] = scale
        return self.S.emit(self.S.act, lambda: nc.scalar.activation(out=out, in_=in_, func=func, **kw),
                           **self._rw(r, w))

    def v(self, eng, fn, r=(), w=()):
        return self.S.emit(eng, fn, **self._rw(r, w))

    def ts(self, eng, out, in0, s1, op0, s2=None, op1=None, r=(), w=()):
        kw = dict(out=out, in0=in0, scalar1=s1, scalar2=s2, op0=op0)
        if op1 is not None:
            kw["op1"] = op1
        return self.S.emit(eng, lambda: eng.h.tensor_scalar(**kw), **self._rw(r, w))

    def tt(self, eng, out, in0, in1, op, r=(), w=()):
        return self.S.emit(eng, lambda: eng.h.tensor_tensor(out=out, in0=in0, in1=in1, op=op),
                           **self._rw(r, w))

    def cp(self, eng, out, in_, r=(), w=()):
        return self.S.emit(eng, lambda: eng.h.tensor_copy(out=out, in_=in_), **self._rw(r, w))


def bc_rows(dram_ap_row, n):
    return dram_ap_row.to_broadcast([128, n])


def build_program(phases=("A0",), debug=False):
    nc = bass.Bass("TRN2", target_bir_lowering=False)
    kb = K(nc)
    S = kb.S
    DVE, ACT, POOL, PE = S.dve, S.act, S.pool, S.pe

    def din(name, shape, dt=F32):
        return nc.dram_tensor(name, list(shape), dt, kind="ExternalInput").ap()

    def dout(name, shape, dt=F32):
        return nc.dram_tensor(name, list(shape), dt, kind="ExternalOutput").ap()

    def dint(name, shape, dt=F32):
        return nc.dram_tensor(name, list(shape), dt, kind="Internal").ap()

    x_in = din("x_in", [NTOK, D])
    stC_in = din("stC_in", [2, NSEQ_S, 4, 256, 256])
    stn_in = din("stn_in", [2, NSEQ_S, 4, 256])
    stm_in = din("stm_in", [2, NSEQ_S, 4])
    w_in_a = din("w_in_a", [2, D, 4104])
    b_gates_a = din("b_gates_a", [2, 8])
    norm_a = din("norm_a", [2, D])
    w_out_a = din("w_out_a", [2, D, D])
    ln_g = din("ln_g", [4, 2, D])
    ln_b = din("ln_b", [4, 2, D])
    w_kv = din("w_kv", [D, 2048])
    w_q_b = din("w_q_b", [2, D, D])
    w_out_b = din("w_out_b", [2, D, D])
    cache_k = din("cache_k", [2560 * 128, D])
    cache_v = din("cache_v", [2560 * 128, D])
    page_table = din("page_table", [NSEQ_S, 16], I32)
    k_rows = dout("k_rows", [NTOK, D])
    v_rows = dout("v_rows", [NTOK, D])
    peer_wq = din("peer_wq", [4, D, 2048])
    peer_keysT = din("peer_keysT", [4, 16, 128, 128])
    peer_u = [din(f"peer_u{i}", [16384, D]) for i in range(4)]
    peer_v = [din(f"peer_v{i}", [16384, D]) for i in range(4)]

    y_out = dout("y_out", [NTOK, D])
    C_out = dout("C_out", [2, NSEQ, 4, 256, 256])
    n_out = dout("n_out", [2, NSEQ, 4, 256])
    m_out = dout("m_out", [2, NSEQ, 4])
    out_bufs = {k: Buf(k) for k in ("y", "C", "n", "m", "k", "v")}

    xs = [dint("xs0", [NTOK, D]), dint("xs1", [NTOK, D])]
    ub16 = [dint(f"ub16_{i}", [16384, D], BF16) for i in range(4)]
    vb16 = [dint(f"vb16_{i}", [16384, D], BF16) for i in range(4)]
    tb_b = {("u", i): Buf(f"ub{i}") for i in range(4)}
    tb_b.update({("v", i): Buf(f"vb{i}") for i in range(4)})
    xs_b = [Buf("xs0"), Buf("xs1")]

    root = ExitStack()
    kb.stack = root
    ident_f = kb.sb("ident_f", [128, 128], F32)
    ident_b = kb.sb("ident_b", [128, 128], BF16)
    iota_pi = kb.sb("iota_pi", [128, 1], I32)
    iota_fi = kb.sb("iota_fi", [128, 128], I32)
    iota_p = kb.sb("iota_p", [128, 1], F32)
    iota_f = kb.sb("iota_f", [128, 128], F32)
    pdiv = kb.sb("pdiv", [128, 1], F32)
    fdiv = kb.sb("fdiv", [128, 128], F32)
    tmpi = kb.sb("tmpi", [128, 128], I32)
    maskP = kb.sb("maskP", [128, 128], F32)
    maskS = kb.sb("maskS", [128, 128], F32)
    seq1h = kb.sb("seq1h", [128, 16], F32)
    sel4 = kb.sb("sel4", [4, 4, 128], F32)
    scn = kb.sb("scn", [4, 4, 128], F32)
    one_col = kb.sb("one_col", [128, 1], F32)
    eps_col = kb.sb("eps_col", [128, 1], F32)

    kb.v(POOL, lambda: nc.gpsimd.iota(iota_pi[:], pattern=[[0, 1]], base=0, channel_multiplier=1), w=[iota_pi])
    kb.v(POOL, lambda: nc.gpsimd.iota(iota_fi[:], pattern=[[1, 128]], base=0, channel_multiplier=0), w=[iota_fi])
    kb.cp(DVE, iota_p[:], iota_pi[:], r=[iota_pi], w=[iota_p])
    kb.cp(DVE, iota_f[:], iota_fi[:], r=[iota_fi], w=[iota_f])
    kb.v(DVE, lambda: nc.vector.tensor_single_scalar(out=tmpi[:, 0:1], in_=iota_pi[:], scalar=3, op=ALU.arith_shift_right),
         r=[iota_pi], w=[tmpi])
    kb.cp(DVE, pdiv[:], tmpi[:, 0:1], r=[tmpi], w=[pdiv])
    kb.v(DVE, lambda: nc.vector.tensor_single_scalar(out=tmpi[:], in_=iota_fi[:], scalar=3, op=ALU.arith_shift_right),
         r=[iota_fi], w=[tmpi])
    kb.cp(DVE, fdiv[:], tmpi[:], r=[tmpi], w=[fdiv])
    kb.ts(DVE, ident_f[:], iota_f[:], iota_p[:, 0:1], ALU.is_equal, r=[iota_f, iota_p], w=[ident_f])
    kb.cp(DVE, ident_b[:], ident_f[:], r=[ident_f], w=[ident_b])
    kb.ts(DVE, maskP[:], iota_f[:], iota_p[:, 0:1], ALU.is_ge, -1.0, ALU.add, r=[iota_f, iota_p], w=[maskP])
    kb.ts(DVE, maskP[:], maskP[:], -NEG, ALU.mult, r=[maskP], w=[maskP])
    kb.ts(DVE, maskS[:], iota_f[:], iota_p[:, 0:1], ALU.is_ge, r=[iota_f, iota_p], w=[maskS])
    kb.ts(DVE, fdiv[:], fdiv[:], pdiv[:, 0:1], ALU.is_equal, r=[fdiv, pdiv], w=[fdiv])
    kb.tt(DVE, maskS[:], maskS[:], fdiv[:], ALU.mult, r=[maskS, fdiv], w=[maskS])
    kb.ts(DVE, maskS[:], maskS[:], -1.0, ALU.add, -NEG, ALU.mult, r=[maskS], w=[maskS])
    kb.ts(DVE, seq1h[:], iota_f[:, 0:16], pdiv[:, 0:1], ALU.is_equal, r=[iota_f, pdiv], w=[seq1h])
    kb.cp(DVE, sel4[:], ident_f[0:4, 0:4].unsqueeze(2).to_broadcast([4, 4, 128]), r=[ident_f], w=[sel4])
    kb.v(DVE, lambda: nc.vector.memset(scn[:, 0, :], 1.0), w=[scn])
    kb.v(DVE, lambda: nc.vector.memset(scn[:, 0, 0:1], 0.0), w=[scn])
    kb.v(DVE, lambda: nc.vector.memset(scn[:, 1, :], 0.0), w=[scn])
    kb.v(DVE, lambda: nc.vector.memset(scn[:, 1, 0:1], -1e30), w=[scn])
    kb.v(DVE, lambda: nc.vector.memset(scn[:, 2, :], 1.0), w=[scn])
    kb.v(DVE, lambda: nc.vector.memset(scn[:, 3, :], 0.0), w=[scn])
    kb.v(DVE, lambda: nc.vector.memset(scn[:, 2, :].rearrange("p (j t) -> p j t", t=8)[:, :, 0:1], 0.0), w=[scn])
    kb.v(DVE, lambda: nc.vector.memset(scn[:, 3, :].rearrange("p (j t) -> p j t", t=8)[:, :, 0:1], -1e30), w=[scn])
    kb.v(DVE, lambda: nc.vector.memset(one_col[:], 1.0), w=[one_col])
    kb.v(DVE, lambda: nc.vector.memset(eps_col[:], LN_EPS), w=[eps_col])

    C = dict(ident_f=ident_f, ident_b=ident_b, maskP=maskP, maskS=maskS, seq1h=seq1h, sel4=sel4, scn=scn,
             one_col=one_col, eps_col=eps_col)

    def layer_norm_tile(z, lng, lnb, out_t, tmp_stats, eng2=None):
        eng2 = eng2 or POOL
        st, mv, rstd = tmp_stats
        for i in range(2):
            kb.v(DVE, lambda i=i: nc.vector.bn_stats(out=st[:, i, :], in_=z[:, i * 512:(i + 1) * 512]), r=[z], w=[st])
        kb.v(DVE, lambda: nc.vector.bn_aggr(out=mv[:], in_=st[:].rearrange("p a b -> p (a b)")), r=[st], w=[mv])
        kb.act(rstd[:], mv[:, 1:2], AF.Sqrt, r=[mv, eps_col], w=[rstd], bias=eps_col[:, 0:1], scale=1.0)
        kb.v(DVE, lambda: nc.vector.reciprocal(out=rstd[:], in_=rstd[:]), r=[rstd], w=[rstd])
        kb.ts(DVE, out_t[:], z[:], mv[:, 0:1], ALU.subtract, rstd[:, 0:1], ALU.mult, r=[z, mv, rstd], w=[out_t])
        kb.tt(eng2, out_t[:], out_t[:], lng[:], ALU.mult, r=[out_t, lng], w=[out_t])
        kb.tt(eng2, out_t[:], out_t[:], lnb[:], ALU.add, r=[out_t, lnb], w=[out_t])

    def mlstm_layer(l, src, src_b, dst, dst_b):
        with ExitStack() as ph:
            kb.stack = ph
            wbf = kb.sb("wbf", [128, 8, 4104], BF16)
            wout = kb.sb("wout", [128, 8, D], BF16)
            wst = [kb.sb("wst", [128, 1026], F32) for _ in range(2)]
            normg = kb.sb("normg", [128, D], F32)
            lng = kb.sb("lng", [128, D], F32)
            lnb = kb.sb("lnb", [128, D], F32)
            bi_col = kb.sb("bi_col", [4, 1], F32)
            nbf_col = kb.sb("nbf_col", [4, 1], F32)
            i = 0
            for kc in range(8):
                for q4 in range(4):
                    st = wst[i % 2]
                    n = 1026
                    kb.dma(st[:, 0:n], w_in_a[l, kc * 128:(kc + 1) * 128, q4 * 1026:q4 * 1026 + n], w=[st])
                    eng = (DVE, POOL)[i % 2]
                    kb.cp(eng, wbf[:, kc, q4 * 1026:q4 * 1026 + n], st[:, 0:n], r=[st], w=[wbf])
                    i += 1
            for kc in range(8):
                st = wst[i % 2]
                kb.dma(st[:, 0:1024], w_out_a[l, kc * 128:(kc + 1) * 128, :], w=[st])
                eng = (DVE, POOL)[i % 2]
                kb.cp(eng, wout[:, kc, :], st[:, 0:1024], r=[st], w=[wout])
                i += 1
            kb.dma(normg[:], bc_rows(norm_a[l:l + 1, :], D), w=[normg])
            kb.dma(lng[:], bc_rows(ln_g[l, 0:1, :], D), w=[lng])
            kb.dma(lnb[:], bc_rows(ln_b[l, 0:1, :], D), w=[lnb])
            kb.dma(bi_col[:], b_gates_a[l, 0:4].rearrange("(p o) -> p o", o=1), w=[bi_col], slow=True)
            kb.dma(nbf_col[:], b_gates_a[l, 4:8].rearrange("(p o) -> p o", o=1), w=[nbf_col], slow=True)
            kb.ts(DVE, nbf_col[:], nbf_col[:], -1.0, ALU.mult, r=[nbf_col], w=[nbf_col])

            x_t = [kb.sb("x_t", [128, D], F32) for _ in range(2)]
            x_b = kb.sb("x_b", [128, D], BF16)
            xT = kb.sb("xT", [128, 8, 128], BF16)
            qT = kb.sb("qT", [128, 8, 128], BF16)
            kT = kb.sb("kT", [128, 8, 128], BF16)
            k_tm = kb.sb("k_tm", [128, D], BF16)
            v_aug = kb.sb("v_aug", [128, 4, 257], BF16)
            sig_o = kb.sb("sig_o", [128, D], BF16)
            hfin = kb.sb("hfin", [128, D], F32)
            hfin_b = x_b
            hT = xT
            z_t = kb.sb("z_t", [128, D], F32)
            xo_t = z_t
            rows = kb.sb("rows", [4, 12, 128], F32)
            seqr = kb.sb("seqr", [4, 4, 16], F32)
            cols = kb.sb("cols", [128, 3, 4], F32)
            decay_bc = kb.sb("decay_bc", [128, 4, 16], F32)
            E_sb = kb.sb("E_sb", [128, 128], F32)
            PT = kb.sb("PT", [128, 128], BF16)
            wi_bc = kb.sb("wi_bc", [128, 128], F32)
            qpT = kb.sb("qpT", [128, 2, 2176], BF16)
            wm = kb.sb("wm", [128, 16], F32)
            wv_blk = kb.sb("wv_blk", [128, 16, 257], BF16)
            dm = kb.sb("dm", [128, 2], F32)
            hraw = kb.sb("hraw", [128, 256], F32)
            bst = kb.sb("bst", [128, 6], F32)
            bmv = kb.sb("bmv", [128, 2], F32)
            brs = kb.sb("brs", [128, 1], F32)
            lst = kb.sb("lst", [128, 2, 6], F32)
            lmv = kb.sb("lmv", [128, 2], F32)
            lrs = kb.sb("lrs", [128, 1], F32)
            CTp = kb.sb("CTp", [128, 4, 2, 257], F32)
            CTbp = kb.sb("CTbp", [128, 4, 2, 257], BF16)
            CTf = [kb.sb("CTf", [128, 2, 257], F32) for _ in range(2)]
            CTbs = kb.sb("CTbs", [128, 2, 2, 257], BF16)
            Cio = [kb.sb("Cio", [128, 2, 256], F32) for _ in range(2)]
            nio = kb.sb("nio", [16, 256], F32)
            m_carry = kb.sb("m_carry", [4, 1], F32)
            p_a = [kb.ps("p_a", [128, 512], F32) for _ in range(2)]
            p_tr = kb.ps("p_tr", [128, 1024], BF16)
            p_e = kb.ps("p_e", [128, 512], F32)
            p_s = kb.ps("p_s", [128, 512], F32)
            p_n = kb.ps("p_n", [128, 512], F32)
            p_u = [kb.ps("p_u", [128, 512], F32) for _ in range(2)]
            pe_b = pw_b = pd_b = pc_b = p_e
            ps_b = pg_b = p_s

            kb.v(DVE, lambda: nc.vector.memset(v_aug[:, :, 256:257], 1.0), w=[v_aug])
            kb.v(DVE, lambda: nc.vector.memset(qpT[:], 0.0), w=[qpT])
            kb.v(POOL, lambda: nc.gpsimd.memset(CTp[:], 0.0), w=[CTp])
            kb.v(POOL, lambda: nc.gpsimd.memset(CTbp[:], 0.0), w=[CTbp])
            kb.v(DVE, lambda: nc.vector.memset(m_carry[:], 0.0), w=[m_carry])

            pa_i = [0]

            def next_pa():
                pa_i[0] += 1
                return p_a[pa_i[0] % 2]

            def state_in(h, j, seq, ctf):
                cio = Cio[j % 2]
                kb.dma(cio[:], stC_in[l, seq, h].rearrange("(vc p) k -> p vc k", p=128), w=[cio])
                pt = next_pa()
                for vc in range(2):
                    for kc in range(2):
                        kb.tr(pt[:, kc * 256 + vc * 128: kc * 256 + vc * 128 + 128], cio[:, vc, kc * 128:(kc + 1) * 128],
                              ident_f[:], r=[cio, ident_f], w=[pt])
                kb.cp(DVE, ctf[:, :, 0:256], pt[:].rearrange("p (c v) -> p c v", c=2), r=[pt], w=[ctf])

            def state_out(h, seq_out, ctf, j):
                cio = Cio[j % 2]
                pt = next_pa()
                for vc in range(2):
                    for kc in range(2):
                        kb.tr(pt[:, vc * 256 + kc * 128: vc * 256 + kc * 128 + 128], ctf[:, kc, vc * 128:(vc + 1) * 128],
                              ident_f[:], r=[ctf, ident_f], w=[pt])
                kb.cp(ACT_or_DVE(j), cio[:], pt[:].rearrange("p (c v) -> p c v", c=2), r=[pt], w=[cio])
                kb.dma(C_out[l, seq_out, h].rearrange("(vc p) k -> p vc k", p=128), cio[:], r=[cio], w=[out_bufs["C"]])

            def ACT_or_DVE(j):
                return DVE

            def tile_step(ti):
                is_p = ti < NT_P
                nseq = 1 if is_p else 16
                L = 128 // nseq
                mask = maskP if is_p else maskS
                r_row = scn[:, 0, :] if is_p else scn[:, 2, :]
                a_row = scn[:, 1, :] if is_p else scn[:, 3, :]
                xt = x_t[ti % 2]
                kb.dma(xt[:], src[ti * 128:(ti + 1) * 128, :], r=[src_b], w=[xt])
                kb.act(x_b[:], xt[:], AF.Copy, r=[xt], w=[x_b])
                for kc in range(8):
                    kb.tr(p_tr[:, kc * 128:(kc + 1) * 128], x_b[:, kc * 128:(kc + 1) * 128], ident_b[:], r=[x_b, ident_b], w=[p_tr])
                kb.cp(DVE, xT[:].rearrange("p a b -> p (a b)"), p_tr[:], r=[p_tr], w=[xT])
                for g4 in range(4):
                    pa = next_pa()
                    for gg in range(4):
                        g = g4 * 4 + gg
                        col = g * 128 if g < 8 else 1024 + (g - 8) * 128
                        for kc in range(8):
                            kb.mm(pa[:, gg * 128:(gg + 1) * 128], wbf[:, kc, col:col + 128], xT[:, kc, :], kc == 0, kc == 7,
                                  r=[wbf, xT], w=[pa])
                    if g4 < 2:
                        kb.act(qT[:, g4 * 4:(g4 + 1) * 4, :].rearrange("p a b -> p (a b)"), pa[:], AF.Copy, r=[pa], w=[qT])
                    else:
                        kb.act(kT[:, (g4 - 2) * 4:(g4 - 1) * 4, :].rearrange("p a b -> p (a b)"), pa[:], AF.Copy, r=[pa], w=[kT],
                               scale=1.0 / 16.0)
                for blk in range(6):
                    pa = next_pa()
                    col = 1024 + blk * 512
                    for kc in range(8):
                        kb.mm(pa[:], xT[:, kc, :], wbf[:, kc, col:col + 512], kc == 0, kc == 7, r=[wbf, xT], w=[pa])
                    if blk < 2:
                        kb.act(k_tm[:, blk * 512:(blk + 1) * 512], pa[:], AF.Copy, r=[pa], w=[k_tm], scale=1.0 / 16.0)
                    elif blk < 4:
                        b2 = blk - 2
                        kb.cp(DVE, v_aug[:, 2 * b2:2 * b2 + 2, 0:256], pa[:].rearrange("p (h v) -> p h v", h=2), r=[pa], w=[v_aug])
                    else:
                        b2 = blk - 4
                        kb.act(sig_o[:, b2 * 512:(b2 + 1) * 512], pa[:], AF.Sigmoid, r=[pa], w=[sig_o])
                for gi in range(2):
                    for kc in range(8):
                        kb.mm(p_s[0:4, 128 + gi * 128: 256 + gi * 128], wbf[:, kc, 4096 + gi * 4:4100 + gi * 4], xT[:, kc, :],
                              kc == 0, kc == 7, r=[wbf, xT], w=[pg_b])
                R = lambda i: rows[:, i, :]
                kb.ts(DVE, R(0), p_s[0:4, 128:256], bi_col[:, 0:1], ALU.add, r=[pg_b, bi_col], w=[rows])
                kb.act(R(1), p_s[0:4, 256:384], AF.Exp, r=[pg_b, nbf_col], w=[rows], bias=nbf_col[:, 0:1], scale=-1.0)
                kb.act(R(1), R(1), AF.Ln, r=[rows, one_col], w=[rows], bias=one_col[0:4, 0:1], scale=1.0)
                kb.v(DVE, lambda: nc.vector.tensor_tensor_scan(out=R(2), data0=r_row, data1=R(1), initial=0.0,
                                                                op0=ALU.mult, op1=ALU.add), r=[rows, scn], w=[rows])
                kb.tt(DVE, R(3), R(0), R(2), ALU.add, r=[rows], w=[rows])
                kb.cp(DVE, R(4), R(3), r=[rows], w=[rows])
                if is_p:
                    kb.cp(DVE, seqr[:, 0, 0:1], m_carry[:, 0:1], r=[m_carry], w=[seqr])
                else:
                    s0 = (ti - NT_P) * 16
                    kb.dma(seqr[:, 0, 0:16], stm_in[l, s0:s0 + 16, :].rearrange("j h -> h j"), w=[seqr], slow=True)
                v3 = lambda ap: ap.rearrange("p (j t) -> p j t", t=L)
                kb.tt(DVE, v3(R(4))[:, :, 0], v3(R(3))[:, :, 0], seqr[:, 0, 0:nseq], ALU.max, r=[rows, seqr], w=[rows])
                kb.v(DVE, lambda: nc.vector.tensor_tensor_scan(out=R(5), data0=a_row, data1=R(4), initial=0.0,
                                                                op0=ALU.add, op1=ALU.max), r=[rows, scn], w=[rows])
                kb.ts(DVE, R(6), R(5), -1.0, ALU.mult, r=[rows], w=[rows])
                kb.tt(DVE, R(7), R(2), R(5), ALU.subtract, r=[rows], w=[rows])
                kb.cp(DVE, seqr[:, 1, 0:nseq], v3(R(5))[:, :, L - 1], r=[rows], w=[seqr])
                kb.cp(DVE, v3(R(8)), seqr[:, 0, 0:nseq].unsqueeze(2).to_broadcast([4, nseq, L]), r=[seqr], w=[rows])
                kb.cp(DVE, v3(R(9)), seqr[:, 1, 0:nseq].unsqueeze(2).to_broadcast([4, nseq, L]), r=[seqr], w=[rows])
                kb.tt(DVE, R(10), R(8), R(5), ALU.subtract, r=[rows], w=[rows])
                kb.tt(DVE, R(11), R(3), R(9), ALU.subtract, r=[rows], w=[rows])
                kb.tt(DVE, seqr[:, 2, 0:nseq], seqr[:, 0, 0:nseq], seqr[:, 1, 0:nseq], ALU.subtract, r=[seqr], w=[seqr])
                kb.tt(DVE, seqr[:, 3, 0:nseq], seqr[:, 1, 0:nseq], v3(R(2))[:, :, L - 1], ALU.subtract, r=[seqr, rows], w=[seqr])
                if is_p:
                    kb.cp(DVE, m_carry[:, 0:1], seqr[:, 3, 0:1], r=[seqr], w=[m_carry])
                for i, ri in enumerate((3, 7, 11)):
                    kb.tr(p_e[:, 320 + 4 * i: 324 + 4 * i], R(ri), ident_f[0:4, 0:4], r=[rows, ident_f], w=[pc_b])
                kb.cp(DVE, cols[:, 0, :], p_e[:, 320:324], r=[pc_b], w=[cols])
                kb.act(cols[:, 1:3, :].rearrange("p a b -> p (a b)"), p_e[:, 324:332], AF.Exp, r=[pc_b], w=[cols])
                for h in range(4):
                    kb.mm(p_e[:, 256 + h * 16: 256 + h * 16 + nseq], sel4[:, h, :], seqr[:, 2, 0:nseq], True, True,
                          r=[sel4, seqr], w=[pd_b])
                kb.act(decay_bc[:, :, 0:nseq], p_e[:, 256:320].rearrange("p (h j) -> p h j", h=4)[:, :, 0:nseq], AF.Exp,
                       r=[pd_b], w=[decay_bc])
                for h in range(4):
                    kb.mm(p_e[:, 0:128], sel4[:, h, :], R(6), True, False, r=[sel4, rows], w=[pe_b])
                    kb.mm(p_e[:, 0:128], ident_f[:], mask[:], False, True, r=[ident_f, mask], w=[pe_b])
                    kb.act(E_sb[:], p_e[:, 0:128], AF.Exp, r=[pe_b, cols], w=[E_sb], bias=cols[:, 0, h:h + 1], scale=1.0)
                    for c in range(2):
                        kb.mm(p_s[:, 0:128], kT[:, 2 * h + c, :], qT[:, 2 * h + c, :], c == 0, c == 1, r=[kT, qT], w=[ps_b])
                    kb.tt(DVE, PT[:], p_s[:, 0:128], E_sb[:], ALU.mult, r=[ps_b, E_sb], w=[PT])
                    kb.mm(p_e[:, 128:256], sel4[:, h, :], R(10), True, True, r=[sel4, rows], w=[pw_b])
                    kb.act(wi_bc[:], p_e[:, 128:256], AF.Exp, r=[pw_b], w=[wi_bc])
                    qv = qpT[:].rearrange("p c (j x) -> p c j x", x=136)[:, :, 0:nseq, 0:L] if not is_p else None
                    if is_p:
                        kb.tt(DVE, qpT[:, :, 0:128], qT[:, 2 * h:2 * h + 2, :], wi_bc[:].unsqueeze(1).to_broadcast([128, 2, 128]),
                              ALU.mult, r=[qT, wi_bc], w=[qpT])
                    else:
                        kb.tt(DVE, qv, qT[:, 2 * h:2 * h + 2, :].rearrange("p c (j t) -> p c j t", t=L),
                              wi_bc[:].rearrange("p (j t) -> p j t", t=L).unsqueeze(1).to_broadcast([128, 2, nseq, L]),
                              ALU.mult, r=[qT, wi_bc], w=[qpT])
                    ctfs = []
                    if not is_p:
                        s0 = (ti - NT_P) * 16
                        kb.dma(nio[:], stn_in[l, s0:s0 + 16, h, :], w=[nio])
                        pa = next_pa()
                        for c in range(2):
                            kb.tr(pa[:, c * 16:(c + 1) * 16], nio[:, c * 128:(c + 1) * 128], ident_f[0:16, 0:16],
                                  r=[nio, ident_f], w=[pa])
                        kb.cp(DVE, nstage[:], pa[:, 0:32], r=[pa], w=[nstage])
                    nmm = 1 + 2 * nseq
                    kb.mm(p_n[:, 0:257], PT[:], v_aug[:, h, :], True, False, r=[PT, v_aug], w=[p_n])
                    if is_p:
                        for c in range(2):
                            kb.mm(p_n[:, 0:257], qpT[:, c, 0:128], CTbp[:, h, c, :], False, c == 1, r=[qpT, CTbp], w=[p_n])
                    else:
                        for j in range(nseq):
                            ctf = CTf[j % 2]
                            state_in(h, j, s0 + j, ctf)
                            kb.cp(DVE, ctf[:, :, 256], nstage[:, :].rearrange("p (c j) -> p c j", c=2)[:, :, j], r=[nstage], w=[ctf])
                            kb.cp(POOL, CTbs[:, j % 2, :, :], ctf[:], r=[ctf], w=[CTbs])
                            for c in range(2):
                                kb.mm(p_n[:, 0:257], qpT[:, c, j * 128:(j + 1) * 128], CTbs[:, j % 2, c, :], False,
                                      (j == nseq - 1) and c == 1, r=[qpT, CTbs], w=[p_n])
                            if j == 0:
                                kb.ts(DVE, wm[:, 0:16], seq1h[:], cols[:, 2, h:h + 1], ALU.mult, r=[seq1h, cols], w=[wm])
                                kb.tt(DVE, wv_blk[:], v_aug[:, h, :].unsqueeze(1).to_broadcast([128, 16, 257]),
                                      wm[:].unsqueeze(2).to_broadcast([128, 16, 257]), ALU.mult, r=[v_aug, wm], w=[wv_blk])
                            for c in range(2):
                                pu = p_u[c]
                                kb.mm(pu[:, 0:257], k_tm[:, h * 256 + c * 128: h * 256 + c * 128 + 128], wv_blk[:, j, :], True, True,
                                      r=[k_tm, wv_blk], w=[pu])
                                kb.v(DVE, lambda c=c, pu=pu, ctf=ctf, j=j: nc.vector.scalar_tensor_tensor(
                                    out=ctf[:, c, :], in0=ctf[:, c, :], scalar=decay_bc[:, h, j:j + 1], in1=pu[:, 0:257],
                                    op0=ALU.mult, op1=ALU.add), r=[ctf, decay_bc, pu], w=[ctf])
                            state_out(h, 1 + s0 + j, ctf, j)
                            kb.cp(DVE, nstage2[:].rearrange("p (c j) -> p c j", c=2)[:, :, j], ctf[:, :, 256], r=[ctf], w=[nstage2])
                        pa = next_pa()
                        for c in range(2):
                            kb.tr(pa[0:16, c * 128:(c + 1) * 128], nstage2[:, c * 16:(c + 1) * 16], ident_f[:], r=[nstage2, ident_f], w=[pa])
                        kb.cp(DVE, nio[:], pa[0:16, 0:256], r=[pa], w=[nio])
                        kb.dma(n_out[l, 1 + s0:1 + s0 + 16, h, :], nio[:], r=[nio], w=[out_bufs["n"]])
                    kb.act(dm[:, 0:1], p_n[:, 256:257], AF.Abs, r=[p_n], w=[dm])
                    kb.ts(DVE, dm[:, 0:1], dm[:, 0:1], cols[:, 1, h:h + 1], ALU.max, r=[dm, cols], w=[dm])
                    kb.v(DVE, lambda: nc.vector.reciprocal(out=dm[:, 1:2], in_=dm[:, 0:1]), r=[dm], w=[dm])
                    kb.ts(DVE, hraw[:], p_n[:, 0:256], dm[:, 1:2], ALU.mult, r=[p_n, dm], w=[hraw])
                    kb.v(DVE, lambda: nc.vector.bn_stats(out=bst[:], in_=hraw[:]), r=[hraw], w=[bst])
                    kb.v(DVE, lambda: nc.vector.bn_aggr(out=bmv[:], in_=bst[:]), r=[bst], w=[bmv])
                    kb.act(brs[:], bmv[:, 1:2], AF.Sqrt, r=[bmv, eps_col], w=[brs], bias=eps_col[:, 0:1], scale=1.0)
                    kb.v(DVE, lambda: nc.vector.reciprocal(out=brs[:], in_=brs[:]), r=[brs], w=[brs])
                    kb.ts(DVE, hfin[:, h * 256:(h + 1) * 256], hraw[:], bmv[:, 0:1], ALU.subtract, brs[:, 0:1], ALU.mult,
                          r=[hraw, bmv, brs], w=[hfin])
                    if is_p:
                        kb.ts(DVE, wv_blk[:, 0, :], v_aug[:, h, :], cols[:, 2, h:h + 1], ALU.mult, r=[v_aug, cols], w=[wv_blk])
                        for c in range(2):
                            pu = p_u[c]
                            kb.mm(pu[:, 0:257], k_tm[:, h * 256 + c * 128: h * 256 + c * 128 + 128], wv_blk[:, 0, :], True, True,
                                  r=[k_tm, wv_blk], w=[pu])
                            kb.v(DVE, lambda c=c, pu=pu: nc.vector.scalar_tensor_tensor(
                                out=CTp[:, h, c, :], in0=CTp[:, h, c, :], scalar=decay_bc[:, h, 0:1], in1=pu[:, 0:257],
                                op0=ALU.mult, op1=ALU.add), r=[CTp, decay_bc, pu], w=[CTp])
                        kb.cp(POOL, CTbp[:, h, :, :], CTp[:, h, :, :], r=[CTp], w=[CTbp])
                if not is_p:
                    s0 = (ti - NT_P) * 16
                    kb.dma(m_out[l, 1 + s0:1 + s0 + 16, :].rearrange("j h -> h j"), seqr[:, 3, 0:16], r=[seqr], w=[out_bufs["m"]], slow=True)
                kb.tt(POOL, hfin[:], hfin[:], normg[:], ALU.mult, r=[hfin, normg], w=[hfin])
                kb.tt(DVE, hfin_b[:], hfin[:], sig_o[:], ALU.mult, r=[hfin, sig_o], w=[hfin_b])
                for kc in range(8):
                    kb.tr(p_tr[:, kc * 128:(kc + 1) * 128], hfin_b[:, kc * 128:(kc + 1) * 128], ident_b[:], r=[hfin_b, ident_b], w=[p_tr])
                kb.cp(DVE, hT[:].rearrange("p a b -> p (a b)"), p_tr[:], r=[p_tr], w=[hT])
                for half in range(2):
                    pa = next_pa()
                    for kc in range(8):
                        kb.mm(pa[:], hT[:, kc, :], wout[:, kc, half * 512:(half + 1) * 512], kc == 0, kc == 7, r=[hT, wout], w=[pa])
                    kb.v(DVE, lambda half=half, pa=pa: nc.vector.scalar_tensor_tensor(
                        out=z_t[:, half * 512:(half + 1) * 512], in0=xt[:, half * 512:(half + 1) * 512], scalar=DN_ALPHA,
                        in1=pa[:], op0=ALU.mult, op1=ALU.add), r=[xt, pa], w=[z_t])
                layer_norm_tile(z_t, lng, lnb, xo_t, (lst, lmv, lrs))
                kb.dma(dst[ti * 128:(ti + 1) * 128, :], xo_t[:], r=[xo_t], w=[dst_b])

            nstage = kb.sb("nstage", [128, 32], F32)
            nstage2 = kb.sb("nstage2", [128, 32], F32)
            for ti in range(NT):
                tile_step(ti)
                if ti == NT_P - 1:
                    kb.v(DVE, lambda: nc.vector.memset(qpT[:], 0.0), w=[qpT])
                    for h in range(4):
                        ctf = CTf[h % 2]
                        kb.cp(DVE, ctf[:], CTp[:, h, :, :], r=[CTp], w=[ctf])
                        state_out(h, 0, ctf, h)
                        pa = next_pa()
                        for c in range(2):
                            kb.tr(pa[0:1, c * 128:(c + 1) * 128], CTp[:, h, c, 256:257], ident_f[:], r=[CTp, ident_f], w=[pa])
                        kb.cp(DVE, nio[0:1, :], pa[0:1, 0:256], r=[pa], w=[nio])
                        kb.dma(n_out[l, 0:1, h, :], nio[0:1, :], r=[nio], w=[out_bufs["n"]])
                    kb.dma(m_out[l, 0:1, :].rearrange("j h -> h j"), m_carry[:, 0:1], r=[m_carry], w=[out_bufs["m"]], slow=True)
        kb.stack = root

    def convert_tables():
        with ExitStack() as ph:
            kb.stack = ph
            NR = 4
            stg = [kb.sb("cv_f", [128, NR * D], F32) for _ in range(2)]
            stb = [kb.sb("cv_b", [128, NR * D], BF16) for _ in range(2)]
            i = 0
            for l in range(4):
                for kind, src_t, dst_t in (("u", peer_u[l], ub16[l]), ("v", peer_v[l], vb16[l])):
                    sv = src_t.rearrange("(c p r) d -> c p (r d)", p=128, r=NR)
                    dv = dst_t.rearrange("(c p r) d -> c p (r d)", p=128, r=NR)
                    for c in range(16384 // (128 * NR)):
                        f, b_ = stg[i % 2], stb[i % 2]
                        kb.dma(f[:], sv[c], w=[f])
                        eng = (DVE, ACT, POOL)[i % 3]
                        if eng is ACT:
                            kb.act(b_[:], f[:], AF.Copy, r=[f], w=[b_])
                        else:
                            kb.cp(eng, b_[:], f[:], r=[f], w=[b_])
                        kb.dma(dv[c], b_[:], r=[b_], w=[tb_b[(kind, l)]], q=ACT)
                        i += 1
        kb.stack = root

    def peer_layer(l, src, src_b, dst, dst_b):
        with ExitStack() as ph:
            kb.stack = ph
            wq = kb.sb("wq", [128, 8, 2048], BF16)
            wst = [kb.sb("wst", [128, 1024], F32) for _ in range(2)]
            keysT = kb.sb("keysT", [128, 16, 128], BF16)
            lng = kb.sb("lng", [128, D], F32)
            lnb = kb.sb("lnb", [128, D], F32)
            i = 0
            for kc in range(8):
                for hf in range(2):
                    st = wst[i % 2]
                    kb.dma(st[:], peer_wq[l, kc * 128:(kc + 1) * 128, hf * 1024:(hf + 1) * 1024], w=[st])
                    kb.cp((DVE, POOL)[i % 2], wq[:, kc, hf * 1024:(hf + 1) * 1024], st[:], r=[st], w=[wq])
                    i += 1
            for g in range(2):
                st = wst[i % 2]
                kb.dma(st[:].rearrange("p (a b) -> p a b", a=8), peer_keysT[l, g * 8:(g + 1) * 8].rearrange("a d k -> d a k"), w=[st])
                kb.cp((DVE, POOL)[i % 2], keysT[:, g * 8:(g + 1) * 8, :], st[:].rearrange("p (a b) -> p a b", a=8), r=[st], w=[keysT])
                i += 1
            kb.dma(lng[:], bc_rows(ln_g[l, 1:2, :], D), w=[lng])
            kb.dma(lnb[:], bc_rows(ln_b[l, 1:2, :], D), w=[lnb])
            x_t = [kb.sb("x_t", [128, D], F32) for _ in range(2)]
            x_b = kb.sb("x_b", [128, D], BF16)
            xT = kb.sb("xT", [128, 8, 128], BF16)
            qT16 = kb.sb("qT16", [128, 16, 128], BF16)
            s_sb = kb.sb("s_sb", [128, 16, 128], F32)
            s_wk = kb.sb("s_wk", [128, 256], F32)
            stop = kb.sb("stop", [128, 16, 16], F32)
            itop = kb.sb("itop", [128, 16, 16], U32)
            itopf = kb.sb("itopf", [128, 16, 16], F32)
            cand = kb.sb("cand", [128, 8, 256], F32)
            gval = kb.sb("gval", [128, 8, 16], F32)
            gpos = kb.sb("gpos", [128, 8, 16], U32)
            ai = kb.sb("ai", [128, 128], U32)
            af = kb.sb("af", [128, 2, 128], F32)
            oh = kb.sb("oh", [128, 128, 16], F32)
            i01 = kb.sb("i01", [128, 2, 128], F32)
            eidx = kb.sb("eidx", [128, 128], I32)
            gate = kb.sb("gate", [128, 8, 16], F32)
            gsum = kb.sb("gsum", [128, 8], F32)
            actv = kb.sb("actv", [128, 128], F32)
            g1 = kb.sb("g1", [128, 128], F32)
            g2 = kb.sb("g2", [128, 128], F32)
            hw = kb.sb("hw", [128, 128], F32)
            NB = 12
            ubuf = [kb.sb("ubuf", [128, D], BF16) for _ in range(NB)]
            vbuf = [kb.sb("vbuf", [128, D], BF16) for _ in range(NB)]
            junk = kb.sb("junk", [128, D], BF16)
            y_acc = kb.sb("y_acc", [128, D], F32)
            vs = [kb.sb("vs", [128, D], BF16) for _ in range(3)]
            eidx2 = [eidx, kb.sb("eidx_b", [128, 128], I32)]
            lst = kb.sb("lst", [128, 2, 6], F32)
            lmv = kb.sb("lmv", [128, 2], F32)
            lrs = kb.sb("lrs", [128, 1], F32)
            p_a = [kb.ps("p_a", [128, 512], F32) for _ in range(2)]
            p_tr = kb.ps("p_tr", [128, 1024], BF16)
            p_s4 = [kb.ps("p_s4", [128, 512], F32) for _ in range(2)]
            p_y = [kb.ps("p_y", [128, 512], F32) for _ in range(2)]
            pa_i = [0]

            def next_pa():
                pa_i[0] += 1
                return p_a[pa_i[0] % 2]

            def top16(src_ap, vals_out, idx_out):
                n = src_ap.shape[1]
                kb.v(DVE, lambda: nc.vector.max(out=vals_out[:, 0:8], in_=src_ap), r=[s_sb, cand], w=[stop, gval])
                kb.v(DVE, lambda: nc.vector.max_index(out=idx_out[:, 0:8], in_max=vals_out[:, 0:8], in_values=src_ap),
                     r=[s_sb, cand, stop, gval], w=[itop, gpos])
                kb.v(DVE, lambda: nc.vector.match_replace(out=s_wk[:, 0:n], in_to_replace=vals_out[:, 0:8], in_values=src_ap,
                                                          imm_value=-1e30), r=[s_sb, cand, stop, gval], w=[s_wk])
                kb.v(DVE, lambda: nc.vector.max(out=vals_out[:, 8:16], in_=s_wk[:, 0:n]), r=[s_wk], w=[stop, gval])
                kb.v(DVE, lambda: nc.vector.max_index(out=idx_out[:, 8:16], in_max=vals_out[:, 8:16], in_values=s_wk[:, 0:n]),
                     r=[s_wk, stop, gval], w=[itop, gpos])

            def retrieval(ti, eidx):
                xt = x_t[ti % 2]
                kb.dma(xt[:], src[ti * 128:(ti + 1) * 128, :], r=[src_b], w=[xt])
                kb.act(x_b[:], xt[:], AF.Copy, r=[xt], w=[x_b])
                for kc in range(8):
                    kb.tr(p_tr[:, kc * 128:(kc + 1) * 128], x_b[:, kc * 128:(kc + 1) * 128], ident_b[:], r=[x_b, ident_b], w=[p_tr])
                kb.cp(DVE, xT[:].rearrange("p a b -> p (a b)"), p_tr[:], r=[p_tr], w=[xT])
                for g4 in range(4):
                    pa = next_pa()
                    for gg in range(4):
                        hp = g4 * 4 + gg
                        for kc in range(8):
                            kb.mm(pa[:, gg * 128:(gg + 1) * 128], wq[:, kc, hp * 128:(hp + 1) * 128], xT[:, kc, :], kc == 0, kc == 7,
                                  r=[wq, xT], w=[pa])
                    kb.act(qT16[:, g4 * 4:(g4 + 1) * 4, :].rearrange("p a b -> p (a b)"), pa[:], AF.Copy, r=[pa], w=[qT16])
                for g4 in range(4):
                    pq = p_s4[g4 % 2]
                    for gg in range(4):
                        hp = g4 * 4 + gg
                        kb.mm(pq[:, gg * 128:(gg + 1) * 128], qT16[:, hp, :], keysT[:, hp, :], True, True, r=[qT16, keysT], w=[pq])
                    kb.act(s_sb[:, g4 * 4:(g4 + 1) * 4, :].rearrange("p a b -> p (a b)"), pq[:], AF.Copy, r=[pq], w=[s_sb])
                for hp in range(16):
                    top16(s_sb[:, hp, :], stop[:, hp, :], itop[:, hp, :])
                kb.cp(DVE, itopf[:], itop[:], r=[itop], w=[itopf])
                for h in range(8):
                    kb.tt(DVE, cand[:, h, :].rearrange("p (a b) -> p a b", a=16),
                          stop[:, 2 * h, :].unsqueeze(2).to_broadcast([128, 16, 16]),
                          stop[:, 2 * h + 1, :].unsqueeze(1).to_broadcast([128, 16, 16]), ALU.add, r=[stop], w=[cand])
                for h in range(8):
                    top16(cand[:, h, :], gval[:, h, :], gpos[:, h, :])
                gp = gpos[:].rearrange("p a b -> p (a b)")
                kb.v(DVE, lambda: nc.vector.tensor_single_scalar(out=ai[:], in_=gp, scalar=4, op=ALU.logical_shift_right), r=[gpos], w=[ai])
                kb.cp(DVE, af[:, 0, :], ai[:], r=[ai], w=[af])
                kb.v(DVE, lambda: nc.vector.tensor_single_scalar(out=ai[:], in_=gp, scalar=15, op=ALU.bitwise_and), r=[gpos], w=[ai])
                kb.cp(DVE, af[:, 1, :], ai[:], r=[ai], w=[af])
                for p in range(2):
                    kb.tt(DVE, oh[:], iota_f[:, 0:16].unsqueeze(1).to_broadcast([128, 128, 16]),
                          af[:, p, :].unsqueeze(2).to_broadcast([128, 128, 16]), ALU.is_equal, r=[iota_f, af], w=[oh])
                    ohv = oh[:].rearrange("p (h k) a -> p h k a", h=8)
                    itv = itopf[:].rearrange("p (h q) a -> p h q a", q=2)[:, :, p, :]
                    for h in range(8):
                        kb.tt(DVE, ohv[:, h], ohv[:, h], itv[:, h].unsqueeze(1).to_broadcast([128, 16, 16]), ALU.mult, r=[oh, itopf], w=[oh])
                    kb.v(DVE, lambda p=p: nc.vector.tensor_reduce(out=i01[:, p, :], in_=oh[:], axis=AX.X, op=ALU.add), r=[oh], w=[i01])
                kb.v(DVE, lambda: nc.vector.scalar_tensor_tensor(out=i01[:, 0, :], in0=i01[:, 0, :], scalar=128.0, in1=i01[:, 1, :],
                                                                  op0=ALU.mult, op1=ALU.add), r=[i01], w=[i01])
                kb.cp(DVE, eidx[:], i01[:, 0, :], r=[i01], w=[eidx])
                kb.tt(DVE, gate[:], gval[:], gval[:, :, 0:1].to_broadcast([128, 8, 16]), ALU.subtract, r=[gval], w=[gate])
                kb.act(gate[:].rearrange("p a b -> p (a b)"), gate[:].rearrange("p a b -> p (a b)"), AF.Exp, r=[gate], w=[gate])
                kb.v(DVE, lambda: nc.vector.tensor_reduce(out=gsum[:], in_=gate[:], axis=AX.X, op=ALU.add), r=[gate], w=[gsum])
                kb.v(DVE, lambda: nc.vector.reciprocal(out=gsum[:], in_=gsum[:]), r=[gsum], w=[gsum])
                kb.tt(DVE, gate[:], gate[:], gsum[:].unsqueeze(2).to_broadcast([128, 8, 16]), ALU.mult, r=[gate, gsum], w=[gate])

            def tile_rest(ti, eidx, pending):
                xt = x_t[ti % 2]
                for sl in range(128):
                    ub = ubuf[sl % NB]
                    S.emit(POOL, lambda ub=ub, sl=sl: nc.gpsimd.indirect_dma_start(
                        out=ub[:], out_offset=None, in_=ub16[l],
                        in_offset=bass.IndirectOffsetOnAxis(ap=eidx[:, sl:sl + 1], axis=0)), r=[eidx.b, tb_b[("u", l)]], w=[ub.b], dma=True)
                    kb.v(DVE, lambda ub=ub, sl=sl: nc.vector.scalar_tensor_tensor(
                        out=junk[:], in0=ub[:], scalar=1.0, in1=xt[:], op0=ALU.mult, op1=ALU.mult,
                        accum_out=actv[:, sl:sl + 1]), r=[ub, xt], w=[junk, actv])
                kb.tt(DVE, g1[:], actv[:], actv[:], ALU.mult, r=[actv], w=[g1])
                kb.ts(DVE, g1[:], g1[:], 0.044715, ALU.mult, 1.0, ALU.add, r=[g1], w=[g1])
                kb.tt(DVE, g1[:], g1[:], actv[:], ALU.mult, r=[g1, actv], w=[g1])
                kb.act(g2[:], g1[:], AF.Tanh, r=[g1], w=[g2], scale=0.7978845608028654)
                kb.ts(DVE, g2[:], g2[:], 1.0, ALU.add, 0.5, ALU.mult, r=[g2], w=[g2])
                kb.tt(DVE, g2[:], g2[:], actv[:], ALU.mult, r=[g2, actv], w=[g2])
                kb.tt(DVE, hw[:], g2[:], gate[:].rearrange("p a b -> p (a b)"), ALU.mult, r=[g2, gate], w=[hw])
                for sl in range(128):
                    vb = vbuf[sl % NB]
                    S.emit(POOL, lambda vb=vb, sl=sl: nc.gpsimd.indirect_dma_start(
                        out=vb[:], out_offset=None, in_=vb16[l],
                        in_offset=bass.IndirectOffsetOnAxis(ap=eidx[:, sl:sl + 1], axis=0)), r=[eidx.b, tb_b[("v", l)]], w=[vb.b], dma=True)
                    vsb = vs[sl % 3]
                    kb.act(vsb[:], vb[:], AF.Copy, r=[vb, hw], w=[vsb], scale=hw[:, sl:sl + 1])
                    for half in range(2):
                        kb.mm(p_y[half][:], ident_b[:], vsb[:, half * 512:(half + 1) * 512], sl == 0, sl == 127,
                              r=[ident_b, vsb], w=[p_y[half]])
                    S.flush(pending, 3)
                S.flush(pending, 1 << 30)
                for half in range(2):
                    kb.v(DVE, lambda half=half: nc.vector.scalar_tensor_tensor(
                        out=y_acc[:, half * 512:(half + 1) * 512], in0=xt[:, half * 512:(half + 1) * 512], scalar=DN_ALPHA,
                        in1=p_y[half][:], op0=ALU.mult, op1=ALU.add), r=[xt, p_y[half]], w=[y_acc])
                layer_norm_tile(y_acc, lng, lnb, y_acc, (lst, lmv, lrs), eng2=DVE)
                kb.dma(dst[ti * 128:(ti + 1) * 128, :], y_acc[:], r=[y_acc], w=[dst_b])

            retrieval(0, eidx2[0])
            for ti in range(NT):
                pending = []
                if ti + 1 < NT:
                    S.defer = pending
                    retrieval(ti + 1, eidx2[(ti + 1) % 2])
                    S.defer = None
                tile_rest(ti, eidx2[ti % 2], pending)
        kb.stack = root

    cosT = kb.sb("cosT", [128, NT, 32], F32)
    sinT = kb.sb("sinT", [128, NT, 32], F32)
    kmP = kb.sb("kmP", [128, 8, 16], BF16)
    kmS = kb.sb("kmS", [128, NSEQ_S, 8, 8], BF16)
    idx_all = kb.sb("idx_all", [128, NSEQ_S * 16], I32)
    inv256 = kb.sb("inv256", [128, 1], F32)
    ones_b = kb.sb("ones_b", [128, 128], BF16)
    maskP_b = kb.sb("maskP_b", [128, 128], BF16)
    eye8 = kb.sb("eye8", [8, 8], F32)
    hm01 = kb.sb("hm01", [128, 2], F32)
    KT_d = dint("KT_d", [128, 8, NTOK], BF16)
    V_d = dint("V_d", [NTOK, D], BF16)
    attn_d = dint("attn_d", [NT_S * 128, D], F32)
    attn_dp = dint("attn_dp", [NT_P * 128, D], F32)
    attn_dpb = Buf("attn_dp")
    KT_db, V_db, attn_db = Buf("KT_d"), Buf("V_d"), Buf("attn_d")
    PI = 3.14159265358979

    def setup_moba_consts():
        with ExitStack() as tmpst:
            kb.stack = tmpst
            _setup_moba_consts()
        kb.stack = root

    def _setup_moba_consts():
        invf = kb.sb("invf", [128, 32], F32)
        posc = kb.sb("posc", [128, 2], F32)
        ang = kb.sb("ang", [128, 2, 32], F32)
        angq = kb.sb("angq", [128, 64], F32)
        angi = kb.sb("angi", [128, 64], I32)
        idx_f = kb.sb("idx_f", [128, NSEQ_S * 16], F32)
        kb.v(DVE, lambda: nc.vector.memset(inv256[:], 1.0 / 256.0), w=[inv256])
        kb.v(DVE, lambda: nc.vector.memset(ones_b[:], 1.0), w=[ones_b])
        kb.cp(DVE, maskP_b[:], maskP[:], r=[maskP], w=[maskP_b])
        kb.cp(DVE, eye8[:], ident_f[0:8, 0:8], r=[ident_f], w=[eye8])
        kb.ts(DVE, hm01[:, 1:2], iota_p[:], 64.0, ALU.is_ge, r=[iota_p], w=[hm01])
        kb.ts(DVE, hm01[:, 0:1], hm01[:, 1:2], -1.0, ALU.mult, 1.0, ALU.add, r=[hm01], w=[hm01])
        kb.act(invf[:], iota_f[:, 0:32], AF.Exp, r=[iota_f], w=[invf], scale=-float(np.log(10000.0)) / 32.0)
        kb.v(DVE, lambda: nc.vector.scalar_tensor_tensor(out=posc[:, 1:2], in0=pdiv[:], scalar=-8.0, in1=iota_p[:],
                                                          op0=ALU.mult, op1=ALU.add), r=[pdiv, iota_p], w=[posc])
        kb.ts(DVE, posc[:, 1:2], posc[:, 1:2], 2048.0, ALU.add, r=[posc], w=[posc])
        for ti in range(NT):
            if ti < NT_P:
                kb.ts(DVE, posc[:, 0:1], iota_p[:], float(ti * 128), ALU.add, r=[iota_p], w=[posc])
                pc = posc[:, 0:1]
            else:
                pc = posc[:, 1:2]
            kb.ts(DVE, ang[:, 0, :], invf[:], pc, ALU.mult, r=[invf, posc], w=[ang])
            kb.ts(DVE, ang[:, 1, :], ang[:, 0, :], 0.5 * PI, ALU.add, r=[ang], w=[ang])
            A2 = ang[:].rearrange("p a b -> p (a b)")
            kb.ts(DVE, angq[:], A2, 1.0 / (2.0 * PI), ALU.mult, r=[ang], w=[angq])
            kb.cp(DVE, angi[:], angq[:], r=[angq], w=[angi])
            kb.cp(DVE, angq[:], angi[:], r=[angi], w=[angq])
            kb.v(DVE, lambda: nc.vector.scalar_tensor_tensor(out=A2, in0=angq[:], scalar=-2.0 * PI, in1=A2, op0=ALU.mult, op1=ALU.add),
                 r=[angq, ang], w=[ang])
            kb.ts(DVE, angq[:], A2, PI, ALU.is_ge, r=[ang], w=[angq])
            kb.v(DVE, lambda: nc.vector.scalar_tensor_tensor(out=A2, in0=angq[:], scalar=-2.0 * PI, in1=A2, op0=ALU.mult, op1=ALU.add),
                 r=[angq, ang], w=[ang])
            kb.ts(DVE, angq[:], A2, -1.0, ALU.mult, PI, ALU.is_ge, r=[ang], w=[angq])
            kb.v(DVE, lambda: nc.vector.scalar_tensor_tensor(out=A2, in0=angq[:], scalar=2.0 * PI, in1=A2, op0=ALU.mult, op1=ALU.add),
                 r=[angq, ang], w=[ang])
            kb.act(sinT[:, ti, :], ang[:, 0, :], AF.Sin, r=[ang], w=[sinT], scale=0.999999)
            kb.act(cosT[:, ti, :], ang[:, 1, :], AF.Sin, r=[ang], w=[cosT], scale=0.999999)
        kb.dma(idx_all[:], page_table.rearrange("s p -> (s p)").rearrange("(o n) -> o n", o=1).to_broadcast([128, NSEQ_S * 16]), w=[idx_all])
        kb.cp(DVE, idx_f[:], idx_all[:], r=[idx_all], w=[idx_f])
        kb.ts(DVE, idx_f[:], idx_f[:], 128.0, ALU.mult, iota_p[:, 0:1], ALU.add, r=[idx_f, iota_p], w=[idx_f])
        kb.cp(DVE, idx_all[:], idx_f[:], r=[idx_f], w=[idx_all])

    def rope_tile(ti, srcf, dstf, tmp):
        sv = srcf[:].rearrange("p (h e d) -> p h e d", h=16, e=2)
        dv = dstf[:].rearrange("p (h e d) -> p h e d", h=16, e=2)
        tv = tmp[:].rearrange("p a (h d) -> p a h d", h=16)
        cb = cosT[:, ti, :].unsqueeze(1).to_broadcast([128, 16, 32])
        sb_ = sinT[:, ti, :].unsqueeze(1).to_broadcast([128, 16, 32])
        kb.tt(DVE, tv[:, 0], sv[:, :, 0, :], cb, ALU.mult, r=[srcf, cosT], w=[tmp])
        kb.tt(DVE, tv[:, 1], sv[:, :, 1, :], sb_, ALU.mult, r=[srcf, sinT], w=[tmp])
        kb.tt(DVE, dv[:, :, 0, :], tv[:, 0], tv[:, 1], ALU.subtract, r=[tmp], w=[dstf])
        kb.tt(POOL, tv[:, 0], sv[:, :, 1, :], cb, ALU.mult, r=[srcf, cosT, dstf], w=[tmp])
        kb.tt(POOL, tv[:, 1], sv[:, :, 0, :], sb_, ALU.mult, r=[srcf, sinT], w=[tmp])
        kb.tt(POOL, dv[:, :, 1, :], tv[:, 0], tv[:, 1], ALU.add, r=[tmp], w=[dstf])

    def load_w_bf16(dst_t, w_ap, ncols, wst, i0=0):
        i = i0
        for kc in range(8):
            for c0 in range(0, ncols, 1024):
                st = wst[i % 2]
                kb.dma(st[:], w_ap[kc * 128:(kc + 1) * 128, c0:c0 + 1024], w=[st])
                kb.cp((DVE, POOL)[i % 2], dst_t[:, kc, c0:c0 + 1024], st[:], r=[st], w=[dst_t])
                i += 1
        return i

    def x_to_xT(xt, x_b, xT, p_tr):
        kb.act(x_b[:], xt[:], AF.Copy, r=[xt], w=[x_b])
        for kc in range(8):
            kb.tr(p_tr[:, kc * 128:(kc + 1) * 128], x_b[:, kc * 128:(kc + 1) * 128], ident_b[:], r=[x_b, ident_b], w=[p_tr])
        kb.cp(DVE, xT[:].rearrange("p a b -> p (a b)"), p_tr[:], r=[p_tr], w=[xT])

    def kv_phase(src, src_b):
        with ExitStack() as ph:
            kb.stack = ph
            wkv = kb.sb("wkv", [128, 8, 2048], BF16)
            wst = [kb.sb("wst", [128, 1024], F32) for _ in range(2)]
            load_w_bf16(wkv, w_kv, 2048, wst)
            x_t = [kb.sb("x_t", [128, D], F32) for _ in range(2)]
            x_b = kb.sb("x_b", [128, D], BF16)
            xT = kb.sb("xT", [128, 8, 128], BF16)
            kf = kb.sb("kf", [128, D], F32)
            kr = kb.sb("kr", [128, D], F32)
            krb = kb.sb("krb", [128, D], BF16)
            vf = kb.sb("vf", [128, D], F32)
            vb = kb.sb("vb", [128, D], BF16)
            ktst = kb.sb("ktst", [128, 8, 128], BF16)
            rtmp = kb.sb("rtmp", [128, 2, 512], F32)
            kpg = [kb.sb("kpg", [128, D], F32) for _ in range(3)]
            p_a = [kb.ps("p_a", [128, 512], F32) for _ in range(2)]
            p_tr = kb.ps("p_tr", [128, 1024], BF16)
            p_km = kb.ps("p_km", [128, 512], F32)
            kb.v(DVE, lambda: nc.vector.memset(p_km[:], 0.0), w=[p_km])
            for ti in range(NT):
                xt = x_t[ti % 2]
                kb.dma(xt[:], src[ti * 128:(ti + 1) * 128, :], r=[src_b], w=[xt])
                x_to_xT(xt, x_b, xT, p_tr)
                for blk in range(4):
                    pa = p_a[blk % 2]
                    for kc in range(8):
                        kb.mm(pa[:], xT[:, kc, :], wkv[:, kc, blk * 512:(blk + 1) * 512], kc == 0, kc == 7, r=[wkv, xT], w=[pa])
                    if blk < 2:
                        kb.act(kf[:, blk * 512:(blk + 1) * 512], pa[:], AF.Copy, r=[pa], w=[kf])
                    else:
                        kb.act(vf[:, (blk - 2) * 512:(blk - 1) * 512], pa[:], AF.Copy, r=[pa], w=[vf])
                rope_tile(ti, kf, kr, rtmp)
                kb.dma(k_rows[ti * 128:(ti + 1) * 128, :], kr[:], r=[kr], w=[out_bufs["k"]])
                kb.dma(v_rows[ti * 128:(ti + 1) * 128, :], vf[:], r=[vf], w=[out_bufs["v"]])
                kb.cp(POOL, vb[:], vf[:], r=[vf], w=[vb])
                kb.dma(V_d[ti * 128:(ti + 1) * 128, :], vb[:], r=[vb], w=[V_db])
                kb.act(krb[:], kr[:], AF.Copy, r=[kr], w=[krb])
                for c in range(8):
                    kb.tr(p_tr[:, c * 128:(c + 1) * 128], krb[:, c * 128:(c + 1) * 128], ident_b[:], r=[krb, ident_b], w=[p_tr])
                kb.cp(DVE, ktst[:].rearrange("p a b -> p (a b)"), p_tr[:], r=[p_tr], w=[ktst])
                kb.dma(KT_d[:, :, ti * 128:(ti + 1) * 128], ktst[:], r=[ktst], w=[KT_db])
                if ti < NT_P:
                    kbk = ti // 2
                    for c in range(8):
                        S.emit(PE, lambda c=c, kbk=kbk: nc.tensor.matmul(p_km[:, c * 16 + kbk:c * 16 + kbk + 1], kr[:, c * 128:(c + 1) * 128],
                                                                         inv256[:], start=False, stop=True, skip_group_check=True),
                               r=[kr.b, inv256.b], w=[p_km.b])
            kb.cp(DVE, kmP[:].rearrange("p a b -> p (a b)"), p_km[:, 0:128], r=[p_km], w=[kmP])
            for sq in range(NSEQ_S):
                kb.v(DVE, lambda: nc.vector.memset(p_km[:, 0:64], 0.0), w=[p_km])
                for pg in range(16):
                    kp = kpg[pg % 3]
                    col = sq * 16 + pg
                    S.emit(POOL, lambda kp=kp, col=col: nc.gpsimd.indirect_dma_start(
                        out=kp[:], out_offset=None, in_=cache_k,
                        in_offset=bass.IndirectOffsetOnAxis(ap=idx_all[:, col:col + 1], axis=0)), r=[idx_all.b], w=[kp.b], dma=True)
                    for c in range(8):
                        S.emit(PE, lambda c=c, pg=pg, kp=kp: nc.tensor.matmul(
                            p_km[:, c * 8 + pg // 2:c * 8 + pg // 2 + 1], kp[:, c * 128:(c + 1) * 128], inv256[:],
                            start=False, stop=True, skip_group_check=True), r=[kp.b, inv256.b], w=[p_km.b])
                kb.cp(DVE, kmS[:, sq, :, :].rearrange("p a b -> p (a b)"), p_km[:, 0:64], r=[p_km], w=[kmS])
        kb.stack = root

    def moba_layer(jl, l, src, src_b, dst, dst_b):
        with ExitStack() as ph:
            kb.stack = ph
            wqb = kb.sb("wqb", [128, 8, D], BF16)
            with ExitStack() as phw:
                kb.stack = phw
                wst = [kb.sb("wst", [128, 1024], F32) for _ in range(2)]
                load_w_bf16(wqb, w_q_b[jl], 1024, wst)
            kb.stack = ph
            wob_box = []
            lng = kb.sb("lng", [128, D], F32)
            lnb = kb.sb("lnb", [128, D], F32)
            kb.dma(lng[:], bc_rows(ln_g[l, 0:1, :], D), w=[lng])
            kb.dma(lnb[:], bc_rows(ln_b[l, 0:1, :], D), w=[lnb])
            x_t = [kb.sb("x_t", [128, D], F32)] * 2
            x_b = kb.sb("x_b", [128, D], BF16)
            xT = kb.sb("xT", [128, 8, 128], BF16)
            qTe = [kb.sb("qTe", [128, 8, 128], BF16) for _ in range(2)]
            rtmp = kb.sb("rtmp", [128, 2, 512], F32)
            attn = kb.sb("attn", [128, D], F32)
            z_t = kb.sb("z_t", [128, D], F32)
            qf = z_t
            qr = attn
            lst = kb.sb("lst", [128, 2, 6], F32)
            lmv = kb.sb("lmv", [128, 2], F32)
            lrs = kb.sb("lrs", [128, 1], F32)
            g0 = kb.sb("g0", [128, 16, 16], F32)
            gw = kb.sb("gw", [128, 16, 16], F32)
            ge = kb.sb("ge", [128, 16, 16], F32)
            gm = kb.sb("gm", [128, 16], F32)
            selb = kb.sb("selb", [128, 16, 16], F32)
            pastm = kb.sb("pastm", [128, 16], F32)
            dg = [kb.sb("dg", [128, 128], BF16) for _ in range(2)]
            PTm = [kb.sb("PTm", [128, 512], BF16) for _ in range(2)]
            rden = kb.sb("rden", [128, 16], F32)
            p_a = [kb.ps("p_a", [128, 512], F32) for _ in range(2)]
            p_tr = kb.ps("p_tr", [128, 1024], BF16)
            p_g = kb.ps("p_g", [128, 512], F32)
            p_sc = [kb.ps("p_sc", [128, 512], F32) for _ in range(2)]
            p_o = [kb.ps("p_o", [128, 512], F32) for _ in range(2)]

            def q_proj(ti, xt):
                x_to_xT(xt, x_b, xT, p_tr)
                for half in range(2):
                    pa = p_a[half]
                    for kc in range(8):
                        kb.mm(pa[:], xT[:, kc, :], wqb[:, kc, half * 512:(half + 1) * 512], kc == 0, kc == 7, r=[wqb, xT], w=[pa])
                    kb.act(qf[:, half * 512:(half + 1) * 512], pa[:], AF.Copy, r=[pa], w=[qf])
                rope_tile(ti, qf, qr, rtmp)
                kb.act(x_b[:], qr[:], AF.Copy, r=[qr], w=[x_b])
                for c in range(8):
                    kb.tr(p_tr[:, c * 128:(c + 1) * 128], x_b[:, c * 128:(c + 1) * 128], ident_b[:], r=[x_b, ident_b], w=[p_tr])
                for e in range(2):
                    kb.ts(DVE, qTe[e][:].rearrange("p a b -> p (a b)"), p_tr[:], hm01[:, e:e + 1], ALU.mult, r=[p_tr, hm01], w=[qTe[e]])

            def select_blocks(nq, nkb, own, km_of_head):
                for h in range(16):
                    e, hp = h % 2, h // 2
                    kb.mm(p_g[0:nq, h * 16:h * 16 + nkb], qTe[e][:, hp, 0:nq] if nq == 128 else qcols(e, hp),
                          km_of_head(e, hp), True, True, r=[qTe[0], qTe[1], kmP, kmS], w=[p_g])
                kb.v(DVE, lambda: nc.vector.memset(g0[0:nq], -1e9), w=[g0])
                npast = min(own, nkb)
                if npast > 0:
                    kb.cp(DVE, g0[0:nq, :, 0:npast], p_g[0:nq, 0:256].rearrange("p (h k) -> p h k", h=16)[:, :, 0:npast], r=[p_g], w=[g0])
                kb.cp(DVE, gw[0:nq], g0[0:nq], r=[g0], w=[gw])
                for rnd in range(3):
                    kb.v(DVE, lambda: nc.vector.tensor_reduce(out=gm[0:nq], in_=gw[0:nq], axis=AX.X, op=ALU.max), r=[gw], w=[gm])
                    if rnd < 2:
                        kb.tt(DVE, ge[0:nq], gw[0:nq], gm[0:nq].unsqueeze(2).to_broadcast([nq, 16, 16]), ALU.is_equal, r=[gw, gm], w=[ge])
                        kb.v(DVE, lambda: nc.vector.scalar_tensor_tensor(out=gw[0:nq], in0=ge[0:nq], scalar=-1e9, in1=gw[0:nq],
                                                                          op0=ALU.mult, op1=ALU.add), r=[ge, gw], w=[gw])
                kb.tt(DVE, ge[0:nq], g0[0:nq], gm[0:nq].unsqueeze(2).to_broadcast([nq, 16, 16]), ALU.is_ge, r=[g0, gm], w=[ge])
                kb.ts(DVE, selb[0:nq], ge[0:nq], -1.0, ALU.add, -NEG, ALU.mult, r=[ge], w=[selb])
                kb.ts(DVE, pastm[0:nq], iota_f[0:nq, 0:16], float(npast), ALU.is_ge, r=[iota_f], w=[pastm])
                kb.ts(DVE, pastm[0:nq], pastm[0:nq], -1.0, ALU.mult, 1.0, ALU.add, r=[pastm], w=[pastm])
                kb.tt(DVE, selb[0:nq], selb[0:nq], pastm[0:nq].unsqueeze(1).to_broadcast([nq, 16, 16]), ALU.mult, r=[selb, pastm], w=[selb])

            qcols_state = {}

            def qcols(e, hp):
                j = qcols_state["j"]
                return qTe[e][:, hp, 8 * j:8 * j + 8]

            def out_proj(ti, xt, attn_src):
                kb.act(x_b[:], attn_src[:], AF.Copy, r=[attn_src], w=[x_b])
                for c in range(8):
                    kb.tr(p_tr[:, c * 128:(c + 1) * 128], x_b[:, c * 128:(c + 1) * 128], ident_b[:], r=[x_b, ident_b], w=[p_tr])
                kb.cp(DVE, xT[:].rearrange("p a b -> p (a b)"), p_tr[:], r=[p_tr], w=[xT])
                for half in range(2):
                    pa = p_a[half]
                    for kc in range(8):
                        kb.mm(pa[:], xT[:, kc, :], wob_box[0][:, kc, half * 512:(half + 1) * 512], kc == 0, kc == 7, r=[xT, wob_box[0]], w=[pa])
                    kb.v(DVE, lambda half=half, pa=pa: nc.vector.scalar_tensor_tensor(
                        out=z_t[:, half * 512:(half + 1) * 512], in0=xt[:, half * 512:(half + 1) * 512], scalar=DN_ALPHA,
                        in1=pa[:], op0=ALU.mult, op1=ALU.add), r=[xt, pa], w=[z_t])
                layer_norm_tile(z_t, lng, lnb, z_t, (lst, lmv, lrs))
                kb.dma(dst[ti * 128:(ti + 1) * 128, :], z_t[:], r=[z_t], w=[dst_b])

            with ExitStack() as ph2:
                kb.stack = ph2
                KT = kb.sb("KT", [128, 8, NT_P * 128], BF16)
                Vaug = kb.sb("Vaug", [128, NT_P, 16, 64], BF16)
                for c in range(8):
                    kb.dma(KT[:, c, :], KT_d[:, c, 0:NT_P * 128], r=[KT_db], w=[KT])
                for t in range(NT_P):
                    kb.dma(Vaug[:, t, :, 0:64], V_d[t * 128:(t + 1) * 128, :].rearrange("p (h d) -> p h d", h=16), r=[V_db], w=[Vaug])
                for qt in range(DBG["np_tiles"]):
                    xt = x_t[qt % 2]
                    kb.dma(xt[:], src[qt * 128:(qt + 1) * 128, :], r=[src_b], w=[xt])
                    q_proj(qt, xt)
                    own = qt // 2
                    if DBG["select"]:
                        select_blocks(128, 16, own, lambda e, hp: kmP[:, hp, :])
                    vi = 0
                    for h in range(DBG["heads"]):
                        e, hp = h % 2, h // 2
                        if e == 1 and not DBG["e1"]:
                            continue
                        po = p_o[h % 2]
                        kb.v(DVE, lambda po=po: nc.vector.memset(po[:, 0:65], 0.0), w=[po])
                        for kc0 in range(0, qt + 1, 4):
                            ncz = min(4, qt + 1 - kc0)
                            psc = p_sc[vi % 2]
                            ptm = PTm[vi % 2]
                            for ci in range(ncz):
                                kc = kc0 + ci
                                kbk = kc // 2
                                need_sel = kbk < own
                                need_caus = kc == qt
                                reg = psc[:, ci * 128:(ci + 1) * 128]
                                kb.mm(reg, KT[:, hp, kc * 128:(kc + 1) * 128], qTe[e][:, hp, :],
                                      True, not (need_sel or need_caus), r=[KT, qTe[e]], w=[psc])
                                if need_sel:
                                    d_ = dg[(kc // 2) % 2]
                                    if kc % 2 == 0:
                                        kb.ts(DVE, d_[:], ident_f[:], selb[:, h, kbk:kbk + 1], ALU.mult, r=[ident_f, selb], w=[d_])
                                    kb.mm(reg, ones_b[:], d_[:], False, True, r=[ones_b, d_], w=[psc])
                                if need_caus:
                                    kb.mm(reg, ident_b[:], maskP_b[:], False, True, r=[ident_b, maskP_b], w=[psc])
                            kb.act(ptm[:, 0:ncz * 128], psc[:, 0:ncz * 128], AF.Exp, r=[psc], w=[ptm], scale=0.125)
                            for ci in range(ncz):
                                kc = kc0 + ci
                                S.emit(PE, lambda po=po, ptm=ptm, kc=kc, h=h, ci=ci: nc.tensor.matmul(
                                    po[:, 0:64], ptm[:, ci * 128:(ci + 1) * 128], Vaug[:, kc, h, :], start=False, stop=True,
                                    skip_group_check=True), r=[ptm.b, Vaug.b], w=[po.b])
                                S.emit(PE, lambda po=po, ptm=ptm, ci=ci: nc.tensor.matmul(
                                    po[:, 64:65], ptm[:, ci * 128:(ci + 1) * 128], ones_b[:, 0:1], start=False, stop=True,
                                    skip_group_check=True), r=[ptm.b, ones_b.b], w=[po.b])
                            vi += 1
                        kb.v(DVE, lambda po=po, h=h: nc.vector.reciprocal(out=rden[:, h:h + 1], in_=po[:, 64:65]), r=[po], w=[rden])
                        kb.ts(DVE, attn[:, h * 64:(h + 1) * 64], po[:, 0:64], rden[:, h:h + 1], ALU.mult, r=[po, rden], w=[attn])
                    kb.dma(attn_dp[qt * 128:(qt + 1) * 128, :], attn[:], r=[attn], w=[attn_dpb])
            with ExitStack() as ph2:
                kb.stack = ph2
                wob = kb.sb("wob", [128, 8, D], BF16)
                wob_box.append(wob)
                with ExitStack() as phw:
                    kb.stack = phw
                    wst = [kb.sb("wst", [128, 1024], F32) for _ in range(2)]
                    load_w_bf16(wob, w_out_b[jl], 1024, wst)
                kb.stack = ph2
                for qt in range(NT_P):
                    xt = x_t[0]
                    kb.dma(xt[:], src[qt * 128:(qt + 1) * 128, :], r=[src_b], w=[xt])
                    kb.dma(attn[:], attn_dp[qt * 128:(qt + 1) * 128, :], r=[attn_dpb], w=[attn])
                    out_proj(qt, xt, attn)
                KTn = kb.sb("KTn", [128, 8, 128], BF16)
                Vn = kb.sb("Vn", [8, 16, 65], BF16)
                kpg = [kb.sb("kpg", [128, D], F32) for _ in range(2)]
                vpg = [kb.sb("vpg", [128, D], F32) for _ in range(2)]
                vpb = kb.sb("vpb", [128, 16, 65], BF16)
                KTs = kb.sb("KTs", [128, 8, 128], BF16)
                rsel = kb.sb("rsel", [8, 9, 16, 8], BF16)
                o_s = kb.sb("o_s", [8, D], F32)
                dn_s = kb.sb("dn_s", [8, 16], F32)
                p_t4 = [p_a[0]]
                kb.v(DVE, lambda: nc.vector.memset(vpb[:, :, 64:65], 1.0), w=[vpb])
                kb.v(DVE, lambda: nc.vector.memset(Vn[:, :, 64:65], 1.0), w=[Vn])
                kb.v(DVE, lambda: nc.vector.memset(rsel[:], 0.0), w=[rsel])
                p_oa, p_ob, p_od = p_o[0], p_o[1], p_g
                for ts_ in range(DBG["ns_tiles"]):
                    ti = NT_P + ts_
                    xt = x_t[ti % 2]
                    kb.dma(xt[:], src[ti * 128:(ti + 1) * 128, :], r=[src_b], w=[xt])
                    q_proj(ti, xt)
                    kb.dma(KTn[:], KT_d[:, :, ti * 128:(ti + 1) * 128], r=[KT_db], w=[KTn])
                    for j in range(DBG["ns_seq"]):
                        sq = ts_ * 16 + j
                        qcols_state["j"] = j
                        kb.dma(Vn[:, :, 0:64], V_d[ti * 128 + 8 * j: ti * 128 + 8 * j + 8, :].rearrange("p (h d) -> p h d", h=16),
                               r=[V_db], w=[Vn])
                        select_blocks(8, 8, 8, lambda e, hp: kmS[:, sq, hp, :])
                        for kbk in range(8):
                            kb.tt(DVE, rsel[:, kbk, :, :], selb[0:8, :, kbk:kbk + 1].to_broadcast([8, 16, 8]),
                                  eye8[:].unsqueeze(1).to_broadcast([8, 16, 8]), ALU.mult, r=[selb, eye8], w=[rsel])
                        kb.v(DVE, lambda: nc.vector.memset(p_oa[0:8, :], 0.0), w=[p_oa])
                        kb.v(DVE, lambda: nc.vector.memset(p_ob[0:8, :], 0.0), w=[p_ob])
                        kb.v(DVE, lambda: nc.vector.memset(p_od[0:8, 256:272], 0.0), w=[p_od])
                        for pg in range(17):
                            psc = p_sc[pg % 2]
                            ptm = PTm[pg % 2]
                            if pg < 16:
                                nk = 128
                                kp, vp = kpg[pg % 2], vpg[pg % 2]
                                col = sq * 16 + pg
                                S.emit(POOL, lambda kp=kp, col=col: nc.gpsimd.indirect_dma_start(
                                    out=kp[:], out_offset=None, in_=cache_k,
                                    in_offset=bass.IndirectOffsetOnAxis(ap=idx_all[:, col:col + 1], axis=0)), r=[idx_all.b], w=[kp.b], dma=True)
                                S.emit(POOL, lambda vp=vp, col=col: nc.gpsimd.indirect_dma_start(
                                    out=vp[:], out_offset=None, in_=cache_v,
                                    in_offset=bass.IndirectOffsetOnAxis(ap=idx_all[:, col:col + 1], axis=0)), r=[idx_all.b], w=[vp.b], dma=True)
                                for half in range(2):
                                    pt = p_t4[0]
                                    for c4 in range(4):
                                        c = half * 4 + c4
                                        kb.tr(pt[:, c4 * 128:(c4 + 1) * 128], kp[:, c * 128:(c + 1) * 128], ident_f[:], r=[kp, ident_f], w=[pt])
                                    kb.act(KTs[:, half * 4:(half + 1) * 4, :].rearrange("p a b -> p (a b)"), pt[:], AF.Copy, r=[pt], w=[KTs])
                                kb.cp(POOL, vpb[:, :, 0:64], vp[:].rearrange("p (h d) -> p h d", h=16), r=[vp], w=[vpb])
                                ksrc = lambda e, hp: KTs[:, hp, :]
                                vsrc = lambda h: vpb[:, h, 0:64]
                                onesrc = vpb[:, 0, 64:65]
                                ksb, vsb = KTs, vpb
                            else:
                                nk = 8
                                ksrc = lambda e, hp: KTn[:, hp, 8 * j:8 * j + 8]
                                vsrc = lambda h: Vn[:, h, 0:64]
                                onesrc = Vn[:, 0, 64:65]
                                ksb, vsb = KTn, Vn
                            kb.v(DVE, lambda psc=psc: nc.vector.memset(psc[:, 0:128], 0.0), w=[psc])
                            for h in range(16):
                                e, hp = h % 2, h // 2
                                S.emit(PE, lambda h=h, e=e, hp=hp, ksrc=ksrc, psc=psc, nk=nk: nc.tensor.matmul(
                                    psc[0:nk, h * 8:(h + 1) * 8], ksrc(e, hp), qcols(e, hp), start=False, stop=True,
                                    skip_group_check=True), r=[ksb.b, qTe[0].b, qTe[1].b], w=[psc.b])
                            if pg < 16:
                                S.emit(PE, lambda psc=psc, pg=pg: nc.tensor.matmul(
                                    psc[:, 0:128], ones_b[0:8, :], rsel[:, pg // 2, :, :].rearrange("p a b -> p (a b)"), start=False, stop=True,
                                    skip_group_check=True), r=[ones_b.b, rsel.b], w=[psc.b])
                            else:
                                S.emit(PE, lambda psc=psc: nc.tensor.matmul(
                                    psc[0:8, 0:128], ident_b[0:8, 0:8], caus8[:].rearrange("p a b -> p (a b)"), start=False, stop=True,
                                    skip_group_check=True), r=[ident_b.b, caus8.b], w=[psc.b])
                            kb.act(ptm[0:nk, 0:128], psc[0:nk, 0:128], AF.Exp, r=[psc], w=[ptm], scale=0.125)
                            for h in range(16):
                                po = p_oa if h < 8 else p_ob
                                S.emit(PE, lambda h=h, po=po, ptm=ptm, vsrc=vsrc, nk=nk: nc.tensor.matmul(
                                    po[0:8, (h % 8) * 64:(h % 8 + 1) * 64], ptm[0:nk, h * 8:(h + 1) * 8], vsrc(h), start=False, stop=True,
                                    skip_group_check=True), r=[ptm.b, vsb.b], w=[po.b])
                                S.emit(PE, lambda h=h, ptm=ptm, onesrc=onesrc, nk=nk: nc.tensor.matmul(
                                    p_od[0:8, 256 + h:257 + h], ptm[0:nk, h * 8:(h + 1) * 8], onesrc, start=False, stop=True,
                                    skip_group_check=True), r=[ptm.b, vsb.b], w=[p_od.b])
                        kb.v(DVE, lambda: nc.vector.reciprocal(out=dn_s[:], in_=p_od[0:8, 256:272]), r=[p_od], w=[dn_s])
                        for hh in range(2):
                            po = p_oa if hh == 0 else p_ob
                            kb.tt(DVE, o_s[:, hh * 512:(hh + 1) * 512].rearrange("p (h d) -> p h d", h=8),
                                  po[0:8, :].rearrange("p (h d) -> p h d", h=8),
                                  dn_s[:, hh * 8:(hh + 1) * 8].unsqueeze(2).to_broadcast([8, 8, 64]), ALU.mult, r=[po, dn_s], w=[o_s])
                        kb.dma(attn_d[ts_ * 128 + 8 * j: ts_ * 128 + 8 * j + 8, :], o_s[:], r=[o_s], w=[attn_db])
                    kb.dma(attn[:], attn_d[ts_ * 128:(ts_ + 1) * 128, :], r=[attn_db], w=[attn])
                    out_proj(ti, xt, attn)
        kb.stack = root

    caus8 = kb.sb("caus8", [8, 16, 8], BF16)

    cur, cur_b = x_in, Buf("x_in")
    nxt = 0
    if any(p[0] == "P" for p in phases):
        convert_tables()
    for phs in phases:
        if phs[0] == "A":
            mlstm_layer(int(phs[1]), cur, cur_b, xs[nxt], xs_b[nxt])
        elif phs[0] == "P":
            peer_layer(int(phs[1]), cur, cur_b, xs[nxt], xs_b[nxt])
        elif phs == "KV":
            setup_moba_consts()
            kb.cp(DVE, caus8[:], maskP[0:8, 0:8].unsqueeze(1).to_broadcast([8, 16, 8]), r=[maskP], w=[caus8])
            kv_phase(cur, cur_b)
            continue
        elif phs[0] == "B":
            moba_layer(int(phs[1]) - 2, int(phs[1]), cur, cur_b, xs[nxt], xs_b[nxt])
        cur, cur_b = xs[nxt], xs_b[nxt]
        nxt = 1 - nxt
    with ExitStack() as ph:
        kb.stack = ph
        cpb = [kb.sb("cpb", [128, D], F32) for _ in range(2)]
        for ti in range(NT):
            t = cpb[ti % 2]
            kb.dma(t[:], cur[ti * 128:(ti + 1) * 128, :], r=[cur_b], w=[t])
            kb.dma(y_out[ti * 128:(ti + 1) * 128, :], t[:], r=[t], w=[out_bufs["y"]])
    S.finish(list(out_bufs.values()) + xs_b + [KT_db, V_db, attn_db, attn_dpb])
    root.close()
    print("instr counts:", [(e.name, e.n_inst, e.n_wait) for e in S.engs])
    return nc


def make_in_maps(inp):
    maps = []
    for c in range(8):
        b = c % 4
        x = np.concatenate([inp["x_prompt"][b], inp["x_sample"][32 * b:32 * b + 32].reshape(256, D)], axis=0)
        m = {
            "x_in": np.ascontiguousarray(x),
            "stC_in": np.ascontiguousarray(inp["state_C"][:, 32 * b:32 * b + 32]),
            "stn_in": np.ascontiguousarray(inp["state_n"][:, 32 * b:32 * b + 32]),
            "stm_in": np.ascontiguousarray(inp["state_m"][:, 32 * b:32 * b + 32]),
            "w_in_a": inp["w_in_a"], "b_gates_a": inp["b_gates_a"], "norm_a": inp["norm_a"],
            "w_out_a": inp["w_out_a"], "ln_g": inp["ln_g"], "ln_b": inp["ln_b"],
            "peer_wq": inp["peer_wq"],
            "w_kv": inp["w_kv"], "w_q_b": inp["w_q_b"], "w_out_b": inp["w_out_b"],
            "cache_k": inp["cache_k"].reshape(2560 * 128, D), "cache_v": inp["cache_v"].reshape(2560 * 128, D),
            "page_table": np.ascontiguousarray(inp["page_table"][32 * b:32 * b + 32]).astype(np.int32),
            "peer_keysT": np.ascontiguousarray(inp["peer_keys"].reshape(4, 16, 128, 128).transpose(0, 1, 3, 2)),
        }
        for i in range(4):
            m[f"peer_u{i}"] = inp["peer_u"][i]
            m[f"peer_v{i}"] = inp["peer_v"][i]
        maps.append(m)
    return maps


DBG = {"np_tiles": NT_P, "ns_tiles": NT_S, "ns_seq": 16, "heads": 16, "select": True, "e1": True}
FULL_PHASES = ("A0", "P0", "A1", "P1", "KV", "B2", "P2", "B3", "P3")


def kernel(**inp):
    inp = {k: np.asarray(v) for k, v in inp.items()}
    nc = build_program(phases=FULL_PHASES)
    res = run_bass_kernel_spmd(nc, make_in_maps(inp), core_ids=list(range(8)))
    R = res.results
    f32 = np.float32
    y_p = np.stack([R[b]["y_out"][:4096] for b in range(4)]).astype(f32)
    y_s = np.concatenate([R[b]["y_out"][4096:].reshape(32, 8, D) for b in range(4)]).astype(f32)
    C_p = np.stack([R[b]["C_out"][:, 0] for b in range(4)], axis=1).astype(f32)
    n_p = np.stack([R[b]["n_out"][:, 0] for b in range(4)], axis=1).astype(f32)
    m_p = np.stack([R[b]["m_out"][:, 0] for b in range(4)], axis=1).astype(f32)
    C_s = np.concatenate([R[b]["C_out"][:, 1:] for b in range(4)], axis=1).astype(f32)
    n_s = np.concatenate([R[b]["n_out"][:, 1:] for b in range(4)], axis=1).astype(f32)
    m_s = np.concatenate([R[b]["m_out"][:, 1:] for b in range(4)], axis=1).astype(f32)
    k_p = np.stack([R[b]["k_rows"][:4096].reshape(4096, 16, 64) for b in range(4)]).astype(f32)
    v_p = np.stack([R[b]["v_rows"][:4096].reshape(4096, 16, 64) for b in range(4)]).astype(f32)
    k_s = np.concatenate([R[b]["k_rows"][4096:].reshape(32, 8, 16, 64) for b in range(4)]).astype(f32)
    v_s = np.concatenate([R[b]["v_rows"][4096:].reshape(32, 8, 16, 64) for b in range(4)]).astype(f32)
    return (y_p, y_s, C_p, n_p, m_p, k_p, v_p, C_s, n_s, m_s, k_s, v_s)
```
